# Optimizing a Trainium2 kernel written in Bass

```python
import math
import jax
import jax.numpy as jnp
from jax import lax
import numpy as np

D_MODEL = 1024
BATCH = 2
SEQ = 8192
DEPTH = 1
DEC_BATCH = 128
DEC_SEQ = 4
PAST_LEN = 8192
PAGE_SIZE = 128

SSD_HEADS = 8
SSD_HEAD_DIM = 64
SSD_INNER = SSD_HEADS * SSD_HEAD_DIM
SSD_GROUPS = 2
SSD_STATE = 128
CONV_WIDTH = 4
SSD_CHUNK = 128
CONV_DIM = SSD_INNER + 2 * SSD_GROUPS * SSD_STATE
MLA_HEADS = 8
QK_NOPE = 64
QK_ROPE = 32
V_DIM = 64
KV_RANK = 256
Q_RANK = 384
MLA_INNER = MLA_HEADS * V_DIM
ROPE_THETA = 10000.0
ATTN_SCALE = 1.0 / math.sqrt(QK_NOPE + QK_ROPE)
Q_BLOCK = 128
MIX_WIDTH = SSD_INNER + MLA_INNER
D_FF = 4 * D_MODEL
IN_PROJ_DIM = SSD_INNER + CONV_DIM + SSD_HEADS + Q_RANK + KV_RANK + QK_ROPE
EPS = 1e-6

kernel_name = "hymba_ssd_mla_adaln_decode_step"


def rmsnorm(x, g):
    xf = x.astype(jnp.float32)
    y = xf * lax.rsqrt(jnp.mean(xf * xf, axis=-1, keepdims=True) + EPS)
    return (y * g.astype(jnp.float32)).astype(x.dtype)


def modulate(x, g, shift, scale):
    return rmsnorm(x, g) * (1.0 + scale[:, None, :]) + shift[:, None, :]


def rope_tables(pos):
    inv = 1.0 / (ROPE_THETA ** (jnp.arange(0, QK_ROPE, 2, dtype=jnp.float32) / QK_ROPE))
    ang = pos.astype(jnp.float32)[:, None] * inv[None, :]
    return jnp.cos(ang), jnp.sin(ang)


def apply_rope(x, cos, sin):
    x1, x2 = jnp.split(x.astype(jnp.float32), 2, axis=-1)
    return jnp.concatenate([x1 * cos - x2 * sin, x1 * sin + x2 * cos], axis=-1).astype(x.dtype)


def causal_conv(xp, w, b):
    y = lax.conv_general_dilated(xp, w[:, None, :], window_strides=(1,), padding="VALID",
                                 dimension_numbers=("NWC", "WIO", "NWC"), feature_group_count=CONV_DIM)
    return y + b


def ssd_scan(xs, dt, a, bm, cm, h0, chunk):
    b, L, H, P = xs.shape
    nc = L // chunk
    rep = H // SSD_GROUPS
    f32 = jnp.float32
    xc = xs.astype(f32).reshape(b, nc, chunk, H, P)
    bc = jnp.repeat(bm.astype(f32), rep, axis=2).reshape(b, nc, chunk, H, SSD_STATE)
    cc = jnp.repeat(cm.astype(f32), rep, axis=2).reshape(b, nc, chunk, H, SSD_STATE)
    dtc = dt.reshape(b, nc, chunk, H)
    acs = jnp.cumsum(dtc * a, axis=2)
    seg = acs[:, :, :, None, :] - acs[:, :, None, :, :]
    causal = jnp.tril(jnp.ones((chunk, chunk), bool))[None, None, :, :, None]
    decay = jnp.exp(jnp.where(causal, seg, -jnp.inf))
    scores = jnp.einsum("bcihn,bcjhn->bcijh", cc, bc) * decay * dtc[:, :, None, :, :]
    y_diag = jnp.einsum("bcijh,bcjhp->bcihp", scores, xc)
    to_end = jnp.exp(acs[:, :, -1:, :] - acs) * dtc
    s_chunk = jnp.einsum("bcjhn,bcjhp->bchpn", bc, xc * to_end[..., None])
    chunk_decay = jnp.exp(acs[:, :, -1, :])

    def step(h, inp):
        dec, s = inp
        return dec[:, :, None, None] * h + s, h

    h_fin, h_prev = lax.scan(step, h0.astype(f32),
                             (jnp.moveaxis(chunk_decay, 1, 0), jnp.moveaxis(s_chunk, 1, 0)))
    h_prev = jnp.moveaxis(h_prev, 0, 1)
    y_off = jnp.einsum("bcihn,bchpn->bcihp", cc, h_prev) * jnp.exp(acs)[..., None]
    return (y_diag + y_off).reshape(b, L, H, P), h_fin


def ssd_mixer(z, xbc_pad, dt_raw, h0, lw):
    b, L, _ = z.shape
    xbc = jax.nn.silu(causal_conv(xbc_pad, lw["conv_w"], lw["conv_b"]))
    xs, bm, cm = jnp.split(xbc, [SSD_INNER, SSD_INNER + SSD_GROUPS * SSD_STATE], axis=-1)
    xs = xs.reshape(b, L, SSD_HEADS, SSD_HEAD_DIM)
    bm = bm.reshape(b, L, SSD_GROUPS, SSD_STATE)
    cm = cm.reshape(b, L, SSD_GROUPS, SSD_STATE)
    dt = jax.nn.softplus(dt_raw.astype(jnp.float32) + lw["dt_bias"].astype(jnp.float32))
    a = -jnp.exp(lw["a_log"].astype(jnp.float32))
    chunk = SSD_CHUNK if L % SSD_CHUNK == 0 else L
    y, h_fin = ssd_scan(xs, dt, a, bm, cm, h0, chunk)
    y = y + lw["d_skip"].astype(jnp.float32)[:, None] * xs.astype(jnp.float32)
    y = y.reshape(b, L, SSD_INNER) * jax.nn.silu(z.astype(jnp.float32))
    return rmsnorm(y, lw["norm_ssd_g"]).astype(z.dtype), h_fin.astype(z.dtype)


def mla_project(q_lat, kv_lat, kr_raw, pos, lw):
    b, L, _ = q_lat.shape
    q = (rmsnorm(q_lat, lw["q_norm_g"]) @ lw["w_uq"]).reshape(b, L, MLA_HEADS, QK_NOPE + QK_ROPE)
    q_nope, q_rope = jnp.split(q, [QK_NOPE], axis=-1)
    cos, sin = rope_tables(pos)
    q_rope = apply_rope(q_rope, cos[None, :, None, :], sin[None, :, None, :])
    kr = apply_rope(kr_raw, cos[None], sin[None])
    ckv = rmsnorm(kv_lat, lw["kv_norm_g"])
    q_abs = jnp.einsum("blhd,rhd->blhr", q_nope, lw["w_uk"])
    return q_abs, q_rope, ckv, kr


def mla_attend_prompt(q_abs, q_rope, ckv, kr):
    b, L = q_abs.shape[:2]
    nb = L // Q_BLOCK
    qa = jnp.moveaxis(q_abs.reshape(b, nb, Q_BLOCK, MLA_HEADS, KV_RANK), 1, 0)
    qr = jnp.moveaxis(q_rope.reshape(b, nb, Q_BLOCK, MLA_HEADS, QK_ROPE), 1, 0)
    kpos = jnp.arange(L, dtype=jnp.int32)

    def one_block(args):
        qa_b, qr_b, start = args
        s = jnp.einsum("bqhr,bkr->bhqk", qa_b, ckv) + jnp.einsum("bqhd,bkd->bhqk", qr_b, kr)
        qpos = start + jnp.arange(Q_BLOCK, dtype=jnp.int32)
        s = jnp.where(kpos[None, :] <= qpos[:, None], s.astype(jnp.float32) * ATTN_SCALE, -jnp.inf)
        p = jax.nn.softmax(s, axis=-1).astype(ckv.dtype)
        return jnp.einsum("bhqk,bkr->bqhr", p, ckv)

    o = lax.map(one_block, (qa, qr, jnp.arange(nb, dtype=jnp.int32) * Q_BLOCK))
    return jnp.moveaxis(o, 0, 1).reshape(b, L, MLA_HEADS, KV_RANK)


def mla_attend_sample(q_abs, q_rope, ckv, kr, ckv_past, kr_past):
    t = q_abs.shape[1]
    past = ckv_past.shape[1]
    s_past = jnp.einsum("bqhr,bkr->bhqk", q_abs, ckv_past) + jnp.einsum("bqhd,bkd->bhqk", q_rope, kr_past)
    s_new = jnp.einsum("bqhr,bkr->bhqk", q_abs, ckv) + jnp.einsum("bqhd,bkd->bhqk", q_rope, kr)
    causal = jnp.tril(jnp.ones((t, t), bool))
    s_new = jnp.where(causal, s_new.astype(jnp.float32) * ATTN_SCALE, -jnp.inf)
    s = jnp.concatenate([s_past.astype(jnp.float32) * ATTN_SCALE, s_new], axis=-1)
    p = jax.nn.softmax(s, axis=-1).astype(ckv.dtype)
    return (jnp.einsum("bhqk,bkr->bqhr", p[..., :past], ckv_past)
            + jnp.einsum("bhqk,bkr->bqhr", p[..., past:], ckv))


def mixer(h, pos, conv_prev, h0, ckv_past, kr_past, lw):
    b, L, _ = h.shape
    proj = h @ lw["w_in"]
    c1 = SSD_INNER
    c2 = c1 + CONV_DIM
    c3 = c2 + SSD_HEADS
    c4 = c3 + Q_RANK
    c5 = c4 + KV_RANK
    z, xbc, dt_raw, q_lat, kv_lat, kr_raw = jnp.split(proj, [c1, c2, c3, c4, c5], axis=-1)
    xbc_pad = jnp.concatenate([conv_prev.astype(xbc.dtype), xbc], axis=1)
    new_conv = xbc_pad[:, L:]
    y_ssd, h_fin = ssd_mixer(z, xbc_pad, dt_raw, h0, lw)
    q_abs, q_rope, ckv, kr = mla_project(q_lat, kv_lat, kr_raw, pos, lw)
    if ckv_past is None:
        o_lat = mla_attend_prompt(q_abs, q_rope, ckv, kr)
    else:
        o_lat = mla_attend_sample(q_abs, q_rope, ckv, kr, ckv_past, kr_past)
    o = jnp.einsum("blhr,rhd->blhd", o_lat, lw["w_uv"]).reshape(b, L, MLA_INNER)
    y_attn = rmsnorm(o, lw["norm_attn_g"])
    mix = jnp.concatenate([y_ssd, y_attn], axis=-1) @ lw["w_out"]
    return mix, ckv, kr, new_conv, h_fin


def layer_block(x, c, pos, conv_prev, h0, ckv_past, kr_past, lw):
    ada = jax.nn.silu(c) @ lw["w_ada"] + lw["b_ada"]
    sh1, sc1, g1, sh2, sc2, g2 = jnp.split(ada, 6, axis=-1)
    mix, ckv, kr, new_conv, h_fin = mixer(modulate(x, lw["norm_mix_g"], sh1, sc1), pos,
                                          conv_prev, h0, ckv_past, kr_past, lw)
    x = x + g1[:, None, :] * mix
    u = jax.nn.relu(modulate(x, lw["norm_mlp_g"], sh2, sc2) @ lw["w_up"])
    x = x + g2[:, None, :] * (jnp.square(u) @ lw["w_down"])
    return x, ckv, kr, new_conv, h_fin


def final_norm(x, c, w_ada_final, b_ada_final, norm_final_g):
    ada = jax.nn.silu(c) @ w_ada_final + b_ada_final
    shift, scale = jnp.split(ada, 2, axis=-1)
    return modulate(x, norm_final_g, shift, scale)


def setup_inputs(seed: int = 0) -> dict:
    key = jax.random.key(seed)
    ks = jax.random.split(key, 32)
    f32 = jnp.float32
    n_pages = PAST_LEN // PAGE_SIZE
    n_phys = (DEC_BATCH * n_pages * 5) // 4

    def nrm(k, shape, scale):
        return jax.random.normal(k, shape, f32) * scale

    def gain(k, n):
        return 1.0 + nrm(k, (DEPTH, n), 0.02)

    x_prompt = nrm(ks[0], (BATCH, SEQ, D_MODEL), 1.0)
    x_sample = nrm(ks[1], (DEC_BATCH, DEC_SEQ, D_MODEL), 1.0)
    cache_kv_latent = nrm(ks[2], (DEPTH, n_phys, PAGE_SIZE, KV_RANK), 1.0)
    cache_k_rope = nrm(ks[3], (DEPTH, n_phys, PAGE_SIZE, QK_ROPE), 1.0)
    state_conv = nrm(ks[4], (DEPTH, DEC_BATCH, CONV_WIDTH - 1, CONV_DIM), 1.0)
    state_ssm = nrm(ks[5], (DEPTH, DEC_BATCH, SSD_HEADS, SSD_HEAD_DIM, SSD_STATE), 0.5)
    page_table = jax.random.permutation(ks[6], n_phys)[: DEC_BATCH * n_pages].reshape(DEC_BATCH, n_pages).astype(jnp.int32)
    c_prompt = nrm(ks[7], (BATCH, D_MODEL), 1.0)
    c_sample = nrm(ks[8], (DEC_BATCH, D_MODEL), 1.0)
    w_ada = nrm(ks[9], (DEPTH, D_MODEL, 6 * D_MODEL), 0.5 * D_MODEL ** -0.5)
    b_ada = nrm(ks[10], (DEPTH, 6 * D_MODEL), 0.02)
    norm_mix_g = gain(ks[11], D_MODEL)
    w_in = nrm(ks[12], (DEPTH, D_MODEL, IN_PROJ_DIM), D_MODEL ** -0.5)
    conv_w = nrm(ks[13], (DEPTH, CONV_WIDTH, CONV_DIM), CONV_WIDTH ** -0.5)
    conv_b = nrm(ks[14], (DEPTH, CONV_DIM), 0.02)
    dt0 = jnp.exp(jax.random.uniform(ks[15], (DEPTH, SSD_HEADS), f32, math.log(1e-3), math.log(1e-1)))
    dt_bias = dt0 + jnp.log(-jnp.expm1(-dt0))
    a_log = jnp.log(jax.random.uniform(ks[16], (DEPTH, SSD_HEADS), f32, 1.0, 16.0))
    d_skip = gain(ks[17], SSD_HEADS)
    norm_ssd_g = gain(ks[18], SSD_INNER)
    q_norm_g = gain(ks[19], Q_RANK)
    kv_norm_g = gain(ks[20], KV_RANK)
    w_uq = nrm(ks[21], (DEPTH, Q_RANK, MLA_HEADS * (QK_NOPE + QK_ROPE)), Q_RANK ** -0.5)
    w_uk = nrm(ks[22], (DEPTH, KV_RANK, MLA_HEADS, QK_NOPE), KV_RANK ** -0.5)
    w_uv = nrm(ks[23], (DEPTH, KV_RANK, MLA_HEADS, V_DIM), KV_RANK ** -0.5)
    norm_attn_g = gain(ks[24], MLA_INNER)
    w_out = nrm(ks[25], (DEPTH, MIX_WIDTH, D_MODEL), MIX_WIDTH ** -0.5)
    norm_mlp_g = gain(ks[26], D_MODEL)
    w_up = nrm(ks[27], (DEPTH, D_MODEL, D_FF), D_MODEL ** -0.5)
    w_down = nrm(ks[28], (DEPTH, D_FF, D_MODEL), D_FF ** -0.5)
    w_ada_final = nrm(ks[29], (D_MODEL, 2 * D_MODEL), 0.5 * D_MODEL ** -0.5)
    b_ada_final = nrm(ks[30], (2 * D_MODEL,), 0.02)
    norm_final_g = 1.0 + nrm(ks[31], (D_MODEL,), 0.02)
    return {"x_prompt": x_prompt, "x_sample": x_sample,
            "cache_kv_latent": cache_kv_latent, "cache_k_rope": cache_k_rope,
            "state_conv": state_conv, "state_ssm": state_ssm, "page_table": page_table,
            "c_prompt": c_prompt, "c_sample": c_sample,
            "w_ada": w_ada, "b_ada": b_ada, "norm_mix_g": norm_mix_g, "w_in": w_in,
            "conv_w": conv_w, "conv_b": conv_b, "dt_bias": dt_bias, "a_log": a_log,
            "d_skip": d_skip, "norm_ssd_g": norm_ssd_g, "q_norm_g": q_norm_g,
            "kv_norm_g": kv_norm_g, "w_uq": w_uq, "w_uk": w_uk, "w_uv": w_uv,
            "norm_attn_g": norm_attn_g, "w_out": w_out, "norm_mlp_g": norm_mlp_g,
            "w_up": w_up, "w_down": w_down, "w_ada_final": w_ada_final,
            "b_ada_final": b_ada_final, "norm_final_g": norm_final_g}


def reference(x_prompt, x_sample, cache_kv_latent, cache_k_rope, state_conv, state_ssm, page_table,
              c_prompt, c_sample, w_ada, b_ada, norm_mix_g, w_in, conv_w, conv_b, dt_bias, a_log,
              d_skip, norm_ssd_g, q_norm_g, kv_norm_g, w_uq, w_uk, w_uv, norm_attn_g, w_out,
              norm_mlp_g, w_up, w_down, w_ada_final, b_ada_final, norm_final_g):
    b_p, seq = x_prompt.shape[0], x_prompt.shape[1]
    n_seq = page_table.shape[0]
    past_len = page_table.shape[1] * PAGE_SIZE
    pos_p = jnp.arange(seq, dtype=jnp.int32)
    pos_s = past_len + jnp.arange(x_sample.shape[1], dtype=jnp.int32)
    xp, xs = x_prompt, x_sample
    kvp, krp, cvp, ssp = [], [], [], []
    kvs, krs, cvs, sss = [], [], [], []
    for l in range(DEPTH):
        lw = {"w_ada": w_ada[l], "b_ada": b_ada[l], "norm_mix_g": norm_mix_g[l], "w_in": w_in[l],
              "conv_w": conv_w[l], "conv_b": conv_b[l], "dt_bias": dt_bias[l], "a_log": a_log[l],
              "d_skip": d_skip[l], "norm_ssd_g": norm_ssd_g[l], "q_norm_g": q_norm_g[l],
              "kv_norm_g": kv_norm_g[l], "w_uq": w_uq[l], "w_uk": w_uk[l], "w_uv": w_uv[l],
              "norm_attn_g": norm_attn_g[l], "w_out": w_out[l], "norm_mlp_g": norm_mlp_g[l],
              "w_up": w_up[l], "w_down": w_down[l]}
        conv0 = jnp.zeros((b_p, CONV_WIDTH - 1, CONV_DIM), x_prompt.dtype)
        h0 = jnp.zeros((b_p, SSD_HEADS, SSD_HEAD_DIM, SSD_STATE), jnp.float32)
        xp, ckv, kr, cv, hs = layer_block(xp, c_prompt, pos_p, conv0, h0, None, None, lw)
        kvp.append(ckv)
        krp.append(kr)
        cvp.append(cv)
        ssp.append(hs)
        ckv_past = cache_kv_latent[l, page_table].reshape(n_seq, past_len, KV_RANK)
        kr_past = cache_k_rope[l, page_table].reshape(n_seq, past_len, QK_ROPE)
        xs, ckv, kr, cv, hs = layer_block(xs, c_sample, pos_s, state_conv[l], state_ssm[l],
                                          ckv_past, kr_past, lw)
        kvs.append(ckv)
        krs.append(kr)
        cvs.append(cv)
        sss.append(hs)
    y_prompt = final_norm(xp, c_prompt, w_ada_final, b_ada_final, norm_final_g)
    y_sample = final_norm(xs, c_sample, w_ada_final, b_ada_final, norm_final_g)
    return (y_prompt, y_sample, jnp.stack(kvp), jnp.stack(krp), jnp.stack(cvp), jnp.stack(ssp),
            jnp.stack(kvs), jnp.stack(krs), jnp.stack(cvs), jnp.stack(sss))
```

```python
import math
from contextlib import ExitStack
import numpy as np
import concourse.bass as bass
import concourse.mybir as mybir
from concourse.bass_utils import run_bass_kernel_spmd

F32 = mybir.dt.float32
BF16 = mybir.dt.bfloat16
I32 = mybir.dt.int32
AF = mybir.ActivationFunctionType
ALU = mybir.AluOpType

D = 1024
SEQ = 8192
NT = SEQ // 128
EPS = 1e-6
ATTN_SCALE = 1.0 / math.sqrt(96.0)
NSEQ_CORE = 16
PAST = 8192
NPAGES = 64
WIN = 2248


class MK:
    def __init__(self, nc):
        self.nc = nc
        self.eng = {"pe": nc.tensor, "act": nc.scalar, "dve": nc.vector,
                    "pool": nc.gpsimd, "sp": nc.sync}
        self.sems = {}
        self.ecnt = {}
        for k in self.eng:
            self.sems["es_" + k] = nc.alloc_semaphore("es_" + k)
            self.ecnt[k] = 0
        self.waited = {k: {} for k in self.eng}
        self.dcnt = {}
        self.last_w = {}
        self.readers = {}
        self.counting = False
        self.limit = None
        self.n = 0

    def _skip(self):
        if self.counting:
            self.n += 1
            if self.limit is not None and self.n > self.limit:
                return True
        return False

    def _deps(self, reads, writes):
        toks = []
        for k in reads:
            t = self.last_w.get(k)
            if t is not None:
                toks.append(t)
        for k in writes:
            t = self.last_w.get(k)
            if t is not None:
                toks.append(t)
            toks.extend(self.readers.get(k, ()))
        return toks

    def _wait(self, e, toks):
        best = {}
        for (s, v) in toks:
            if v > best.get(s, 0):
                best[s] = v
        w = self.waited[e]
        for s, v in best.items():
            if w.get(s, 0) < v:
                self.eng[e].wait_ge(self.sems[s], v)
                w[s] = v

    def _record(self, tok, reads, writes):
        for k in reads:
            lst = self.readers.setdefault(k, [])
            lst.append(tok)
            if len(lst) > 24:
                best = {}
                for (s, v) in lst:
                    if v > best.get(s, 0):
                        best[s] = v
                self.readers[k] = list(best.items())
        for k in writes:
            self.last_w[k] = tok
            self.readers[k] = []

    def op(self, e, fn, reads=(), writes=()):
        if self._skip():
            return None
        self._wait(e, self._deps(reads, writes))
        ins = fn()
        self.ecnt[e] += 1
        ins.then_inc(self.sems["es_" + e], 1)
        tok = ("es_" + e, self.ecnt[e])
        self._record(tok, reads, writes)
        return tok

    def _slot(self, slot):
        if slot not in self.sems:
            self.sems[slot] = self.nc.alloc_semaphore(slot)
            self.dcnt[slot] = 0

    def dma(self, q, slot, out, in_, reads=(), writes=(), **kw):
        if self._skip():
            return None
        if slot.startswith("ld_") and writes:
            slot = "d_" + writes[0]
        self._slot(slot)
        self._wait(q, self._deps(reads, writes))
        ins = self.eng[q].dma_start(out=out, in_=in_, **kw)
        self.dcnt[slot] += 16
        ins.then_inc(self.sems[slot], 16)
        tok = (slot, self.dcnt[slot])
        self._record(tok, reads, writes)
        return tok

    def gather(self, slot, out, in_, idx_ap, reads=(), writes=()):
        self._slot(slot)
        self._wait("pool", self._deps(reads, writes))
        ins = self.nc.gpsimd.indirect_dma_start(
            out=out, out_offset=None, in_=in_,
            in_offset=bass.IndirectOffsetOnAxis(ap=idx_ap, axis=0))
        self.dcnt[slot] += 16
        ins.then_inc(self.sems[slot], 16)
        tok = (slot, self.dcnt[slot])
        self._record(tok, reads, writes)
        return tok

    def barrier(self):
        toks = [(s, v) for s, v in self.dcnt.items() if v]
        for k, c in self.ecnt.items():
            if c:
                toks.append(("es_" + k, c))
        for e in self.eng:
            self._wait(e, [t for t in toks if t[0] != "es_" + e])

    def finish(self, e="sp"):
        toks = [(s, v) for s, v in self.dcnt.items() if v]
        for k, c in self.ecnt.items():
            if c and k != e:
                toks.append(("es_" + k, c))
        self._wait(e, toks)


def build(n_phys, do_prompt=True, do_sample=True, ntiles=NT, nseq=NSEQ_CORE, stop=99, sstop=99, oplimit=None):
    nc = bass.Bass("TRN2", target_bir_lowering=False)
    mk = MK(nc)
    V, A, G, PE = nc.vector, nc.scalar, nc.gpsimd, nc.tensor

    def din(name, shape, dt=F32):
        return nc.dram_tensor(name, list(shape), dt, kind="ExternalInput").ap()

    def dout(name, shape, dt=F32):
        return nc.dram_tensor(name, list(shape), dt, kind="ExternalOutput").ap()

    def dscr(name, shape, dt=F32):
        return nc.dram_tensor(name, list(shape), dt).ap()

    xp = din("xp", [SEQ, D])
    cp = din("cp", [128, D])
    xs = din("xs", [64, D])
    cs = din("cs", [64, D])
    w_ada = din("w_ada", [D, 6144])
    w_adaf = din("w_adaf", [D, 2048])
    b_ada = din("b_ada", [1, 8192])
    rows = din("rows", [1, 7 * 1024])
    small = din("small", [1, 32])
    w_in = din("w_in", [D, WIN])
    convw = din("convw", [128, 8, 5])
    qg = din("qg", [128, 3])
    w_uq = din("w_uq", [384, 1024])
    w_ukT = din("w_ukT", [64, 8 * 256])
    w_uv = din("w_uv", [256, 512])
    w_out = din("w_out", [D, D])
    w_up = din("w_up", [D, 4096])
    w_down = din("w_down", [4096, D])
    rope_tok = din("rope_tok", [SEQ + 64, 64])
    rope_ch = din("rope_ch", [64, SEQ + 64])
    consts = din("consts", [128, 4 * 128])
    if do_sample:
        cache_kv = din("cache_kv", [n_phys * 128, 256])
        cache_kr = din("cache_kr", [n_phys * 128, 32])
        ptab = din("ptab", [nseq, 64], I32)
        st_conv = din("st_conv", [nseq * 3, D])
        st_ssm = din("st_ssm", [nseq * 512, 128])

    o_y = dout("o_y", [SEQ, D])
    o_kv = dout("o_kv", [SEQ, 256])
    o_kr = dout("o_kr", [SEQ, 32])
    o_conv = dout("o_conv", [3, D])
    o_ssm = dout("o_ssm", [512, 128])
    if do_sample:
        os_y = dout("os_y", [64, D])
        os_kv = dout("os_kv", [64, 256])
        os_kr = dout("os_kr", [64, 32])
        os_conv = dout("os_conv", [nseq * 3, D])
        os_ssm = dout("os_ssm", [nseq * 512, 128])

    ada_p = dscr("ada_p", [128, 8192])
    ada_s = dscr("ada_s", [64, 8192])
    yssd_scr = dscr("yssd_scr", [SEQ, 512], BF16)
    x1_scr = dscr("x1_scr", [SEQ, D])

    es = ExitStack()

    def sb(name, shape, dt=F32):
        return es.enter_context(nc.sbuf_tensor(name, list(shape), dt))

    pb = [nc.alloc_psum_tensor("pb%d" % i, [128, 512], F32) for i in range(7)]
    pbT = nc.alloc_psum_tensor("pbT", [128, 1024], BF16)

    cst = sb("cst", [128, 512])
    mk.dma("sp", "ld_c", cst[:], consts, writes=["cst"])
    identb = sb("identb", [128, 128], BF16)
    mk.op("dve", lambda: V.tensor_copy(identb[:], cst[:, 0:128]), reads=["cst"], writes=["identb"])
    ident = cst[:, 0:128]
    M_le = cst[:, 128:256]
    M_gt = cst[:, 256:384]
    ones = cst[:, 384:512]
    cstb = sb("cstb", [128, 512], BF16)
    mk.op("dve", lambda: V.tensor_copy(cstb[:], cst[:]), reads=["cst"], writes=["cstb"])
    onesb = sb("onesb", [128, 128], BF16)
    mk.op("dve", lambda: V.tensor_copy(onesb[:], cst[:, 384:512]), reads=["cst"], writes=["onesb"])
    smallt = sb("smallt", [128, 32])
    mk.dma("sp", "ld_c", smallt[:], small.partition_broadcast(128), writes=["smallt"])
    a_t = sb("a_t", [128, 8])
    mk.op("act", lambda: A.activation(out=a_t[:], in_=smallt[:, 8:16], func=AF.Exp), reads=["smallt"], writes=["a_t"])
    mk.op("dve", lambda: V.tensor_scalar(a_t[:], a_t[:], -1.0, None, ALU.mult), reads=["a_t"], writes=["a_t"])
    dtb = smallt[:, 0:8]
    dsk = smallt[:, 16:24]
    convw_t = sb("convw_t", [128, 8, 5])
    mk.dma("sp", "ld_c", convw_t[:], convw, writes=["convw_t"])
    qg_t = sb("qg_t", [128, 3])
    mk.dma("sp", "ld_c", qg_t[:], qg, writes=["qg_t"])
    grow = sb("grow", [128, 2048])
    mk.dma("sp", "ld_c", grow[:, 0:1024], rows[:, 3072:4096].partition_broadcast(128), writes=["grow"])
    mk.dma("sp", "ld_c", grow[:, 1024:1280], rows[:, 4096:4352].partition_broadcast(128), writes=["grow"])
    g_ssd = grow[:, 0:512]
    g_attn = grow[:, 512:1024]
    g_kv = grow[:, 1024:1280]

    ss = sb("ss", [128, 4])
    junk = sb("junk", [128, 1024])

    def rstd_from(src_ap, T, n, col, keyr):
        mk.op("act", lambda: A.activation(out=junk[:T, 0:n], in_=src_ap, func=AF.Square,
                                          accum_out=ss[:T, col:col + 1]),
              reads=keyr, writes=["junk", "ss%d" % col])
        mk.op("dve", lambda: V.tensor_scalar(ss[:T, col:col + 1], ss[:T, col:col + 1], 1.0 / n, EPS, ALU.mult, ALU.add),
              reads=["ss%d" % col], writes=["ss%d" % col])
        mk.op("act", lambda: A.activation(out=ss[:T, col:col + 1], in_=ss[:T, col:col + 1], func=AF.Sqrt),
              reads=["ss%d" % col], writes=["ss%d" % col])
        mk.op("dve", lambda: V.reciprocal(ss[:T, col:col + 1], ss[:T, col:col + 1]),
              reads=["ss%d" % col], writes=["ss%d" % col])

    def load_w_bf16(dst, dst_key, src, k_chunks, c0, c1, slot):
        step = 2048
        for a in range(c0, c1, step):
            b = min(c1, a + step)
            mk.dma("pool", slot, dst[:, :, a - c0:b - c0],
                   src[:, a:b].rearrange("(k p) n -> p k n", p=128), writes=[dst_key])

    with ExitStack() as es0:
        def sb0(name, shape, dt=F32):
            return es0.enter_context(nc.sbuf_tensor(name, list(shape), dt))
        ct = sb0("ct", [128, D])
        cb = sb0("cb", [128, D], BF16)
        scT = sb0("scT", [128, 8, 256], BF16)
        wch = [sb0("wch%d" % i, [128, 8, 512], BF16) for i in range(2)]
        bch = [sb0("bch%d" % i, [128, 512]) for i in range(2)]
        ost = [sb0("ost%d" % i, [128, 512]) for i in range(2)]
        for which, (src, T, c0) in enumerate(((cp, 128, 0), (cs, 64, 128))):
            mk.dma("sp", "ld_ct", ct[:T, :], src, writes=["ct"])
            mk.op("act", lambda T=T: A.activation(out=cb[:T, :], in_=ct[:T, :], func=AF.Silu), reads=["ct"], writes=["cb"])
            for k in range(8):
                mk.op("pe", lambda k=k, T=T: PE.transpose(pbT[:, k * 128:k * 128 + T], cb[:T, k * 128:(k + 1) * 128], identb[:T, :T]),
                      reads=["cb", "identb"], writes=["pbT"])
            mk.op("dve", lambda T=T, c0=c0: V.tensor_copy(
                scT[:, :, c0:c0 + T], pbT[:].rearrange("p (k t) -> p k t", k=8)[:, :, 0:T]),
                reads=["pbT"], writes=["scT"])
        for j in range(16):
            s = j % 2
            wsrc, wc0 = (w_ada, j * 512) if j < 12 else (w_adaf, (j - 12) * 512)
            load_w_bf16(wch[s], "wch%d" % s, wsrc, 8, wc0, wc0 + 512, "ld_wch%d" % s)
            mk.dma("sp", "ld_bch%d" % s, bch[s][:], b_ada[:, j * 512:(j + 1) * 512].partition_broadcast(128),
                   writes=["bch%d" % s])
            for which, (T, c0, dst) in enumerate(((128, 0, ada_p), (64, 128, ada_s))):
                bank = pb[which]
                for k in range(8):
                    mk.op("pe", lambda k=k, T=T, c0=c0, bank=bank, s=s: PE.matmul(
                        bank[:T, :], scT[:, k, c0:c0 + T], wch[s][:, k, :], start=(k == 0), stop=(k == 7)),
                        reads=["scT", "wch%d" % s], writes=["pb%d" % which])
                o = ost[which]
                mk.op("dve", lambda T=T, bank=bank, o=o, s=s: V.tensor_tensor(o[:T, :], bank[:T, :], bch[s][:T, :], ALU.add),
                      reads=["pb%d" % which, "bch%d" % s], writes=["ost%d" % which])
                mk.dma("sp", "st_ada%d" % which, dst[:, j * 512:(j + 1) * 512], o[:T, :],
                       reads=["ost%d" % which], writes=["ada%d" % which])

    mk.barrier()

    def load_mod(gs, sh, ada_src, T, which, sc_col, sh_col, g_col, key):
        mk.dma("sp", "ld_" + key, gs[:T, :], ada_src[:, sc_col:sc_col + 1024], reads=["ada%d" % which], writes=[key + "gs"])
        mk.dma("sp", "ld_" + key, sh[:T, :], ada_src[:, sh_col:sh_col + 1024], reads=["ada%d" % which], writes=[key + "sh"])
        mk.dma("sp", "ld_" + key, junk[:T, :], rows[:, g_col:g_col + 1024].partition_broadcast(T), writes=["junk"])
        mk.op("dve", lambda: V.scalar_tensor_tensor(gs[:T, :], gs[:T, :], 1.0, junk[:T, :], ALU.add, ALU.mult),
              reads=[key + "gs", "junk"], writes=[key + "gs"])

    def front(xt, xkey, gs, sh, modkey, T, hb, hT, hkey):
        rstd_from(xt[:T, :], T, 1024, 0, [xkey])
        mk.op("dve", lambda: V.scalar_tensor_tensor(junk[:T, :], xt[:T, :], ss[:T, 0:1], gs[:T, :], ALU.mult, ALU.mult),
              reads=[xkey, "ss0", modkey + "gs"], writes=["junk"])
        mk.op("pool", lambda: G.tensor_tensor(hb[:T, :], junk[:T, :], sh[:T, :], ALU.add),
              reads=["junk", modkey + "sh"], writes=["hb"])
        transpose8(hb, "hb", T, hT, hkey)

    def transpose8(src, skey, T, dstT, dkey, nchunk=8):
        for k in range(nchunk):
            mk.op("pe", lambda k=k: PE.transpose(pbT[:, k * 128:k * 128 + T], src[:T, k * 128:(k + 1) * 128], identb[:T, :T]),
                  reads=[skey, "identb"], writes=["pbT"])
        mk.op("act", lambda: A.copy(dstT[:, 0:nchunk, :T], pbT[:].rearrange("p (k t) -> p k t", k=8)[:, 0:nchunk, 0:T]),
              reads=["pbT"], writes=[dkey])

    def lin_tok(outp, okey, hT, hkey, T, w, wkey, c0, c1, nk=8):
        for k in range(nk):
            mk.op("pe", lambda k=k: PE.matmul(outp, hT[:, k, :T], w[:, k, c0:c1], start=(k == 0), stop=(k == nk - 1)),
                  reads=[hkey, wkey], writes=[okey])

    def ssd_chunk(T, xa_T, xakey, dt_ps, z_ps, zkey, hS, hSb, hkey, ysb, tmp_tiles):
        (xtok, dtt, e3, m1, Eexp, SC, t1, t2, xw, dttb, m1b) = tmp_tiles
        for k in range(6):
            mk.op("pe", lambda k=k: PE.transpose(pbT[:T, k * 128:(k + 1) * 128], xa_T[:, k, :T], identb[:, :]),
                  reads=[xakey, "identb"], writes=["pbT"])
        mk.op("act", lambda: A.copy(xtok[:T, :], pbT[:T, 0:768]), reads=["pbT"], writes=["xtok"])
        mk.op("dve", lambda: V.tensor_tensor(dtt[:T, 0:8], dt_ps, dtb[:T, :], ALU.add), reads=["pb0", "smallt"], writes=["dtt"])
        mk.op("act", lambda: A.activation(out=dtt[:T, 0:8], in_=dtt[:T, 0:8], func=AF.Exp), reads=["dtt"], writes=["dtt"])
        mk.op("act", lambda: A.activation(out=dtt[:T, 0:8], in_=dtt[:T, 0:8], func=AF.Ln, bias=1.0), reads=["dtt"], writes=["dtt"])
        mk.op("dve", lambda: V.tensor_tensor(dtt[:T, 8:16], dtt[:T, 0:8], a_t[:T, :], ALU.mult), reads=["dtt", "a_t"], writes=["dtt"])
        dt_ = dtt[:T, 0:8]
        dtA = dtt[:T, 8:16]
        if T < 128:
            Mle, Mgt, On = cstb[:, 128:256], cstb[:, 256:384], cstb[:, 384:512]
            mk.op("dve", lambda: V.tensor_copy(dttb[:T, :], dtt[:T, 8:16]), reads=["dtt"], writes=["dttb"])
            dtA = dttb[:T, :]
            m1 = m1b
        else:
            Mle, Mgt, On = M_le, M_gt, ones
        mk.op("pe", lambda: PE.matmul(pb[4][:T, 0:8], Mle[:T, :T], dtA, start=True, stop=True), reads=["dtt", "dttb", "cst", "cstb"], writes=["pb4"])
        mk.op("pe", lambda: PE.matmul(pb[4][:T, 8:16], Mgt[:T, :T], dtA, start=True, stop=True), reads=["dtt", "dttb", "cst", "cstb"], writes=["pb4"])
        mk.op("pe", lambda: PE.matmul(pb[4][:, 16:24], On[:T, :], dtA, start=True, stop=True), reads=["dtt", "dttb", "cst", "cstb"], writes=["pb4"])
        mk.op("act", lambda: A.activation(out=e3[:T, 0:16], in_=pb[4][:T, 0:16], func=AF.Exp), reads=["pb4"], writes=["e3"])
        mk.op("act", lambda: A.activation(out=e3[:, 16:24], in_=pb[4][:, 16:24], func=AF.Exp), reads=["pb4"], writes=["e3"])
        mk.op("dve", lambda: V.tensor_tensor(e3[:T, 8:16], e3[:T, 8:16], dt_, ALU.mult), reads=["e3", "dtt"], writes=["e3"])
        for g in range(2):
            mk.op("pe", lambda g=g: PE.matmul(pb[4][:T, 128 + g * 128:128 + g * 128 + T], xa_T[:, 4 + g, :T], xa_T[:, 6 + g, :T],
                                              start=True, stop=True), reads=[xakey], writes=["pb4"])
        for h in range(8):
            mk.op("dve", lambda h=h: V.tensor_scalar(m1[:T, h, :T], M_gt[:T, :T], dtt[:T, 8 + h:9 + h], None, ALU.mult),
                  reads=["cst", "dtt"], writes=["m1_%d" % h])
            bank = pb[5 + h // 4]
            mk.op("pe", lambda h=h, bank=bank: PE.matmul(bank[:T, (h % 4) * 128:(h % 4) * 128 + T], m1[:T, h, :T], Mle[:T, :T],
                                                        start=True, stop=True),
                  reads=["m1_%d" % h, "cst", "cstb"], writes=["pb%d" % (5 + h // 4)])
        for hh in range(2):
            mk.op("act", lambda hh=hh: A.activation(
                out=Eexp[:T, hh * 4:(hh + 1) * 4, :T],
                in_=pb[5 + hh][:T, :].rearrange("p (h i) -> p h i", h=4)[:, :, 0:T], func=AF.Exp),
                reads=["pb%d" % (5 + hh)], writes=["Eexp%d" % hh])
            mk.op("dve", lambda hh=hh: V.tensor_tensor(
                Eexp[:T, hh * 4:(hh + 1) * 4, :T], Eexp[:T, hh * 4:(hh + 1) * 4, :T],
                M_le[:T, :T].unsqueeze(1).to_broadcast([T, 4, T]), ALU.mult),
                reads=["Eexp%d" % hh, "cst"], writes=["Eexp%d" % hh])
            mk.op("pool", lambda hh=hh: G.tensor_tensor(
                Eexp[:T, hh * 4:(hh + 1) * 4, :T], Eexp[:T, hh * 4:(hh + 1) * 4, :T],
                dtt[:T, hh * 4:(hh + 1) * 4].unsqueeze(2).to_broadcast([T, 4, T]), ALU.mult),
                reads=["Eexp%d" % hh, "dtt"], writes=["Eexp%d" % hh])
            mk.op("dve", lambda hh=hh: V.tensor_tensor(
                SC[:T, hh * 4:(hh + 1) * 4, :T], Eexp[:T, hh * 4:(hh + 1) * 4, :T],
                pb[4][:T, 128 + hh * 128:128 + hh * 128 + T].unsqueeze(1).to_broadcast([T, 4, T]), ALU.mult),
                reads=["Eexp%d" % hh, "pb4"], writes=["SC%d" % hh])
        for h in range(8):
            mk.op("pe", lambda h=h: PE.matmul(pb[0][:T, h * 64:(h + 1) * 64], SC[:T, h, :T], xtok[:T, h * 64:(h + 1) * 64],
                                              start=True, stop=True),
                  reads=["SC%d" % (h // 4), "xtok", "dtt"], writes=["pb0"])
        for g in range(2):
            mk.op("pe", lambda g=g: PE.matmul(pb[1][:T, g * 256:(g + 1) * 256], xa_T[:, 6 + g, :T], hSb[:, g * 256:(g + 1) * 256],
                                              start=True, stop=True),
                  reads=[xakey, hkey + "b"], writes=["pb1"])

        def v3(ap):
            return ap.rearrange("p (h d) -> p h d", h=8)

        def bc(ap8):
            return ap8.unsqueeze(2).to_broadcast([T, 8, 64])
        mk.op("dve", lambda: V.tensor_tensor(v3(t1[:T, :]), v3(pb[1][:T, :]), bc(e3[:T, 0:8]), ALU.mult),
              reads=["pb1", "e3"], writes=["t1"])
        mk.op("dve", lambda: V.tensor_tensor(t1[:T, :], t1[:T, :], pb[0][:T, :], ALU.add), reads=["t1", "pb0"], writes=["t1"])
        mk.op("pool", lambda: G.tensor_tensor(v3(t2[:T, :]), v3(xtok[:T, 0:512]), bc(dsk[:T, :]), ALU.mult),
              reads=["xtok", "smallt"], writes=["t2"])
        mk.op("pool", lambda: G.tensor_tensor(t1[:T, :], t1[:T, :], t2[:T, :], ALU.add), reads=["t1", "t2"], writes=["t1"])
        mk.op("act", lambda: A.activation(out=t2[:T, :], in_=z_ps, func=AF.Silu), reads=[zkey, "t2"], writes=["t2"])
        mk.op("dve", lambda: V.tensor_tensor(t1[:T, :], t1[:T, :], t2[:T, :], ALU.mult), reads=["t1", "t2"], writes=["t1"])
        rstd_from(t1[:T, :], T, 512, 1, ["t1"])
        mk.op("dve", lambda: V.scalar_tensor_tensor(ysb[:T, :], t1[:T, :], ss[:T, 1:2], g_ssd[:T, :], ALU.mult, ALU.mult),
              reads=["t1", "ss1", "grow"], writes=["ysb"])
        mk.op("pool", lambda: G.tensor_tensor(v3(xw[:T, :]), v3(xtok[:T, 0:512]), bc(e3[:T, 8:16]), ALU.mult),
              reads=["xtok", "e3"], writes=["xw"])
        for g in range(2):
            mk.op("pe", lambda g=g: PE.matmul(pb[2][:, g * 256:(g + 1) * 256], xtok[:T, 512 + g * 128:512 + (g + 1) * 128],
                                              xw[:T, g * 256:(g + 1) * 256], start=True, stop=True),
                  reads=["xtok", "xw"], writes=["pb2"])
        mk.op("dve", lambda: V.tensor_tensor(hS[:].rearrange("p (h d) -> p h d", h=8), hS[:].rearrange("p (h d) -> p h d", h=8),
                                             e3[:, 16:24].unsqueeze(2).to_broadcast([128, 8, 64]), ALU.mult),
              reads=[hkey, "e3"], writes=[hkey])
        mk.op("dve", lambda: V.tensor_tensor(hS[:], hS[:], pb[2][:], ALU.add), reads=[hkey, "pb2"], writes=[hkey])
        mk.op("act", lambda: A.copy(hSb[:], hS[:]), reads=[hkey], writes=[hkey + "b"])

    ns = nseq if do_sample else 0
    ntl = ntiles if do_prompt else 0
    esKV = ExitStack()

    def sbK(name, shape, dt=F32):
        return esKV.enter_context(nc.sbuf_tensor(name, list(shape), dt))
    KT = sbK("KT", [128, 2, SEQ], BF16)
    KTr = sbK("KTr", [32, SEQ], BF16)
    Vt = sbK("Vt", [128, NT, 256], BF16)
    KTs = sbK("KTs", [128, 2, 64], BF16)
    KTrs = sbK("KTrs", [32, 64], BF16)
    Vs = sbK("Vs", [4, 16, 256], BF16)
    yssd_s = dscr("yssd_s", [64, 512], BF16)
    x1_s = dscr("x1_s", [64, D])

    def v4(ap, T, h=4):
        return ap.rearrange("p (h t) -> p h t", h=h)[:, :, 0:T]

    with ExitStack() as esA:
        def sbA(name, shape, dt=F32):
            return esA.enter_context(nc.sbuf_tensor(name, list(shape), dt))
        win = sbA("win", [128, 8, WIN], BF16)
        load_w_bf16(win, "win", w_in, 8, 0, WIN, "ld_win")
        gs1 = sbA("gs1", [128, D])
        sh1 = sbA("sh1", [128, D])
        xt = [sbA("xt%d" % i, [128, D]) for i in range(2)]
        hb = sbA("hb", [128, D], BF16)
        hT = sbA("hT", [128, 8, 128], BF16)
        raw = sbA("raw", [128, 8, 131])
        acc = sbA("acc", [128, 8, 128])
        acc2 = sbA("acc2", [128, 8, 128])
        xa_T = sbA("xa_T", [128, 8, 128], BF16)
        kvf = sbA("kvf", [128, 256])
        krf = sbA("krf", [128, 64])
        krb = sbA("krb", [128, 32], BF16)
        ropet = sbA("ropet", [128, 64])
        hS = sbA("hS", [128, 512])
        hSb = sbA("hSb", [128, 512], BF16)
        ysb = sbA("ysb", [128, 512], BF16)
        stin = sbA("stin", [128, 4, 128])
        tmps = (sbA("xtok", [128, 768], BF16), sbA("dtt", [128, 16]), sbA("e3", [128, 24]),
                sbA("m1", [128, 8, 128]), sbA("Eexp", [128, 8, 128]), sbA("SC", [128, 8, 128], BF16),
                sbA("t1", [128, 512]), sbA("t2", [128, 512]), sbA("xw", [128, 512], BF16),
                sbA("dttb", [128, 8], BF16), sbA("m1b", [128, 8, 128], BF16))

        def a_tile(T, x_t, xk, rope_src, kv_dst, kr_dst, V_ap, Vkey, K0_ap, K1_ap, Kr_ap, Kkey, yssd_dst, ykey, conv_dst):
            mk.dma("sp", "ld_rope", ropet[:T, :], rope_src, writes=["ropet"])
            front(x_t, xk, gs1, sh1, "m1", T, hb, hT, "hT")
            lin_tok(pb[0][:T, 0:8], "pb0", hT, "hT", T, win, "win", 1536, 1544)
            lin_tok(pb[0][:T, 64:384], "pb0", hT, "hT", T, win, "win", 1928, 2248)
            lin_tok(pb[3][:T, :], "pb3", hT, "hT", T, win, "win", 0, 512)
            for c in range(8):
                bank = pb[1 + c // 4]
                for k in range(8):
                    mk.op("pe", lambda c=c, k=k, bank=bank: PE.matmul(
                        bank[:, (c % 4) * 128:(c % 4) * 128 + T], win[:, k, 512 + c * 128:512 + (c + 1) * 128], hT[:, k, :T],
                        start=(k == 0), stop=(k == 7)), reads=["win", "hT"], writes=["pb%d" % (1 + c // 4)])
            rstd_from(pb[0][:T, 64:320], T, 256, 2, ["pb0"])
            mk.op("dve", lambda: V.scalar_tensor_tensor(kvf[:T, :], pb[0][:T, 64:320], ss[:T, 2:3], g_kv[:T, :], ALU.mult, ALU.mult),
                  reads=["pb0", "ss2", "grow"], writes=["kvf"])
            mk.op("pool", lambda: G.tensor_copy(V_ap, kvf[:T, :]), reads=["kvf"], writes=[Vkey])
            mk.dma("sp", "st_kv", kv_dst, kvf[:T, :], reads=["kvf"])
            mk.op("dve", lambda: V.tensor_tensor(krf[:T, :], pb[0][:T, 320:384], ropet[:T, :], ALU.mult),
                  reads=["pb0", "ropet"], writes=["krf"])
            mk.op("dve", lambda: V.tensor_tensor(krf[:T, 0:32], krf[:T, 0:32], krf[:T, 32:64], ALU.add), reads=["krf"], writes=["krf"])
            mk.op("pool", lambda: G.tensor_copy(krb[:T, :], krf[:T, 0:32]), reads=["krf"], writes=["krb"])
            mk.dma("sp", "st_kr", kr_dst, krf[:T, 0:32], reads=["krf"])
            for rc in range(2):
                mk.op("pe", lambda rc=rc: PE.transpose(pbT[:, rc * 128:rc * 128 + T], V_ap[:, rc * 128:(rc + 1) * 128], identb[:T, :T]),
                      reads=[Vkey, "identb"], writes=["pbT"])
            mk.op("pe", lambda: PE.transpose(pbT[0:32, 256:256 + T], krb[:T, :], identb[:T, :T]), reads=["krb", "identb"], writes=["pbT"])
            mk.op("act", lambda: A.copy(K0_ap, pbT[:, 0:T]), reads=["pbT"], writes=[Kkey])
            mk.op("act", lambda: A.copy(K1_ap, pbT[:, 128:128 + T]), reads=["pbT"], writes=[Kkey])
            mk.op("act", lambda: A.copy(Kr_ap, pbT[0:32, 256:256 + T]), reads=["pbT"], writes=[Kkey])
            for hh in range(2):
                mk.op("act", lambda hh=hh: A.copy(raw[:, hh * 4:(hh + 1) * 4, 3:3 + T], v4(pb[1 + hh][:], T)),
                      reads=["pb%d" % (1 + hh)], writes=["raw"])
            if conv_dst is not None:
                for c in range(8):
                    mk.dma("sp", "st_conv", conv_dst[:, c * 128:(c + 1) * 128].rearrange("r p -> p r"), raw[:, c, T:T + 3],
                           reads=["raw"], allow_slow_non_contiguous=True)

            def wb(k):
                return convw_t[:, :, k:k + 1].to_broadcast([128, 8, T])
            A_ = acc[:, :, 0:T]
            B_ = acc2[:, :, 0:T]
            mk.op("dve", lambda: V.tensor_tensor(A_, raw[:, :, 0:T], wb(0), ALU.mult), reads=["raw", "convw_t"], writes=["acc"])
            mk.op("pool", lambda: G.tensor_tensor(B_, raw[:, :, 1:1 + T], wb(1), ALU.mult), reads=["raw", "convw_t"], writes=["acc2"])
            mk.op("dve", lambda: V.tensor_tensor(A_, A_, B_, ALU.add), reads=["acc", "acc2"], writes=["acc"])
            mk.op("pool", lambda: G.tensor_tensor(B_, raw[:, :, 2:2 + T], wb(2), ALU.mult), reads=["raw", "convw_t", "acc"], writes=["acc2"])
            mk.op("dve", lambda: V.tensor_tensor(A_, A_, B_, ALU.add), reads=["acc", "acc2"], writes=["acc"])
            mk.op("pool", lambda: G.tensor_tensor(B_, raw[:, :, 3:3 + T], wb(3), ALU.mult), reads=["raw", "convw_t", "acc"], writes=["acc2"])
            mk.op("dve", lambda: V.tensor_tensor(A_, A_, B_, ALU.add), reads=["acc", "acc2"], writes=["acc"])
            mk.op("dve", lambda: V.tensor_tensor(A_, A_, wb(4), ALU.add), reads=["acc", "convw_t"], writes=["acc"])
            mk.op("act", lambda: A.activation(out=xa_T[:, :, 0:T], in_=A_, func=AF.Silu), reads=["acc"], writes=["xa_T"])
            mk.op("pool", lambda: G.tensor_copy(raw[:, :, 0:3], raw[:, :, T:T + 3]), reads=["raw"], writes=["raw"])
            ssd_chunk(T, xa_T, "xa_T", pb[0][:T, 0:8], pb[3][:T, :], "pb3", hS, hSb, "hS", ysb, tmps)
            mk.dma("sp", "st_yssd", yssd_dst, ysb[:T, :], reads=["ysb"], writes=[ykey])

        def state_out(dst):
            for c4 in range(4):
                mk.op("pe", lambda c4=c4: PE.transpose(pb[5][:, c4 * 128:(c4 + 1) * 128], hS[:, c4 * 128:(c4 + 1) * 128], ident),
                      reads=["hS", "cst"], writes=["pb5"])
            mk.op("act", lambda: A.copy(junk[:, 0:512], pb[5][:]), reads=["pb5"], writes=["junk"])
            mk.dma("sp", "st_ssm", dst.rearrange("(c p) n -> p c n", p=128), junk[:, 0:512].rearrange("p (c n) -> p c n", c=4),
                   reads=["junk"])

        if ntl:
            load_mod(gs1, sh1, ada_p, 128, 0, 1024, 0, 0, "m1")
            mk.op("dve", lambda: V.memset(hS[:], 0.0), writes=["hS"])
            mk.op("dve", lambda: V.memset(hSb[:], 0.0), writes=["hSb"])
            mk.op("pool", lambda: G.memset(raw[:], 0.0), writes=["raw"])
        for t in range(ntl):
            x_t = xt[t % 2]
            xk = "xt%d" % (t % 2)
            ts_ = slice(t * 128, (t + 1) * 128)
            mk.dma("sp", "ld_" + xk, x_t[:], xp[ts_, :], writes=[xk])
            a_tile(128, x_t, xk, rope_tok[ts_, :], o_kv[ts_, :], o_kr[ts_, :], Vt[:, t, :], "Vt%d" % t,
                   KT[:, 0, ts_], KT[:, 1, ts_], KTr[:, ts_], "KT%d" % t, yssd_scr[ts_, :], "yssd%d" % t,
                   o_conv if t == ntl - 1 else None)
        if ntl:
            state_out(o_ssm)
        mk.counting = True
        mk.limit = oplimit
        for s_ in range(ns):
            x_t = xt[s_ % 2]
            xk = "xt%d" % (s_ % 2)
            r4 = slice(4 * s_, 4 * s_ + 4)
            mk.dma("sp", "ld_" + xk, x_t[:4, :], xs[r4, :], writes=[xk])
            load_mod(gs1, sh1, ada_s[r4, :], 4, 1, 1024, 0, 0, "m1")
            if sstop != 1.5:
                for c in range(8):
                    mk.dma("sp", "ld_raw", raw[:, c, 0:3], st_conv[3 * s_:3 * s_ + 3, c * 128:(c + 1) * 128].rearrange("r p -> p r"),
                           writes=["raw"], allow_slow_non_contiguous=True)
            else:
                mk.op("pool", lambda: G.memset(raw[:, :, 0:3], 0.0), writes=["raw"])
            mk.dma("sp", "ld_stin", stin[:], st_ssm[512 * s_:512 * (s_ + 1), :].rearrange("(c p) n -> p c n", p=128), writes=["stin"])
            for c4 in range(4):
                mk.op("pe", lambda c4=c4: PE.transpose(pb[5][:, c4 * 128:(c4 + 1) * 128], stin[:, c4, :], ident),
                      reads=["stin", "cst"], writes=["pb5"])
            mk.op("dve", lambda: V.tensor_copy(hS[:], pb[5][:]), reads=["pb5"], writes=["hS"])
            mk.op("act", lambda: A.copy(hSb[:], hS[:]), reads=["hS"], writes=["hSb"])
            a_tile(4, x_t, xk, rope_tok[SEQ + 4 * s_:SEQ + 4 * s_ + 4, :], os_kv[r4, :], os_kr[r4, :], Vs[:, s_, :], "Vs%d" % s_,
                   KTs[:, 0, r4], KTs[:, 1, r4], KTrs[:, r4], "KTs%d" % s_, yssd_s[r4, :], "yssds%d" % s_,
                   os_conv[3 * s_:3 * s_ + 3, :])
            state_out(os_ssm[512 * s_:512 * (s_ + 1), :])
        mk.counting = False
        mk.limit = None

    mk.barrier()
    with ExitStack() as esB:
        def sbB(name, shape, dt=F32):
            return esB.enter_context(nc.sbuf_tensor(name, list(shape), dt))
        wq = sbB("wq", [128, 8, 384], BF16)
        load_w_bf16(wq, "wq", w_in, 8, 1544, 1928, "ld_wq")
        wuq = sbB("wuq", [128, 3, 1024], BF16)
        load_w_bf16(wuq, "wuq", w_uq, 3, 0, 1024, "ld_wuq")
        wuk = sbB("wuk", [64, 2048], BF16)
        mk.dma("pool", "ld_wuk", wuk[:], w_ukT, writes=["wuk"])
        wuv = sbB("wuv", [128, 2, 512], BF16)
        load_w_bf16(wuv, "wuv", w_uv, 2, 0, 512, "ld_wuv")
        wo = sbB("wo", [128, 8, D], BF16)
        load_w_bf16(wo, "wo", w_out, 8, 0, D, "ld_wo")
        gs1 = sbB("gs1b", [128, D])
        sh1 = sbB("sh1b", [128, D])
        g1 = sbB("g1", [128, D])
        xt = [sbB("xtb%d" % i, [128, D]) for i in range(2)]
        hb = sbB("hbb", [128, D], BF16)
        hT = sbB("hTb", [128, 8, 128], BF16)
        qlT = sbB("qlT", [128, 3, 128], BF16)
        sq = sbB("sq", [128, 3, 128], BF16)
        rbc = sbB("rbc", [128, 128])
        qT = sbB("qT", [64, 8, 128], BF16)
        qa = sbB("qa", [128, 2, 1024], BF16)
        qr = sbB("qr", [32, 1024], BF16)
        qrt = sbB("qrt", [32, 1024])
        qrs = sbB("qrs", [32, 1024])
        cosT = sbB("cosT", [32, 128])
        sinT = sbB("sinT", [32, 128])
        lbc = sbB("lbc", [128, 512])
        PT = [sbB("PT%d" % i, [128, 512], BF16) for i in range(2)]
        oT = sbB("oT", [128, 2, 1024], BF16)
        of = sbB("of", [128, 512])
        mix = sbB("mix", [128, D], BF16)
        mixT = sbB("mixT", [128, 8, 128], BF16)
        x1 = sbB("x1", [128, D])
        if ns:
            Vpg = [sbB("Vpg%d" % i, [128, 288], BF16) for i in range(2)]
            KTpg = [sbB("KTpg%d" % i, [128, 2, 128], BF16) for i in range(2)]
            KTrpg = [sbB("KTrpg%d" % i, [32, 128], BF16) for i in range(2)]
            ptb = sbB("ptb", [128, 64], I32)
            idx = sbB("idx", [128, 64], I32)
            iot = sbB("iot", [128, 64], I32)
            mk.op("pool", lambda: G.iota(iot[:], [[0, 64]], base=0, channel_multiplier=1), writes=["iot"])

        def b1_tile(T, x_t, xk, cos_src, sin_src, yssd_src, ykeys, ktiles, x1_dst, x1key):
            HPH = 4 if T == 128 else 8
            NHF = 8 // HPH
            CW = HPH * T

            def hv(ap, h=HPH):
                return ap.rearrange("p (h t) -> p h t", h=h)
            mk.dma("sp", "ld_cosT", cosT[:, :T], cos_src, writes=["cosT"])
            mk.dma("sp", "ld_sinT", sinT[:, :T], sin_src, writes=["sinT"])
            mk.dma("sp", "ld_mix", mix[:T, 0:512], yssd_src, reads=ykeys, writes=["mix"])
            front(x_t, xk, gs1, sh1, "m1b", T, hb, hT, "hTb")
            for c in range(3):
                for k in range(8):
                    mk.op("pe", lambda c=c, k=k: PE.matmul(pb[0][:, c * 128:c * 128 + T], wq[:, k, c * 128:(c + 1) * 128], hT[:, k, :T],
                                                           start=(k == 0), stop=(k == 7)), reads=["wq", "hTb"], writes=["pb0"])
            for c in range(3):
                mk.op("act", lambda c=c: A.activation(out=qlT[:, c, :T], in_=pb[0][:, c * 128:c * 128 + T], func=AF.Copy,
                                                      scale=qg_t[:, c:c + 1]), reads=["pb0", "qg_t"], writes=["qlT"])
            mk.op("act", lambda: A.activation(out=sq[:, :, :T], in_=v4(pb[0][:, 0:384], T, 3), func=AF.Square),
                  reads=["pb0"], writes=["sq"])
            for c in range(3):
                mk.op("pe", lambda c=c: PE.matmul(pb[1][:, 0:T], onesb[:, :], sq[:, c, :T], start=(c == 0), stop=(c == 2)),
                      reads=["onesb", "sq"], writes=["pb1"])
            R = rbc[:, :T]
            mk.op("dve", lambda: V.tensor_scalar(R, pb[1][:, 0:T], 1.0 / 384, EPS, ALU.mult, ALU.add), reads=["pb1"], writes=["rbc"])
            mk.op("act", lambda: A.activation(out=R, in_=R, func=AF.Sqrt), reads=["rbc"], writes=["rbc"])
            mk.op("dve", lambda: V.reciprocal(R, R), reads=["rbc"], writes=["rbc"])
            mk.op("dve", lambda: V.tensor_scalar(R, R, ATTN_SCALE, None, ALU.mult), reads=["rbc"], writes=["rbc"])
            for h in range(8):
                bank = pb[2 + h // 4]
                for c in range(3):
                    mk.op("pe", lambda h=h, c=c, bank=bank: PE.matmul(bank[0:64, (h % 4) * 128:(h % 4) * 128 + T], wuq[:, c, h * 128:h * 128 + 64],
                                                                     qlT[:, c, :T], start=(c == 0), stop=(c == 2)),
                          reads=["wuq", "qlT"], writes=["pb%d" % (2 + h // 4)])
            rb = (4, 5)
            sb_ = (6, 0)
            for h in range(8):
                for (banks, off) in ((rb, 64), (sb_, 96)):
                    bi = banks[h // 4]
                    for c in range(3):
                        mk.op("pe", lambda h=h, c=c, bi=bi, off=off: PE.matmul(
                            pb[bi][0:32, (h % 4) * 128:(h % 4) * 128 + T], wuq[:, c, h * 128 + off:h * 128 + off + 32],
                            qlT[:, c, :T], start=(c == 0), stop=(c == 2)),
                            reads=["wuq", "qlT", "sq", "qlT"], writes=["pb%d" % bi])
            qr3 = qr[:, 0:8 * T].rearrange("p (h t) -> p h t", h=8)
            qrt3 = qrt[:, 0:8 * T].rearrange("p (h t) -> p h t", h=8)
            qrs3 = qrs[:, 0:8 * T].rearrange("p (h t) -> p h t", h=8)
            for hh in range(2):
                hs = slice(hh * 4, (hh + 1) * 4)
                mk.op("act", lambda hh=hh, hs=hs: A.copy(qT[:, hs, :T], v4(pb[2 + hh][0:64, :], T)),
                      reads=["pb%d" % (2 + hh)], writes=["qT"])
                mk.op("dve", lambda hh=hh, hs=hs: V.tensor_tensor(qrt3[:, hs, :], v4(pb[rb[hh]][0:32, :], T),
                                                                 cosT[:, :T].unsqueeze(1).to_broadcast([32, 4, T]), ALU.mult),
                      reads=["pb%d" % rb[hh], "cosT"], writes=["qrt"])
                mk.op("dve", lambda hh=hh, hs=hs: V.tensor_tensor(qrs3[:, hs, :], v4(pb[sb_[hh]][0:32, :], T),
                                                                 sinT[:, :T].unsqueeze(1).to_broadcast([32, 4, T]), ALU.mult),
                      reads=["pb%d" % sb_[hh], "sinT"], writes=["qrs"])
                mk.op("pool", lambda hs=hs: G.tensor_tensor(qrt3[:, hs, :], qrt3[:, hs, :], qrs3[:, hs, :], ALU.add),
                      reads=["qrt", "qrs"], writes=["qrt"])
                mk.op("dve", lambda hs=hs: V.tensor_tensor(qr3[:, hs, :], qrt3[:, hs, :],
                                                           rbc[0:32, :T].unsqueeze(1).to_broadcast([32, 4, T]), ALU.mult),
                      reads=["qrt", "rbc"], writes=["qr"])
            for rc in range(2):
                qa3 = qa[:, rc, 0:8 * T].rearrange("p (h t) -> p h t", h=8)
                for hh in range(2):
                    bank = pb[hh]
                    for h4 in range(4):
                        h = hh * 4 + h4
                        mk.op("pe", lambda h=h, h4=h4, rc=rc, bank=bank: PE.matmul(
                            bank[:, h4 * 128:h4 * 128 + T], wuk[:, h * 256 + rc * 128:h * 256 + (rc + 1) * 128], qT[0:64, h, :T],
                            start=True, stop=True), reads=["wuk", "qT"], writes=["pb%d" % hh])
                    mk.op("dve", lambda rc=rc, hh=hh, bank=bank, qa3=qa3: V.tensor_tensor(
                        qa3[:, hh * 4:(hh + 1) * 4, :], v4(bank[:], T),
                        rbc[:, :T].unsqueeze(1).to_broadcast([128, 4, T]), ALU.mult),
                        reads=["pb%d" % hh, "rbc"], writes=["qa"])
            nk_tiles = len(ktiles)
            for hf in range(NHF):
                cols = slice(hf * CW, (hf + 1) * CW)
                for ki, kd in enumerate(ktiles):
                    if kd.get("prep") is not None:
                        kd["prep"]()
                    nk = kd["nk"]
                    sbank = pb[ki % 2]
                    skey = "pb%d" % (ki % 2)
                    P = PT[ki % 2]
                    pkey = "PT%d" % (ki % 2)
                    first = (ki == 0)
                    last = (ki == nk_tiles - 1)
                    mk.op("pe", lambda: PE.matmul(sbank[:nk, :CW], kd["k0"], qa[:, 0, cols], start=True, stop=False),
                          reads=kd["keys"] + ["qa"], writes=[skey])
                    mk.op("pe", lambda: PE.matmul(sbank[:nk, :CW], kd["k1"], qa[:, 1, cols], start=False, stop=False),
                          reads=kd["keys"] + ["qa"], writes=[skey])
                    mk.op("pe", lambda: PE.matmul(sbank[:nk, :CW], kd["kr"], qr[:, cols], start=False, stop=True),
                          reads=kd["keys"] + ["qr"], writes=[skey])
                    mk.op("act", lambda: A.activation(out=P[:nk, :CW], in_=sbank[:nk, :CW], func=AF.Exp), reads=[skey], writes=[pkey])
                    if kd.get("mask") is not None:
                        mk.op("dve", lambda: V.tensor_tensor(hv(P[:nk, :CW]), hv(P[:nk, :CW]),
                                                             kd["mask"].unsqueeze(1).to_broadcast([nk, HPH, T]), ALU.mult),
                              reads=[pkey, "cst"], writes=[pkey])
                    for rc in range(2):
                        mk.op("pe", lambda rc=rc: PE.matmul(pb[2 + rc][:, :CW], kd["v"][:, rc * 128:(rc + 1) * 128], P[:nk, :CW],
                                                            start=first, stop=last),
                              reads=kd["keys"] + [pkey], writes=["pb%d" % (2 + rc)])
                    mk.op("pe", lambda: PE.matmul(pb[4][:, :CW], onesb[:nk, :], P[:nk, :CW], start=first, stop=last),
                          reads=[pkey, "onesb"], writes=["pb4"])
                mk.op("dve", lambda: V.reciprocal(lbc[:, :CW], pb[4][:, :CW]), reads=["pb4"], writes=["lbc"])
                for rc in range(2):
                    mk.op("dve", lambda rc=rc: V.tensor_tensor(oT[:, rc, cols], pb[2 + rc][:, :CW], lbc[:, :CW], ALU.mult),
                          reads=["pb%d" % (2 + rc), "lbc"], writes=["oT"])
            for h in range(8):
                for rc in range(2):
                    mk.op("pe", lambda h=h, rc=rc: PE.matmul(pb[5][:T, h * 64:(h + 1) * 64], oT[:, rc, h * T:(h + 1) * T],
                                                             wuv[:, rc, h * 64:(h + 1) * 64], start=(rc == 0), stop=(rc == 1)),
                          reads=["oT", "wuv"], writes=["pb5"])
            mk.op("act", lambda: A.copy(of[:T, :], pb[5][:T, :]), reads=["pb5"], writes=["of"])
            rstd_from(of[:T, :], T, 512, 3, ["of"])
            mk.op("dve", lambda: V.scalar_tensor_tensor(mix[:T, 512:1024], of[:T, :], ss[:T, 3:4], g_attn[:T, :], ALU.mult, ALU.mult),
                  reads=["of", "ss3", "grow"], writes=["mix"])
            transpose8(mix, "mix", T, mixT, "mixT")
            for hf in range(2):
                lin_tok(pb[5 + hf][:T, :], "pb%d" % (5 + hf), mixT, "mixT", T, wo, "wo", hf * 512, (hf + 1) * 512)
                mk.op("dve", lambda hf=hf: V.tensor_tensor(x1[:T, hf * 512:(hf + 1) * 512], pb[5 + hf][:T, :], g1[:T, hf * 512:(hf + 1) * 512], ALU.mult),
                      reads=["pb%d" % (5 + hf), "g1"], writes=["x1"])
            mk.op("pool", lambda: G.tensor_tensor(x1[:T, :], x1[:T, :], x_t[:T, :], ALU.add), reads=["x1", xk], writes=["x1"])
            mk.dma("sp", "st_x1", x1_dst, x1[:T, :], reads=["x1"], writes=[x1key])

        if ntl and stop >= 1.2:
            load_mod(gs1, sh1, ada_p, 128, 0, 1024, 0, 0, "m1b")
            mk.dma("sp", "ld_g1", g1[:], ada_p[:, 2048:3072], reads=["ada0"], writes=["g1"])
            for t in range(ntl):
                x_t = xt[t % 2]
                xk = "xtb%d" % (t % 2)
                ts_ = slice(t * 128, (t + 1) * 128)
                mk.dma("sp", "ld_" + xk, x_t[:], xp[ts_, :], writes=[xk])
                kts = []
                for kt in range(t + 1):
                    ks = slice(kt * 128, (kt + 1) * 128)
                    kts.append(dict(k0=KT[:, 0, ks], k1=KT[:, 1, ks], kr=KTr[:, ks], v=Vt[:, kt, :], nk=128,
                                    keys=["KT%d" % kt, "Vt%d" % kt], mask=(M_le if kt == t else None)))
                b1_tile(128, x_t, xk, rope_ch[0:32, ts_], rope_ch[32:64, ts_], yssd_scr[ts_, :], ["yssd%d" % t], kts,
                        x1_scr[ts_, :], "x1s%d" % t)
        for s_ in range(ns if sstop >= 2 else 0):
            x_t = xt[s_ % 2]
            xk = "xtb%d" % (s_ % 2)
            r4 = slice(4 * s_, 4 * s_ + 4)
            mk.dma("sp", "ld_" + xk, x_t[:4, :], xs[r4, :], writes=[xk])
            load_mod(gs1, sh1, ada_s[r4, :], 4, 1, 1024, 0, 0, "m1b")
            mk.dma("sp", "ld_g1", g1[:4, :], ada_s[r4, 2048:3072], reads=["ada1"], writes=["g1"])
            mk.dma("sp", "ld_ptb", ptb[:], ptab[s_:s_ + 1, :].partition_broadcast(128), writes=["ptb"])
            mk.op("dve", lambda: V.tensor_scalar(idx[:], ptb[:], 7, None, ALU.logical_shift_left), reads=["ptb"], writes=["idx"])
            mk.op("dve", lambda: V.tensor_tensor(idx[:], idx[:], iot[:], ALU.bitwise_or), reads=["idx", "iot"], writes=["idx"])
            kts = []
            for g in range(NPAGES):
                b_ = g % 2

                def prep(g=g, b_=b_):
                    mk.gather("d_Vpg%d" % b_, Vpg[b_][:, 0:256], cache_kv, idx[:, g:g + 1], reads=["idx"], writes=["Vpg%d" % b_])
                    mk.gather("d_Vpg%d" % b_, Vpg[b_][:, 256:288], cache_kr, idx[:, g:g + 1], reads=["idx"], writes=["Vpg%d" % b_])
                    for rc in range(2):
                        mk.op("pe", lambda rc=rc: PE.transpose(pbT[:, rc * 128:(rc + 1) * 128], Vpg[b_][:, rc * 128:(rc + 1) * 128], identb[:]),
                              reads=["Vpg%d" % b_, "identb"], writes=["pbT"])
                    mk.op("pe", lambda: PE.transpose(pbT[0:32, 256:384], Vpg[b_][:, 256:288], identb[:]),
                          reads=["Vpg%d" % b_, "identb"], writes=["pbT"])
                    mk.op("act", lambda: A.copy(KTpg[b_][:], pbT[:, 0:256].rearrange("p (c k) -> p c k", c=2)),
                          reads=["pbT"], writes=["KTpg%d" % b_])
                    mk.op("dve", lambda: V.tensor_copy(KTrpg[b_][:], pbT[0:32, 256:384]), reads=["pbT"], writes=["KTpg%d" % b_])
                kts.append(dict(k0=KTpg[b_][:, 0, :], k1=KTpg[b_][:, 1, :], kr=KTrpg[b_][:], v=Vpg[b_][:, 0:256], nk=128,
                                keys=["KTpg%d" % b_, "Vpg%d" % b_], mask=None, prep=prep))
            kts.append(dict(k0=KTs[:, 0, r4], k1=KTs[:, 1, r4], kr=KTrs[:, r4], v=Vs[:, s_, :], nk=4,
                            keys=["KTs%d" % s_, "Vs%d" % s_], mask=M_le[0:4, 0:4]))
            b1_tile(4, x_t, xk, rope_ch[0:32, SEQ + 4 * s_:SEQ + 4 * s_ + 4], rope_ch[32:64, SEQ + 4 * s_:SEQ + 4 * s_ + 4],
                    yssd_s[r4, :], ["yssds%d" % s_], kts, x1_s[r4, :], "x1ss%d" % s_)
    esKV.close()
    mk.barrier()

    with ExitStack() as esC:
        def sbC(name, shape, dt=F32):
            return esC.enter_context(nc.sbuf_tensor(name, list(shape), dt))
        wup = sbC("wup", [128, 8, 4096], BF16)
        load_w_bf16(wup, "wup", w_up, 8, 0, 4096, "ld_wup")
        wdn = sbC("wdn", [128, 32, D], BF16)
        load_w_bf16(wdn, "wdn", w_down, 32, 0, D, "ld_wdn")
        gs2 = sbC("gs2", [128, D])
        sh2 = sbC("sh2", [128, D])
        g2 = sbC("g2", [128, D])
        gsf = sbC("gsf", [128, D])
        shf = sbC("shf", [128, D])
        xt = [sbC("xtc%d" % i, [128, D]) for i in range(2)]
        hb = sbC("hbc", [128, D], BF16)
        hT = sbC("hTc", [128, 8, 128], BF16)
        u2T = sbC("u2T", [128, 32, 128], BF16)
        ur = sbC("ur", [128, 512])
        x2 = sbC("x2", [128, D])

        def b2_tile(T, x_t, xk, y_dst):
            front(x_t, xk, gs2, sh2, "m2", T, hb, hT, "hTc")
            for fb in range(8):
                bank = pb[fb % 4]
                bkey = "pb%d" % (fb % 4)
                for f4 in range(4):
                    fc = fb * 4 + f4
                    for k in range(8):
                        mk.op("pe", lambda fc=fc, f4=f4, k=k, bank=bank: PE.matmul(
                            bank[:, f4 * 128:f4 * 128 + T], wup[:, k, fc * 128:(fc + 1) * 128], hT[:, k, :T],
                            start=(k == 0), stop=(k == 7)), reads=["wup", "hTc"], writes=[bkey])
                urv = ur[:, 0:4 * T].rearrange("p (c t) -> p c t", c=4)
                mk.op("act", lambda bank=bank, urv=urv: A.activation(out=urv, in_=v4(bank[:], T), func=AF.Relu), reads=[bkey], writes=["ur"])
                mk.op("dve", lambda fb=fb, urv=urv: V.tensor_tensor(u2T[:, fb * 4:(fb + 1) * 4, :T], urv, urv, ALU.mult),
                      reads=["ur"], writes=["u2T"])
            for hf in range(2):
                bank = pb[4 + hf]
                for fc in range(32):
                    mk.op("pe", lambda fc=fc, hf=hf, bank=bank: PE.matmul(bank[:T, :], u2T[:, fc, :T], wdn[:, fc, hf * 512:(hf + 1) * 512],
                                                                         start=(fc == 0), stop=(fc == 31)),
                          reads=["u2T", "wdn"], writes=["pb%d" % (4 + hf)])
                mk.op("dve", lambda hf=hf, bank=bank: V.tensor_tensor(x2[:T, hf * 512:(hf + 1) * 512], bank[:T, :], g2[:T, hf * 512:(hf + 1) * 512], ALU.mult),
                      reads=["pb%d" % (4 + hf), "g2"], writes=["x2"])
            mk.op("pool", lambda: G.tensor_tensor(x2[:T, :], x2[:T, :], x_t[:T, :], ALU.add), reads=["x2", xk], writes=["x2"])
            rstd_from(x2[:T, :], T, 1024, 1, ["x2"])
            mk.op("dve", lambda: V.scalar_tensor_tensor(x2[:T, :], x2[:T, :], ss[:T, 1:2], gsf[:T, :], ALU.mult, ALU.mult),
                  reads=["x2", "ss1", "mfgs"], writes=["x2"])
            mk.op("pool", lambda: G.tensor_tensor(x2[:T, :], x2[:T, :], shf[:T, :], ALU.add), reads=["x2", "mfsh"], writes=["x2"])
            mk.dma("sp", "st_y", y_dst, x2[:T, :], reads=["x2"])

        if ntl and stop >= 3:
            load_mod(gs2, sh2, ada_p, 128, 0, 4096, 3072, 1024, "m2")
            load_mod(gsf, shf, ada_p, 128, 0, 7168, 6144, 2048, "mf")
            mk.dma("sp", "ld_g2", g2[:], ada_p[:, 5120:6144], reads=["ada0"], writes=["g2"])
            for t in range(ntl):
                x_t = xt[t % 2]
                xk = "xtc%d" % (t % 2)
                ts_ = slice(t * 128, (t + 1) * 128)
                mk.dma("sp", "ld_" + xk, x_t[:], x1_scr[ts_, :], reads=["x1s%d" % t], writes=[xk])
                b2_tile(128, x_t, xk, o_y[ts_, :])
        for s_ in range(ns if sstop >= 3 else 0):
            x_t = xt[s_ % 2]
            xk = "xtc%d" % (s_ % 2)
            r4 = slice(4 * s_, 4 * s_ + 4)
            mk.dma("sp", "ld_" + xk, x_t[:4, :], x1_s[r4, :], reads=["x1ss%d" % s_], writes=[xk])
            load_mod(gs2, sh2, ada_s[r4, :], 4, 1, 4096, 3072, 1024, "m2")
            load_mod(gsf, shf, ada_s[r4, :], 4, 1, 7168, 6144, 2048, "mf")
            mk.dma("sp", "ld_g2", g2[:4, :], ada_s[r4, 5120:6144], reads=["ada1"], writes=["g2"])
            b2_tile(4, x_t, xk, os_y[r4, :])

    mk.finish("sp")
    return nc


def _consts():
    ident = np.eye(128, dtype=np.float32)
    m = np.arange(128)
    M_le = (m[:, None] <= m[None, :]).astype(np.float32)
    M_gt = (m[:, None] > m[None, :]).astype(np.float32)
    ones = np.ones((128, 128), np.float32)
    c = np.concatenate([ident, M_le, M_gt, ones], axis=1)
    return np.ascontiguousarray(c)


def _rope_tables():
    inv = 1.0 / (10000.0 ** (np.arange(0, 32, 2, dtype=np.float32) / 32.0))
    posp = np.arange(SEQ, dtype=np.float32)
    poss = np.tile(PAST + np.arange(4, dtype=np.float32), 16)
    pos = np.concatenate([posp, poss])
    ang = pos[:, None] * inv[None, :]
    cos, sin = np.cos(ang).astype(np.float32), np.sin(ang).astype(np.float32)
    tok = np.concatenate([cos, cos, -sin, sin], axis=1)
    return np.ascontiguousarray(tok), np.ascontiguousarray(tok.T)


def _host_inputs(inp, core, n_phys_full=True, do_sample=True):
    f = np.float32
    b = core // 4
    d = {}
    d["xp"] = np.ascontiguousarray(inp["x_prompt"][b])
    d["cp"] = np.ascontiguousarray(np.broadcast_to(inp["c_prompt"][b][None, :], (128, D))).astype(f)
    s0 = core * NSEQ_CORE
    d["xs"] = np.ascontiguousarray(inp["x_sample"][s0:s0 + NSEQ_CORE].reshape(64, D))
    d["cs"] = np.ascontiguousarray(np.repeat(inp["c_sample"][s0:s0 + NSEQ_CORE], 4, axis=0))
    d["w_ada"] = np.ascontiguousarray(inp["w_ada"][0])
    d["w_adaf"] = np.ascontiguousarray(inp["w_ada_final"])
    d["b_ada"] = np.concatenate([inp["b_ada"][0], inp["b_ada_final"]])[None, :].astype(f)
    rows = np.zeros((1, 7 * 1024), f)
    rows[0, 0:1024] = inp["norm_mix_g"][0]
    rows[0, 1024:2048] = inp["norm_mlp_g"][0]
    rows[0, 2048:3072] = inp["norm_final_g"]
    rows[0, 3072:3584] = inp["norm_ssd_g"][0]
    rows[0, 3584:4096] = inp["norm_attn_g"][0]
    rows[0, 4096:4352] = inp["kv_norm_g"][0]
    d["rows"] = rows
    small = np.zeros((1, 32), f)
    small[0, 0:8] = inp["dt_bias"][0]
    small[0, 8:16] = inp["a_log"][0]
    small[0, 16:24] = inp["d_skip"][0]
    d["small"] = small
    w_in = inp["w_in"][0]
    kr = w_in[:, 2184:2216]
    kr_sw = np.concatenate([kr[:, 16:32], kr[:, 0:16]], axis=1)
    d["w_in"] = np.ascontiguousarray(np.concatenate([w_in, kr_sw], axis=1))
    cw = np.concatenate([inp["conv_w"][0], inp["conv_b"][0][None, :]], axis=0)
    d["convw"] = np.ascontiguousarray(cw.T.reshape(8, 128, 5).transpose(1, 0, 2))
    d["qg"] = np.ascontiguousarray(inp["q_norm_g"][0].reshape(3, 128).T)
    wq = inp["w_uq"][0].reshape(384, 8, 96)
    rope = wq[:, :, 64:96]
    rope_sw = np.concatenate([rope[:, :, 16:32], rope[:, :, 0:16]], axis=2)
    d["w_uq"] = np.ascontiguousarray(np.concatenate([wq, rope_sw], axis=2).reshape(384, 1024))
    d["w_ukT"] = np.ascontiguousarray(inp["w_uk"][0].transpose(2, 1, 0).reshape(64, 2048))
    d["w_uv"] = np.ascontiguousarray(inp["w_uv"][0].reshape(256, 512))
    d["w_out"] = np.ascontiguousarray(inp["w_out"][0])
    d["w_up"] = np.ascontiguousarray(inp["w_up"][0])
    d["w_down"] = np.ascontiguousarray(inp["w_down"][0])
    rt, rc = _rope_tables()
    d["rope_tok"] = rt
    d["rope_ch"] = rc
    d["consts"] = _consts()
    if do_sample:
        d["cache_kv"] = inp["cache_kv_latent"][0].reshape(-1, 256)
        d["cache_kr"] = inp["cache_k_rope"][0].reshape(-1, 32)
        d["ptab"] = np.ascontiguousarray(inp["page_table"][s0:s0 + NSEQ_CORE]).astype(np.int32)
        d["st_conv"] = np.ascontiguousarray(inp["state_conv"][0, s0:s0 + NSEQ_CORE].reshape(-1, D))
        d["st_ssm"] = np.ascontiguousarray(inp["state_ssm"][0, s0:s0 + NSEQ_CORE].reshape(-1, 128))
    return d


def kernel(**inputs):
    inp = {k: np.asarray(v) for k, v in inputs.items()}
    n_phys = int(inp["cache_kv_latent"].shape[1])
    nc = build(n_phys)
    in_maps = [_host_inputs(inp, c) for c in range(8)]
    res = run_bass_kernel_spmd(nc, in_maps, core_ids=list(range(8)))
    r = res.results
    f = np.float32
    pc = (0, 4)
    y_p = np.stack([r[c]["o_y"] for c in pc]).astype(f)
    kv_p = np.stack([r[c]["o_kv"] for c in pc])[None].astype(f)
    kr_p = np.stack([r[c]["o_kr"] for c in pc])[None].astype(f)
    conv_p = np.stack([r[c]["o_conv"] for c in pc])[None].astype(f)
    ssm_p = np.stack([r[c]["o_ssm"].reshape(8, 64, 128) for c in pc])[None].astype(f)
    y_s = np.concatenate([r[c]["os_y"] for c in range(8)]).reshape(128, 4, D).astype(f)
    kv_s = np.concatenate([r[c]["os_kv"] for c in range(8)]).reshape(1, 128, 4, 256).astype(f)
    kr_s = np.concatenate([r[c]["os_kr"] for c in range(8)]).reshape(1, 128, 4, 32).astype(f)
    conv_s = np.concatenate([r[c]["os_conv"] for c in range(8)]).reshape(1, 128, 3, D).astype(f)
    ssm_s = np.concatenate([r[c]["os_ssm"] for c in range(8)]).reshape(1, 128, 8, 64, 128).astype(f)
    return (y_p, y_s, kv_p, kr_p, conv_p, ssm_p, kv_s, kr_s, conv_s, ssm_s)
```

```python
import math
from contextlib import ExitStack
import numpy as np
import concourse.bass as bass
import concourse.mybir as mybir
from concourse.bass_utils import run_bass_kernel_spmd

F32 = mybir.dt.float32
BF16 = mybir.dt.bfloat16
I32 = mybir.dt.int32
AF = mybir.ActivationFunctionType
ALU = mybir.AluOpType

D = 1024
SEQ = 8192
NT = SEQ // 128
EPS = 1e-6
ATTN_SCALE = 1.0 / math.sqrt(96.0)
NSEQ_CORE = 16
PAST = 8192
NPAGES = 64
WIN = 2248


class MK:
    def __init__(self, nc):
        self.nc = nc
        self.eng = {"pe": nc.tensor, "act": nc.scalar, "dve": nc.vector,
                    "pool": nc.gpsimd, "sp": nc.sync}
        self.sems = {}
        self.ecnt = {}
        for k in self.eng:
            self.sems["es_" + k] = nc.alloc_semaphore("es_" + k)
            self.ecnt[k] = 0
        self.waited = {k: {} for k in self.eng}
        self.dcnt = {}
        self.last_w = {}
        self.readers = {}
        self.counting = False
        self.limit = None
        self.n = 0

    def _skip(self):
        if self.counting:
            self.n += 1
            if self.limit is not None and self.n > self.limit:
                return True
        return False

    def _deps(self, reads, writes):
        toks = []
        for k in reads:
            t = self.last_w.get(k)
            if t is not None:
                toks.append(t)
        for k in writes:
            t = self.last_w.get(k)
            if t is not None:
                toks.append(t)
            toks.extend(self.readers.get(k, ()))
        return toks

    def _wait(self, e, toks):
        best = {}
        for (s, v) in toks:
            if v > best.get(s, 0):
                best[s] = v
        w = self.waited[e]
        for s, v in best.items():
            if w.get(s, 0) < v:
                self.eng[e].wait_ge(self.sems[s], v)
                w[s] = v

    def _record(self, tok, reads, writes):
        for k in reads:
            lst = self.readers.setdefault(k, [])
            lst.append(tok)
            if len(lst) > 24:
                best = {}
                for (s, v) in lst:
                    if v > best.get(s, 0):
                        best[s] = v
                self.readers[k] = list(best.items())
        for k in writes:
            self.last_w[k] = tok
            self.readers[k] = []

    def op(self, e, fn, reads=(), writes=()):
        if self._skip():
            return None
        self._wait(e, self._deps(reads, writes))
        ins = fn()
        self.ecnt[e] += 1
        ins.then_inc(self.sems["es_" + e], 1)
        tok = ("es_" + e, self.ecnt[e])
        self._record(tok, reads, writes)
        return tok

    def _slot(self, slot):
        if slot not in self.sems:
            self.sems[slot] = self.nc.alloc_semaphore(slot)
            self.dcnt[slot] = 0

    def dma(self, q, slot, out, in_, reads=(), writes=(), **kw):
        if self._skip():
            return None
        if slot.startswith("ld_") and writes:
            slot = "d_" + writes[0]
        self._slot(slot)
        self._wait(q, self._deps(reads, writes))
        ins = self.eng[q].dma_start(out=out, in_=in_, **kw)
        self.dcnt[slot] += 16
        ins.then_inc(self.sems[slot], 16)
        tok = (slot, self.dcnt[slot])
        self._record(tok, reads, writes)
        return tok

    def gather(self, slot, out, in_, idx_ap, reads=(), writes=()):
        self._slot(slot)
        self._wait("pool", self._deps(reads, writes))
        ins = self.nc.gpsimd.indirect_dma_start(
            out=out, out_offset=None, in_=in_,
            in_offset=bass.IndirectOffsetOnAxis(ap=idx_ap, axis=0))
        self.dcnt[slot] += 16
        ins.then_inc(self.sems[slot], 16)
        tok = (slot, self.dcnt[slot])
        self._record(tok, reads, writes)
        return tok

    def barrier(self):
        toks = [(s, v) for s, v in self.dcnt.items() if v]
        for k, c in self.ecnt.items():
            if c:
                toks.append(("es_" + k, c))
        for e in self.eng:
            self._wait(e, [t for t in toks if t[0] != "es_" + e])

    def finish(self, e="sp"):
        toks = [(s, v) for s, v in self.dcnt.items() if v]
        for k, c in self.ecnt.items():
            if c and k != e:
                toks.append(("es_" + k, c))
        self._wait(e, toks)


def build(n_phys, do_prompt=True, do_sample=True, ntiles=NT, nseq=NSEQ_CORE, stop=99, sstop=99, oplimit=None):
    nc = bass.Bass("TRN2", target_bir_lowering=False)
    mk = MK(nc)
    V, A, G, PE = nc.vector, nc.scalar, nc.gpsimd, nc.tensor

    def din(name, shape, dt=F32):
        return nc.dram_tensor(name, list(shape), dt, kind="ExternalInput").ap()

    def dout(name, shape, dt=F32):
        return nc.dram_tensor(name, list(shape), dt, kind="ExternalOutput").ap()

    def dscr(name, shape, dt=F32):
        return nc.dram_tensor(name, list(shape), dt).ap()

    xp = din("xp", [SEQ, D])
    cp = din("cp", [128, D])
    xs = din("xs", [64, D])
    cs = din("cs", [64, D])
    w_ada = din("w_ada", [D, 6144])
    w_adaf = din("w_adaf", [D, 2048])
    b_ada = din("b_ada", [1, 8192])
    rows = din("rows", [1, 7 * 1024])
    small = din("small", [1, 32])
    w_in = din("w_in", [D, WIN])
    convw = din("convw", [128, 8, 5])
    qg = din("qg", [128, 3])
    w_uq = din("w_uq", [384, 1024])
    w_ukT = din("w_ukT", [64, 8 * 256])
    w_uv = din("w_uv", [256, 512])
    w_out = din("w_out", [D, D])
    w_up = din("w_up", [D, 4096])
    w_down = din("w_down", [4096, D])
    rope_tok = din("rope_tok", [SEQ + 64, 64])
    rope_ch = din("rope_ch", [64, SEQ + 64])
    consts = din("consts", [128, 4 * 128])
    balanced = do_prompt and ntiles > 0 and ntiles % 8 == 0
    NB = ntiles // 4 if balanced else ntiles
    if balanced:
        x_own = din("x_own", [NB * 128, D])
        rope_own = din("rope_own", [64, NB * 128])
        idx_own = din("idx_own", [128, NB], I32)
        amask = din("amask", [128, 8, 128])
    if do_sample:
        cache_kv = din("cache_kv", [n_phys * 128, 256])
        cache_kr = din("cache_kr", [n_phys * 128, 32])
        ptab = din("ptab", [nseq, 64], I32)
        st_conv = din("st_conv", [nseq * 3, D])
        st_ssm = din("st_ssm", [nseq * 512, 128])

    o_y = dout("o_y", [NB * 128 if balanced else SEQ, D])
    o_kv = dout("o_kv", [SEQ, 256])
    o_kr = dout("o_kr", [SEQ, 32])
    o_conv = dout("o_conv", [3, D])
    o_ssm = dout("o_ssm", [512, 128])
    if do_sample:
        os_y = dout("os_y", [64, D])
        os_kv = dout("os_kv", [64, 256])
        os_kr = dout("os_kr", [64, 32])
        os_conv = dout("os_conv", [nseq * 3, D])
        os_ssm = dout("os_ssm", [nseq * 512, 128])

    ada_p = dscr("ada_p", [128, 8192])
    ada_s = dscr("ada_s", [64, 8192])
    yssd_scr = dscr("yssd_scr", [SEQ, 512], BF16)
    x1_scr = dscr("x1_scr", [SEQ, D])

    es = ExitStack()

    def sb(name, shape, dt=F32):
        return es.enter_context(nc.sbuf_tensor(name, list(shape), dt))

    pb = [nc.alloc_psum_tensor("pb%d" % i, [128, 512], F32) for i in range(7)]
    pbT = nc.alloc_psum_tensor("pbT", [128, 1024], BF16)

    cst = sb("cst", [128, 512])
    mk.dma("sp", "ld_c", cst[:], consts, writes=["cst"])
    identb = sb("identb", [128, 128], BF16)
    mk.op("dve", lambda: V.tensor_copy(identb[:], cst[:, 0:128]), reads=["cst"], writes=["identb"])
    ident = cst[:, 0:128]
    M_le = cst[:, 128:256]
    M_gt = cst[:, 256:384]
    ones = cst[:, 384:512]
    cstb = sb("cstb", [128, 512], BF16)
    mk.op("dve", lambda: V.tensor_copy(cstb[:], cst[:]), reads=["cst"], writes=["cstb"])
    onesb = sb("onesb", [128, 128], BF16)
    mk.op("dve", lambda: V.tensor_copy(onesb[:], cst[:, 384:512]), reads=["cst"], writes=["onesb"])
    smallt = sb("smallt", [128, 32])
    mk.dma("sp", "ld_c", smallt[:], small.partition_broadcast(128), writes=["smallt"])
    a_t = sb("a_t", [128, 8])
    mk.op("act", lambda: A.activation(out=a_t[:], in_=smallt[:, 8:16], func=AF.Exp), reads=["smallt"], writes=["a_t"])
    mk.op("dve", lambda: V.tensor_scalar(a_t[:], a_t[:], -1.0, None, ALU.mult), reads=["a_t"], writes=["a_t"])
    dtb = smallt[:, 0:8]
    dsk = smallt[:, 16:24]
    convw_t = sb("convw_t", [128, 8, 5])
    mk.dma("sp", "ld_c", convw_t[:], convw, writes=["convw_t"])
    qg_t = sb("qg_t", [128, 3])
    mk.dma("sp", "ld_c", qg_t[:], qg, writes=["qg_t"])
    grow = sb("grow", [128, 2048])
    mk.dma("sp", "ld_c", grow[:, 0:1024], rows[:, 3072:4096].partition_broadcast(128), writes=["grow"])
    mk.dma("sp", "ld_c", grow[:, 1024:1280], rows[:, 4096:4352].partition_broadcast(128), writes=["grow"])
    g_ssd = grow[:, 0:512]
    g_attn = grow[:, 512:1024]
    g_kv = grow[:, 1024:1280]

    ss = sb("ss", [128, 4])
    junk = sb("junk", [128, 1024])

    def rstd_from(src_ap, T, n, col, keyr):
        mk.op("act", lambda: A.activation(out=junk[:T, 0:n], in_=src_ap, func=AF.Square,
                                          accum_out=ss[:T, col:col + 1]),
              reads=keyr, writes=["junk", "ss%d" % col])
        mk.op("dve", lambda: V.tensor_scalar(ss[:T, col:col + 1], ss[:T, col:col + 1], 1.0 / n, EPS, ALU.mult, ALU.add),
              reads=["ss%d" % col], writes=["ss%d" % col])
        mk.op("act", lambda: A.activation(out=ss[:T, col:col + 1], in_=ss[:T, col:col + 1], func=AF.Sqrt),
              reads=["ss%d" % col], writes=["ss%d" % col])
        mk.op("dve", lambda: V.reciprocal(ss[:T, col:col + 1], ss[:T, col:col + 1]),
              reads=["ss%d" % col], writes=["ss%d" % col])

    def load_w_bf16(dst, dst_key, src, k_chunks, c0, c1, slot):
        step = 2048
        for a in range(c0, c1, step):
            b = min(c1, a + step)
            mk.dma("pool", slot, dst[:, :, a - c0:b - c0],
                   src[:, a:b].rearrange("(k p) n -> p k n", p=128), writes=[dst_key])

    with ExitStack() as es0:
        def sb0(name, shape, dt=F32):
            return es0.enter_context(nc.sbuf_tensor(name, list(shape), dt))
        ct = sb0("ct", [128, D])
        cb = sb0("cb", [128, D], BF16)
        scT = sb0("scT", [128, 8, 256], BF16)
        wch = [sb0("wch%d" % i, [128, 8, 512], BF16) for i in range(2)]
        bch = [sb0("bch%d" % i, [128, 512]) for i in range(2)]
        ost = [sb0("ost%d" % i, [128, 512]) for i in range(2)]
        for which, (src, T, c0) in enumerate(((cp, 128, 0), (cs, 64, 128))):
            mk.dma("sp", "ld_ct", ct[:T, :], src, writes=["ct"])
            mk.op("act", lambda T=T: A.activation(out=cb[:T, :], in_=ct[:T, :], func=AF.Silu), reads=["ct"], writes=["cb"])
            for k in range(8):
                mk.op("pe", lambda k=k, T=T: PE.transpose(pbT[:, k * 128:k * 128 + T], cb[:T, k * 128:(k + 1) * 128], identb[:T, :T]),
                      reads=["cb", "identb"], writes=["pbT"])
            mk.op("dve", lambda T=T, c0=c0: V.tensor_copy(
                scT[:, :, c0:c0 + T], pbT[:].rearrange("p (k t) -> p k t", k=8)[:, :, 0:T]),
                reads=["pbT"], writes=["scT"])
        for j in range(16):
            s = j % 2
            wsrc, wc0 = (w_ada, j * 512) if j < 12 else (w_adaf, (j - 12) * 512)
            load_w_bf16(wch[s], "wch%d" % s, wsrc, 8, wc0, wc0 + 512, "ld_wch%d" % s)
            mk.dma("sp", "ld_bch%d" % s, bch[s][:], b_ada[:, j * 512:(j + 1) * 512].partition_broadcast(128),
                   writes=["bch%d" % s])
            for which, (T, c0, dst) in enumerate(((128, 0, ada_p), (64, 128, ada_s))):
                bank = pb[which]
                for k in range(8):
                    mk.op("pe", lambda k=k, T=T, c0=c0, bank=bank, s=s: PE.matmul(
                        bank[:T, :], scT[:, k, c0:c0 + T], wch[s][:, k, :], start=(k == 0), stop=(k == 7)),
                        reads=["scT", "wch%d" % s], writes=["pb%d" % which])
                o = ost[which]
                mk.op("dve", lambda T=T, bank=bank, o=o, s=s: V.tensor_tensor(o[:T, :], bank[:T, :], bch[s][:T, :], ALU.add),
                      reads=["pb%d" % which, "bch%d" % s], writes=["ost%d" % which])
                mk.dma("sp", "st_ada%d" % which, dst[:, j * 512:(j + 1) * 512], o[:T, :],
                       reads=["ost%d" % which], writes=["ada%d" % which])

    mk.barrier()

    def load_mod(gs, sh, ada_src, T, which, sc_col, sh_col, g_col, key):
        mk.dma("sp", "ld_" + key, gs[:T, :], ada_src[:, sc_col:sc_col + 1024], reads=["ada%d" % which], writes=[key + "gs"])
        mk.dma("sp", "ld_" + key, sh[:T, :], ada_src[:, sh_col:sh_col + 1024], reads=["ada%d" % which], writes=[key + "sh"])
        mk.dma("sp", "ld_" + key, junk[:T, :], rows[:, g_col:g_col + 1024].partition_broadcast(T), writes=["junk"])
        mk.op("dve", lambda: V.scalar_tensor_tensor(gs[:T, :], gs[:T, :], 1.0, junk[:T, :], ALU.add, ALU.mult),
              reads=[key + "gs", "junk"], writes=[key + "gs"])

    def front(xt, xkey, gs, sh, modkey, T, hb, hT, hkey):
        rstd_from(xt[:T, :], T, 1024, 0, [xkey])
        mk.op("dve", lambda: V.scalar_tensor_tensor(junk[:T, :], xt[:T, :], ss[:T, 0:1], gs[:T, :], ALU.mult, ALU.mult),
              reads=[xkey, "ss0", modkey + "gs"], writes=["junk"])
        mk.op("pool", lambda: G.tensor_tensor(hb[:T, :], junk[:T, :], sh[:T, :], ALU.add),
              reads=["junk", modkey + "sh"], writes=["hb"])
        transpose8(hb, "hb", T, hT, hkey)

    def transpose8(src, skey, T, dstT, dkey, nchunk=8):
        for k in range(nchunk):
            mk.op("pe", lambda k=k: PE.transpose(pbT[:, k * 128:k * 128 + T], src[:T, k * 128:(k + 1) * 128], identb[:T, :T]),
                  reads=[skey, "identb"], writes=["pbT"])
        mk.op("act", lambda: A.copy(dstT[:, 0:nchunk, :T], pbT[:].rearrange("p (k t) -> p k t", k=8)[:, 0:nchunk, 0:T]),
              reads=["pbT"], writes=[dkey])

    def lin_tok(outp, okey, hT, hkey, T, w, wkey, c0, c1, nk=8):
        for k in range(nk):
            mk.op("pe", lambda k=k: PE.matmul(outp, hT[:, k, :T], w[:, k, c0:c1], start=(k == 0), stop=(k == nk - 1)),
                  reads=[hkey, wkey], writes=[okey])

    def ssd_chunk(T, xa_T, xakey, dt_ps, z_ps, zkey, hS, hSb, hkey, ysb, tmp_tiles):
        (xtok, dtt, e3, m1, Eexp, SC, t1, t2, xw, dttb, m1b) = tmp_tiles
        for k in range(6):
            mk.op("pe", lambda k=k: PE.transpose(pbT[:T, k * 128:(k + 1) * 128], xa_T[:, k, :T], identb[:, :]),
                  reads=[xakey, "identb"], writes=["pbT"])
        mk.op("act", lambda: A.copy(xtok[:T, :], pbT[:T, 0:768]), reads=["pbT"], writes=["xtok"])
        mk.op("dve", lambda: V.tensor_tensor(dtt[:T, 0:8], dt_ps, dtb[:T, :], ALU.add), reads=["pb0", "smallt"], writes=["dtt"])
        mk.op("act", lambda: A.activation(out=dtt[:T, 0:8], in_=dtt[:T, 0:8], func=AF.Exp), reads=["dtt"], writes=["dtt"])
        mk.op("act", lambda: A.activation(out=dtt[:T, 0:8], in_=dtt[:T, 0:8], func=AF.Ln, bias=1.0), reads=["dtt"], writes=["dtt"])
        mk.op("dve", lambda: V.tensor_tensor(dtt[:T, 8:16], dtt[:T, 0:8], a_t[:T, :], ALU.mult), reads=["dtt", "a_t"], writes=["dtt"])
        dt_ = dtt[:T, 0:8]
        dtA = dtt[:T, 8:16]
        if T < 128:
            Mle, Mgt, On = cstb[:, 128:256], cstb[:, 256:384], cstb[:, 384:512]
            mk.op("dve", lambda: V.tensor_copy(dttb[:T, :], dtt[:T, 8:16]), reads=["dtt"], writes=["dttb"])
            dtA = dttb[:T, :]
            m1 = m1b
        else:
            Mle, Mgt, On = M_le, M_gt, ones
        mk.op("pe", lambda: PE.matmul(pb[4][:T, 0:8], Mle[:T, :T], dtA, start=True, stop=True), reads=["dtt", "dttb", "cst", "cstb"], writes=["pb4"])
        mk.op("pe", lambda: PE.matmul(pb[4][:T, 8:16], Mgt[:T, :T], dtA, start=True, stop=True), reads=["dtt", "dttb", "cst", "cstb"], writes=["pb4"])
        mk.op("pe", lambda: PE.matmul(pb[4][:, 16:24], On[:T, :], dtA, start=True, stop=True), reads=["dtt", "dttb", "cst", "cstb"], writes=["pb4"])
        mk.op("act", lambda: A.activation(out=e3[:T, 0:16], in_=pb[4][:T, 0:16], func=AF.Exp), reads=["pb4"], writes=["e3"])
        mk.op("act", lambda: A.activation(out=e3[:, 16:24], in_=pb[4][:, 16:24], func=AF.Exp), reads=["pb4"], writes=["e3"])
        mk.op("dve", lambda: V.tensor_tensor(e3[:T, 8:16], e3[:T, 8:16], dt_, ALU.mult), reads=["e3", "dtt"], writes=["e3"])
        for g in range(2):
            mk.op("pe", lambda g=g: PE.matmul(pb[4][:T, 128 + g * 128:128 + g * 128 + T], xa_T[:, 4 + g, :T], xa_T[:, 6 + g, :T],
                                              start=True, stop=True), reads=[xakey], writes=["pb4"])
        for h in range(8):
            mk.op("dve", lambda h=h: V.tensor_scalar(m1[:T, h, :T], M_gt[:T, :T], dtt[:T, 8 + h:9 + h], None, ALU.mult),
                  reads=["cst", "dtt"], writes=["m1_%d" % h])
            bank = pb[5 + h // 4]
            mk.op("pe", lambda h=h, bank=bank: PE.matmul(bank[:T, (h % 4) * 128:(h % 4) * 128 + T], m1[:T, h, :T], Mle[:T, :T],
                                                        start=True, stop=True),
                  reads=["m1_%d" % h, "cst", "cstb"], writes=["pb%d" % (5 + h // 4)])
        for hh in range(2):
            mk.op("act", lambda hh=hh: A.activation(
                out=Eexp[:T, hh * 4:(hh + 1) * 4, :T],
                in_=pb[5 + hh][:T, :].rearrange("p (h i) -> p h i", h=4)[:, :, 0:T], func=AF.Exp),
                reads=["pb%d" % (5 + hh)], writes=["Eexp%d" % hh])
            mk.op("dve", lambda hh=hh: V.tensor_tensor(
                Eexp[:T, hh * 4:(hh + 1) * 4, :T], Eexp[:T, hh * 4:(hh + 1) * 4, :T],
                M_le[:T, :T].unsqueeze(1).to_broadcast([T, 4, T]), ALU.mult),
                reads=["Eexp%d" % hh, "cst"], writes=["Eexp%d" % hh])
            mk.op("pool", lambda hh=hh: G.tensor_tensor(
                Eexp[:T, hh * 4:(hh + 1) * 4, :T], Eexp[:T, hh * 4:(hh + 1) * 4, :T],
                dtt[:T, hh * 4:(hh + 1) * 4].unsqueeze(2).to_broadcast([T, 4, T]), ALU.mult),
                reads=["Eexp%d" % hh, "dtt"], writes=["Eexp%d" % hh])
            mk.op("dve", lambda hh=hh: V.tensor_tensor(
                SC[:T, hh * 4:(hh + 1) * 4, :T], Eexp[:T, hh * 4:(hh + 1) * 4, :T],
                pb[4][:T, 128 + hh * 128:128 + hh * 128 + T].unsqueeze(1).to_broadcast([T, 4, T]), ALU.mult),
                reads=["Eexp%d" % hh, "pb4"], writes=["SC%d" % hh])
        for h in range(8):
            mk.op("pe", lambda h=h: PE.matmul(pb[0][:T, h * 64:(h + 1) * 64], SC[:T, h, :T], xtok[:T, h * 64:(h + 1) * 64],
                                              start=True, stop=True),
                  reads=["SC%d" % (h // 4), "xtok", "dtt"], writes=["pb0"])
        for g in range(2):
            mk.op("pe", lambda g=g: PE.matmul(pb[1][:T, g * 256:(g + 1) * 256], xa_T[:, 6 + g, :T], hSb[:, g * 256:(g + 1) * 256],
                                              start=True, stop=True),
                  reads=[xakey, hkey + "b"], writes=["pb1"])

        def v3(ap):
            return ap.rearrange("p (h d) -> p h d", h=8)

        def bc(ap8):
            return ap8.unsqueeze(2).to_broadcast([T, 8, 64])
        mk.op("dve", lambda: V.tensor_tensor(v3(t1[:T, :]), v3(pb[1][:T, :]), bc(e3[:T, 0:8]), ALU.mult),
              reads=["pb1", "e3"], writes=["t1"])
        mk.op("dve", lambda: V.tensor_tensor(t1[:T, :], t1[:T, :], pb[0][:T, :], ALU.add), reads=["t1", "pb0"], writes=["t1"])
        mk.op("pool", lambda: G.tensor_tensor(v3(t2[:T, :]), v3(xtok[:T, 0:512]), bc(dsk[:T, :]), ALU.mult),
              reads=["xtok", "smallt"], writes=["t2"])
        mk.op("pool", lambda: G.tensor_tensor(t1[:T, :], t1[:T, :], t2[:T, :], ALU.add), reads=["t1", "t2"], writes=["t1"])
        mk.op("act", lambda: A.activation(out=t2[:T, :], in_=z_ps, func=AF.Silu), reads=[zkey, "t2"], writes=["t2"])
        mk.op("dve", lambda: V.tensor_tensor(t1[:T, :], t1[:T, :], t2[:T, :], ALU.mult), reads=["t1", "t2"], writes=["t1"])
        rstd_from(t1[:T, :], T, 512, 1, ["t1"])
        mk.op("dve", lambda: V.scalar_tensor_tensor(ysb[:T, :], t1[:T, :], ss[:T, 1:2], g_ssd[:T, :], ALU.mult, ALU.mult),
              reads=["t1", "ss1", "grow"], writes=["ysb"])
        mk.op("pool", lambda: G.tensor_tensor(v3(xw[:T, :]), v3(xtok[:T, 0:512]), bc(e3[:T, 8:16]), ALU.mult),
              reads=["xtok", "e3"], writes=["xw"])
        for g in range(2):
            mk.op("pe", lambda g=g: PE.matmul(pb[2][:, g * 256:(g + 1) * 256], xtok[:T, 512 + g * 128:512 + (g + 1) * 128],
                                              xw[:T, g * 256:(g + 1) * 256], start=True, stop=True),
                  reads=["xtok", "xw"], writes=["pb2"])
        mk.op("dve", lambda: V.tensor_tensor(hS[:].rearrange("p (h d) -> p h d", h=8), hS[:].rearrange("p (h d) -> p h d", h=8),
                                             e3[:, 16:24].unsqueeze(2).to_broadcast([128, 8, 64]), ALU.mult),
              reads=[hkey, "e3"], writes=[hkey])
        mk.op("dve", lambda: V.tensor_tensor(hS[:], hS[:], pb[2][:], ALU.add), reads=[hkey, "pb2"], writes=[hkey])
        mk.op("act", lambda: A.copy(hSb[:], hS[:]), reads=[hkey], writes=[hkey + "b"])

    ns = nseq if do_sample else 0
    ntl = ntiles if do_prompt else 0
    esKV = ExitStack()

    def sbK(name, shape, dt=F32):
        return esKV.enter_context(nc.sbuf_tensor(name, list(shape), dt))
    KT = sbK("KT", [128, 2, SEQ], BF16)
    KTr = sbK("KTr", [32, SEQ], BF16)
    Vt = sbK("Vt", [128, NT, 256], BF16)
    KTs = sbK("KTs", [128, 2, 64], BF16)
    KTrs = sbK("KTrs", [32, 64], BF16)
    Vs = sbK("Vs", [4, 16, 256], BF16)
    yssd_s = dscr("yssd_s", [64, 512], BF16)
    x1_s = dscr("x1_s", [64, D])

    def v4(ap, T, h=4):
        return ap.rearrange("p (h t) -> p h t", h=h)[:, :, 0:T]

    with ExitStack() as esA:
        def sbA(name, shape, dt=F32):
            return esA.enter_context(nc.sbuf_tensor(name, list(shape), dt))
        win = sbA("win", [128, 8, WIN], BF16)
        load_w_bf16(win, "win", w_in, 8, 0, WIN, "ld_win")
        gs1 = sbA("gs1", [128, D])
        sh1 = sbA("sh1", [128, D])
        xt = [sbA("xt%d" % i, [128, D]) for i in range(2)]
        hb = sbA("hb", [128, D], BF16)
        hT = sbA("hT", [128, 8, 128], BF16)
        raw = sbA("raw", [128, 8, 131])
        acc = sbA("acc", [128, 8, 128])
        acc2 = sbA("acc2", [128, 8, 128])
        xa_T = sbA("xa_T", [128, 8, 128], BF16)
        kvf = sbA("kvf", [128, 256])
        krf = sbA("krf", [128, 64])
        krb = sbA("krb", [128, 32], BF16)
        ropet = sbA("ropet", [128, 64])
        hS = sbA("hS", [128, 512])
        hSb = sbA("hSb", [128, 512], BF16)
        ysb = sbA("ysb", [128, 512], BF16)
        stin = sbA("stin", [128, 4, 128])
        tmps = (sbA("xtok", [128, 768], BF16), sbA("dtt", [128, 16]), sbA("e3", [128, 24]),
                sbA("m1", [128, 8, 128]), sbA("Eexp", [128, 8, 128]), sbA("SC", [128, 8, 128], BF16),
                sbA("t1", [128, 512]), sbA("t2", [128, 512]), sbA("xw", [128, 512], BF16),
                sbA("dttb", [128, 8], BF16), sbA("m1b", [128, 8, 128], BF16))

        def a_tile(T, x_t, xk, rope_src, kv_dst, kr_dst, V_ap, Vkey, K0_ap, K1_ap, Kr_ap, Kkey, yssd_dst, ykey, conv_dst):
            mk.dma("sp", "ld_rope", ropet[:T, :], rope_src, writes=["ropet"])
            front(x_t, xk, gs1, sh1, "m1", T, hb, hT, "hT")
            lin_tok(pb[0][:T, 0:8], "pb0", hT, "hT", T, win, "win", 1536, 1544)
            lin_tok(pb[0][:T, 64:384], "pb0", hT, "hT", T, win, "win", 1928, 2248)
            lin_tok(pb[3][:T, :], "pb3", hT, "hT", T, win, "win", 0, 512)
            for c in range(8):
                bank = pb[1 + c // 4]
                for k in range(8):
                    mk.op("pe", lambda c=c, k=k, bank=bank: PE.matmul(
                        bank[:, (c % 4) * 128:(c % 4) * 128 + T], win[:, k, 512 + c * 128:512 + (c + 1) * 128], hT[:, k, :T],
                        start=(k == 0), stop=(k == 7)), reads=["win", "hT"], writes=["pb%d" % (1 + c // 4)])
            rstd_from(pb[0][:T, 64:320], T, 256, 2, ["pb0"])
            mk.op("dve", lambda: V.scalar_tensor_tensor(kvf[:T, :], pb[0][:T, 64:320], ss[:T, 2:3], g_kv[:T, :], ALU.mult, ALU.mult),
                  reads=["pb0", "ss2", "grow"], writes=["kvf"])
            mk.op("pool", lambda: G.tensor_copy(V_ap, kvf[:T, :]), reads=["kvf"], writes=[Vkey])
            mk.dma("sp", "st_kv", kv_dst, kvf[:T, :], reads=["kvf"])
            mk.op("dve", lambda: V.tensor_tensor(krf[:T, :], pb[0][:T, 320:384], ropet[:T, :], ALU.mult),
                  reads=["pb0", "ropet"], writes=["krf"])
            mk.op("dve", lambda: V.tensor_tensor(krf[:T, 0:32], krf[:T, 0:32], krf[:T, 32:64], ALU.add), reads=["krf"], writes=["krf"])
            mk.op("pool", lambda: G.tensor_copy(krb[:T, :], krf[:T, 0:32]), reads=["krf"], writes=["krb"])
            mk.dma("sp", "st_kr", kr_dst, krf[:T, 0:32], reads=["krf"])
            for rc in range(2):
                mk.op("pe", lambda rc=rc: PE.transpose(pbT[:, rc * 128:rc * 128 + T], V_ap[:, rc * 128:(rc + 1) * 128], identb[:T, :T]),
                      reads=[Vkey, "identb"], writes=["pbT"])
            mk.op("pe", lambda: PE.transpose(pbT[0:32, 256:256 + T], krb[:T, :], identb[:T, :T]), reads=["krb", "identb"], writes=["pbT"])
            mk.op("act", lambda: A.copy(K0_ap, pbT[:, 0:T]), reads=["pbT"], writes=[Kkey])
            mk.op("act", lambda: A.copy(K1_ap, pbT[:, 128:128 + T]), reads=["pbT"], writes=[Kkey])
            mk.op("act", lambda: A.copy(Kr_ap, pbT[0:32, 256:256 + T]), reads=["pbT"], writes=[Kkey])
            for hh in range(2):
                mk.op("act", lambda hh=hh: A.copy(raw[:, hh * 4:(hh + 1) * 4, 3:3 + T], v4(pb[1 + hh][:], T)),
                      reads=["pb%d" % (1 + hh)], writes=["raw"])
            if conv_dst is not None:
                for c in range(8):
                    mk.dma("sp", "st_conv", conv_dst[:, c * 128:(c + 1) * 128].rearrange("r p -> p r"), raw[:, c, T:T + 3],
                           reads=["raw"], allow_slow_non_contiguous=True)

            def wb(k):
                return convw_t[:, :, k:k + 1].to_broadcast([128, 8, T])
            A_ = acc[:, :, 0:T]
            B_ = acc2[:, :, 0:T]
            mk.op("dve", lambda: V.tensor_tensor(A_, raw[:, :, 0:T], wb(0), ALU.mult), reads=["raw", "convw_t"], writes=["acc"])
            mk.op("pool", lambda: G.tensor_tensor(B_, raw[:, :, 1:1 + T], wb(1), ALU.mult), reads=["raw", "convw_t"], writes=["acc2"])
            mk.op("dve", lambda: V.tensor_tensor(A_, A_, B_, ALU.add), reads=["acc", "acc2"], writes=["acc"])
            mk.op("pool", lambda: G.tensor_tensor(B_, raw[:, :, 2:2 + T], wb(2), ALU.mult), reads=["raw", "convw_t", "acc"], writes=["acc2"])
            mk.op("dve", lambda: V.tensor_tensor(A_, A_, B_, ALU.add), reads=["acc", "acc2"], writes=["acc"])
            mk.op("pool", lambda: G.tensor_tensor(B_, raw[:, :, 3:3 + T], wb(3), ALU.mult), reads=["raw", "convw_t", "acc"], writes=["acc2"])
            mk.op("dve", lambda: V.tensor_tensor(A_, A_, B_, ALU.add), reads=["acc", "acc2"], writes=["acc"])
            mk.op("dve", lambda: V.tensor_tensor(A_, A_, wb(4), ALU.add), reads=["acc", "convw_t"], writes=["acc"])
            mk.op("act", lambda: A.activation(out=xa_T[:, :, 0:T], in_=A_, func=AF.Silu), reads=["acc"], writes=["xa_T"])
            mk.op("pool", lambda: G.tensor_copy(raw[:, :, 0:3], raw[:, :, T:T + 3]), reads=["raw"], writes=["raw"])
            ssd_chunk(T, xa_T, "xa_T", pb[0][:T, 0:8], pb[3][:T, :], "pb3", hS, hSb, "hS", ysb, tmps)
            mk.dma("sp", "st_yssd", yssd_dst, ysb[:T, :], reads=["ysb"], writes=[ykey])

        def state_out(dst):
            for c4 in range(4):
                mk.op("pe", lambda c4=c4: PE.transpose(pb[5][:, c4 * 128:(c4 + 1) * 128], hS[:, c4 * 128:(c4 + 1) * 128], ident),
                      reads=["hS", "cst"], writes=["pb5"])
            mk.op("act", lambda: A.copy(junk[:, 0:512], pb[5][:]), reads=["pb5"], writes=["junk"])
            mk.dma("sp", "st_ssm", dst.rearrange("(c p) n -> p c n", p=128), junk[:, 0:512].rearrange("p (c n) -> p c n", c=4),
                   reads=["junk"])

        if ntl:
            load_mod(gs1, sh1, ada_p, 128, 0, 1024, 0, 0, "m1")
            mk.op("dve", lambda: V.memset(hS[:], 0.0), writes=["hS"])
            mk.op("dve", lambda: V.memset(hSb[:], 0.0), writes=["hSb"])
            mk.op("pool", lambda: G.memset(raw[:], 0.0), writes=["raw"])
        for t in range(ntl):
            x_t = xt[t % 2]
            xk = "xt%d" % (t % 2)
            ts_ = slice(t * 128, (t + 1) * 128)
            mk.dma("sp", "ld_" + xk, x_t[:], xp[ts_, :], writes=[xk])
            a_tile(128, x_t, xk, rope_tok[ts_, :], o_kv[ts_, :], o_kr[ts_, :], Vt[:, t, :], "Vt%d" % t,
                   KT[:, 0, ts_], KT[:, 1, ts_], KTr[:, ts_], "KT%d" % t, yssd_scr[ts_, :], "yssd%d" % t,
                   o_conv if t == ntl - 1 else None)
        if ntl:
            state_out(o_ssm)
        mk.counting = True
        mk.limit = oplimit
        for s_ in range(ns):
            x_t = xt[s_ % 2]
            xk = "xt%d" % (s_ % 2)
            r4 = slice(4 * s_, 4 * s_ + 4)
            mk.dma("sp", "ld_" + xk, x_t[:4, :], xs[r4, :], writes=[xk])
            load_mod(gs1, sh1, ada_s[r4, :], 4, 1, 1024, 0, 0, "m1")
            if sstop != 1.5:
                for c in range(8):
                    mk.dma("sp", "ld_raw", raw[:, c, 0:3], st_conv[3 * s_:3 * s_ + 3, c * 128:(c + 1) * 128].rearrange("r p -> p r"),
                           writes=["raw"], allow_slow_non_contiguous=True)
            else:
                mk.op("pool", lambda: G.memset(raw[:, :, 0:3], 0.0), writes=["raw"])
            mk.dma("sp", "ld_stin", stin[:], st_ssm[512 * s_:512 * (s_ + 1), :].rearrange("(c p) n -> p c n", p=128), writes=["stin"])
            for c4 in range(4):
                mk.op("pe", lambda c4=c4: PE.transpose(pb[5][:, c4 * 128:(c4 + 1) * 128], stin[:, c4, :], ident),
                      reads=["stin", "cst"], writes=["pb5"])
            mk.op("dve", lambda: V.tensor_copy(hS[:], pb[5][:]), reads=["pb5"], writes=["hS"])
            mk.op("act", lambda: A.copy(hSb[:], hS[:]), reads=["hS"], writes=["hSb"])
            a_tile(4, x_t, xk, rope_tok[SEQ + 4 * s_:SEQ + 4 * s_ + 4, :], os_kv[r4, :], os_kr[r4, :], Vs[:, s_, :], "Vs%d" % s_,
                   KTs[:, 0, r4], KTs[:, 1, r4], KTrs[:, r4], "KTs%d" % s_, yssd_s[r4, :], "yssds%d" % s_,
                   os_conv[3 * s_:3 * s_ + 3, :])
            state_out(os_ssm[512 * s_:512 * (s_ + 1), :])
        mk.counting = False
        mk.limit = None

    mk.barrier()
    with ExitStack() as esB:
        def sbB(name, shape, dt=F32):
            return esB.enter_context(nc.sbuf_tensor(name, list(shape), dt))
        wq = sbB("wq", [128, 8, 384], BF16)
        load_w_bf16(wq, "wq", w_in, 8, 1544, 1928, "ld_wq")
        wuq = sbB("wuq", [128, 3, 1024], BF16)
        load_w_bf16(wuq, "wuq", w_uq, 3, 0, 1024, "ld_wuq")
        wuk = sbB("wuk", [64, 2048], BF16)
        mk.dma("pool", "ld_wuk", wuk[:], w_ukT, writes=["wuk"])
        wuv = sbB("wuv", [128, 2, 512], BF16)
        load_w_bf16(wuv, "wuv", w_uv, 2, 0, 512, "ld_wuv")
        wo = sbB("wo", [128, 8, D], BF16)
        load_w_bf16(wo, "wo", w_out, 8, 0, D, "ld_wo")
        gs1 = sbB("gs1b", [128, D])
        sh1 = sbB("sh1b", [128, D])
        g1 = sbB("g1", [128, D])
        xt = [sbB("xtb%d" % i, [128, D]) for i in range(2)]
        hb = sbB("hbb", [128, D], BF16)
        hT = sbB("hTb", [128, 8, 128], BF16)
        qlT = sbB("qlT", [128, 3, 128], BF16)
        sq = sbB("sq", [128, 3, 128], BF16)
        rbc = sbB("rbc", [128, 128])
        qT = sbB("qT", [64, 8, 128], BF16)
        qa = sbB("qa", [128, 2, 1024], BF16)
        qr = sbB("qr", [32, 1024], BF16)
        qrt = sbB("qrt", [32, 1024])
        qrs = sbB("qrs", [32, 1024])
        cosT = sbB("cosT", [32, 128])
        sinT = sbB("sinT", [32, 128])
        lbc = sbB("lbc", [128, 512])
        PT = [sbB("PT%d" % i, [128, 512], BF16) for i in range(2)]
        oT = sbB("oT", [128, 2, 1024], BF16)
        of = sbB("of", [128, 512])
        mix = sbB("mix", [128, D], BF16)
        mixT = sbB("mixT", [128, 8, 128], BF16)
        x1 = sbB("x1", [128, D])
        if ns:
            Vpg = [sbB("Vpg%d" % i, [128, 288], BF16) for i in range(4)]
            KTpg = [sbB("KTpg%d" % i, [128, 2, 128], BF16) for i in range(4)]
            KTrpg = [sbB("KTrpg%d" % i, [32, 128], BF16) for i in range(4)]
            ptb = sbB("ptb", [128, 64], I32)
            idx = sbB("idx", [128, 64], I32)
            iot = sbB("iot", [128, 64], I32)
            mk.op("pool", lambda: G.iota(iot[:], [[0, 64]], base=0, channel_multiplier=1), writes=["iot"])

        def b1_tile(T, x_t, xk, cos_src, sin_src, yssd_src, ykeys, ktiles, x1_dst, x1key):
            HPH = 4 if T == 128 else 8
            NHF = 8 // HPH
            CW = HPH * T

            def hv(ap, h=HPH):
                return ap.rearrange("p (h t) -> p h t", h=h)
            mk.dma("sp", "ld_cosT", cosT[:, :T], cos_src, writes=["cosT"])
            mk.dma("sp", "ld_sinT", sinT[:, :T], sin_src, writes=["sinT"])
            if callable(yssd_src):
                yssd_src()
            else:
                mk.dma("sp", "ld_mix", mix[:T, 0:512], yssd_src, reads=ykeys, writes=["mix"])
            front(x_t, xk, gs1, sh1, "m1b", T, hb, hT, "hTb")
            for c in range(3):
                for k in range(8):
                    mk.op("pe", lambda c=c, k=k: PE.matmul(pb[0][:, c * 128:c * 128 + T], wq[:, k, c * 128:(c + 1) * 128], hT[:, k, :T],
                                                           start=(k == 0), stop=(k == 7)), reads=["wq", "hTb"], writes=["pb0"])
            for c in range(3):
                mk.op("act", lambda c=c: A.activation(out=qlT[:, c, :T], in_=pb[0][:, c * 128:c * 128 + T], func=AF.Copy,
                                                      scale=qg_t[:, c:c + 1]), reads=["pb0", "qg_t"], writes=["qlT"])
            mk.op("act", lambda: A.activation(out=sq[:, :, :T], in_=v4(pb[0][:, 0:384], T, 3), func=AF.Square),
                  reads=["pb0"], writes=["sq"])
            for c in range(3):
                mk.op("pe", lambda c=c: PE.matmul(pb[1][:, 0:T], onesb[:, :], sq[:, c, :T], start=(c == 0), stop=(c == 2)),
                      reads=["onesb", "sq"], writes=["pb1"])
            R = rbc[:, :T]
            mk.op("dve", lambda: V.tensor_scalar(R, pb[1][:, 0:T], 1.0 / 384, EPS, ALU.mult, ALU.add), reads=["pb1"], writes=["rbc"])
            mk.op("act", lambda: A.activation(out=R, in_=R, func=AF.Sqrt), reads=["rbc"], writes=["rbc"])
            mk.op("dve", lambda: V.reciprocal(R, R), reads=["rbc"], writes=["rbc"])
            mk.op("dve", lambda: V.tensor_scalar(R, R, ATTN_SCALE, None, ALU.mult), reads=["rbc"], writes=["rbc"])
            for h in range(8):
                bank = pb[2 + h // 4]
                for c in range(3):
                    mk.op("pe", lambda h=h, c=c, bank=bank: PE.matmul(bank[0:64, (h % 4) * 128:(h % 4) * 128 + T], wuq[:, c, h * 128:h * 128 + 64],
                                                                     qlT[:, c, :T], start=(c == 0), stop=(c == 2)),
                          reads=["wuq", "qlT"], writes=["pb%d" % (2 + h // 4)])
            rb = (4, 5)
            sb_ = (6, 0)
            for h in range(8):
                for (banks, off) in ((rb, 64), (sb_, 96)):
                    bi = banks[h // 4]
                    for c in range(3):
                        mk.op("pe", lambda h=h, c=c, bi=bi, off=off: PE.matmul(
                            pb[bi][0:32, (h % 4) * 128:(h % 4) * 128 + T], wuq[:, c, h * 128 + off:h * 128 + off + 32],
                            qlT[:, c, :T], start=(c == 0), stop=(c == 2)),
                            reads=["wuq", "qlT", "sq", "qlT"], writes=["pb%d" % bi])
            qr3 = qr[:, 0:8 * T].rearrange("p (h t) -> p h t", h=8)
            qrt3 = qrt[:, 0:8 * T].rearrange("p (h t) -> p h t", h=8)
            qrs3 = qrs[:, 0:8 * T].rearrange("p (h t) -> p h t", h=8)
            for hh in range(2):
                hs = slice(hh * 4, (hh + 1) * 4)
                mk.op("act", lambda hh=hh, hs=hs: A.copy(qT[:, hs, :T], v4(pb[2 + hh][0:64, :], T)),
                      reads=["pb%d" % (2 + hh)], writes=["qT"])
                mk.op("dve", lambda hh=hh, hs=hs: V.tensor_tensor(qrt3[:, hs, :], v4(pb[rb[hh]][0:32, :], T),
                                                                 cosT[:, :T].unsqueeze(1).to_broadcast([32, 4, T]), ALU.mult),
                      reads=["pb%d" % rb[hh], "cosT"], writes=["qrt"])
                mk.op("dve", lambda hh=hh, hs=hs: V.tensor_tensor(qrs3[:, hs, :], v4(pb[sb_[hh]][0:32, :], T),
                                                                 sinT[:, :T].unsqueeze(1).to_broadcast([32, 4, T]), ALU.mult),
                      reads=["pb%d" % sb_[hh], "sinT"], writes=["qrs"])
                mk.op("pool", lambda hs=hs: G.tensor_tensor(qrt3[:, hs, :], qrt3[:, hs, :], qrs3[:, hs, :], ALU.add),
                      reads=["qrt", "qrs"], writes=["qrt"])
                mk.op("dve", lambda hs=hs: V.tensor_tensor(qr3[:, hs, :], qrt3[:, hs, :],
                                                           rbc[0:32, :T].unsqueeze(1).to_broadcast([32, 4, T]), ALU.mult),
                      reads=["qrt", "rbc"], writes=["qr"])
            for rc in range(2):
                qa3 = qa[:, rc, 0:8 * T].rearrange("p (h t) -> p h t", h=8)
                for hh in range(2):
                    bank = pb[hh]
                    for h4 in range(4):
                        h = hh * 4 + h4
                        mk.op("pe", lambda h=h, h4=h4, rc=rc, bank=bank: PE.matmul(
                            bank[:, h4 * 128:h4 * 128 + T], wuk[:, h * 256 + rc * 128:h * 256 + (rc + 1) * 128], qT[0:64, h, :T],
                            start=True, stop=True), reads=["wuk", "qT"], writes=["pb%d" % hh])
                    mk.op("dve", lambda rc=rc, hh=hh, bank=bank, qa3=qa3: V.tensor_tensor(
                        qa3[:, hh * 4:(hh + 1) * 4, :], v4(bank[:], T),
                        rbc[:, :T].unsqueeze(1).to_broadcast([128, 4, T]), ALU.mult),
                        reads=["pb%d" % hh, "rbc"], writes=["qa"])
            nk_tiles = len(ktiles)
            for hf in range(NHF):
                cols = slice(hf * CW, (hf + 1) * CW)

                def st_S(ki):
                    kd = ktiles[ki]
                    nk = kd["nk"]
                    sbank = pb[ki % 2]
                    skey = "pb%d" % (ki % 2)
                    P = PT[ki % 2]
                    pkey = "PT%d" % (ki % 2)
                    mk.op("pe", lambda: PE.matmul(sbank[:nk, :CW], kd["k0"], qa[:, 0, cols], start=True, stop=False),
                          reads=kd["keys"] + ["qa"], writes=[skey])
                    mk.op("pe", lambda: PE.matmul(sbank[:nk, :CW], kd["k1"], qa[:, 1, cols], start=False, stop=False),
                          reads=kd["keys"] + ["qa"], writes=[skey])
                    mk.op("pe", lambda: PE.matmul(sbank[:nk, :CW], kd["kr"], qr[:, cols], start=False, stop=True),
                          reads=kd["keys"] + ["qr"], writes=[skey])
                    mk.op("act", lambda: A.activation(out=P[:nk, :CW], in_=sbank[:nk, :CW], func=AF.Exp), reads=[skey], writes=[pkey])
                    if kd.get("mask") is not None:
                        mk.op("dve", lambda: V.tensor_tensor(hv(P[:nk, :CW]), hv(P[:nk, :CW]),
                                                             kd["mask"].unsqueeze(1).to_broadcast([nk, HPH, T]), ALU.mult),
                              reads=[pkey] + kd.get("mkeys", ["cst"]), writes=[pkey])

                def st_PV(ki):
                    kd = ktiles[ki]
                    nk = kd["nk"]
                    P = PT[ki % 2]
                    pkey = "PT%d" % (ki % 2)
                    first = (ki == 0)
                    last = (ki == nk_tiles - 1)
                    for rc in range(2):
                        mk.op("pe", lambda rc=rc: PE.matmul(pb[2 + rc][:, :CW], kd["v"][:, rc * 128:(rc + 1) * 128], P[:nk, :CW],
                                                            start=first, stop=last),
                              reads=kd["keys"] + [pkey], writes=["pb%d" % (2 + rc)])
                    mk.op("pe", lambda: PE.matmul(pb[4][:, :CW], onesb[:nk, :], P[:nk, :CW], start=first, stop=last),
                          reads=[pkey, "onesb"], writes=["pb4"])

                for it in range(nk_tiles + 2):
                    if it < nk_tiles and ktiles[it].get("prep") is not None:
                        ktiles[it]["prep"]()
                    if 0 <= it - 1 < nk_tiles:
                        st_S(it - 1)
                    if 0 <= it - 2 < nk_tiles:
                        st_PV(it - 2)
                mk.op("dve", lambda: V.reciprocal(lbc[:, :CW], pb[4][:, :CW]), reads=["pb4"], writes=["lbc"])
                for rc in range(2):
                    mk.op("dve", lambda rc=rc: V.tensor_tensor(oT[:, rc, cols], pb[2 + rc][:, :CW], lbc[:, :CW], ALU.mult),
                          reads=["pb%d" % (2 + rc), "lbc"], writes=["oT"])
            for h in range(8):
                for rc in range(2):
                    mk.op("pe", lambda h=h, rc=rc: PE.matmul(pb[5][:T, h * 64:(h + 1) * 64], oT[:, rc, h * T:(h + 1) * T],
                                                             wuv[:, rc, h * 64:(h + 1) * 64], start=(rc == 0), stop=(rc == 1)),
                          reads=["oT", "wuv"], writes=["pb5"])
            mk.op("act", lambda: A.copy(of[:T, :], pb[5][:T, :]), reads=["pb5"], writes=["of"])
            rstd_from(of[:T, :], T, 512, 3, ["of"])
            mk.op("dve", lambda: V.scalar_tensor_tensor(mix[:T, 512:1024], of[:T, :], ss[:T, 3:4], g_attn[:T, :], ALU.mult, ALU.mult),
                  reads=["of", "ss3", "grow"], writes=["mix"])
            transpose8(mix, "mix", T, mixT, "mixT")
            for hf in range(2):
                lin_tok(pb[5 + hf][:T, :], "pb%d" % (5 + hf), mixT, "mixT", T, wo, "wo", hf * 512, (hf + 1) * 512)
                mk.op("dve", lambda hf=hf: V.tensor_tensor(x1[:T, hf * 512:(hf + 1) * 512], pb[5 + hf][:T, :], g1[:T, hf * 512:(hf + 1) * 512], ALU.mult),
                      reads=["pb%d" % (5 + hf), "g1"], writes=["x1"])
            mk.op("pool", lambda: G.tensor_tensor(x1[:T, :], x1[:T, :], x_t[:T, :], ALU.add), reads=["x1", xk], writes=["x1"])
            mk.dma("sp", "st_x1", x1_dst, x1[:T, :], reads=["x1"], writes=[x1key])

        if ntl and stop >= 1.2:
            load_mod(gs1, sh1, ada_p, 128, 0, 1024, 0, 0, "m1b")
            mk.dma("sp", "ld_g1", g1[:], ada_p[:, 2048:3072], reads=["ada0"], writes=["g1"])
            if balanced:
                amask_t = sbB("amask_t", [128, 8, 128], BF16)
                mk.dma("pool", "ld_amask", amask_t[:], amask, writes=["amask_t"])
                idxo = sbB("idxo", [128, NB], I32)
                mk.dma("sp", "ld_idxo", idxo[:], idx_own, writes=["idxo"])
            for m in range(NB):
                x_t = xt[m % 2]
                xk = "xtb%d" % (m % 2)
                ms = slice(m * 128, (m + 1) * 128)
                kts = []
                if balanced:
                    mk.dma("sp", "ld_" + xk, x_t[:], x_own[ms, :], writes=[xk])
                    sel = 0 if m < NB // 2 else 1
                    for kt in range(4 * m + 4):
                        ks = slice(kt * 128, (kt + 1) * 128)
                        msk = amask_t[:, sel * 4 + (kt - 4 * m), :] if kt >= 4 * m else None
                        kts.append(dict(k0=KT[:, 0, ks], k1=KT[:, 1, ks], kr=KTr[:, ks], v=Vt[:, kt, :], nk=128,
                                        keys=["KT%d" % kt, "Vt%d" % kt], mask=msk, mkeys=["amask_t"]))

                    def ld_mix(m=m):
                        mk.gather("d_mix", mix[:, 0:512], yssd_scr[0:ntl * 128, :], idxo[:, m:m + 1], reads=["idxo"], writes=["mix"])
                    b1_tile(128, x_t, xk, rope_own[0:32, ms], rope_own[32:64, ms], ld_mix, [], kts, x1_scr[ms, :], "x1s%d" % m)
                else:
                    t = m
                    mk.dma("sp", "ld_" + xk, x_t[:], xp[ms, :], writes=[xk])
                    for kt in range(t + 1):
                        ks = slice(kt * 128, (kt + 1) * 128)
                        kts.append(dict(k0=KT[:, 0, ks], k1=KT[:, 1, ks], kr=KTr[:, ks], v=Vt[:, kt, :], nk=128,
                                        keys=["KT%d" % kt, "Vt%d" % kt], mask=(M_le if kt == t else None)))
                    b1_tile(128, x_t, xk, rope_ch[0:32, ms], rope_ch[32:64, ms], yssd_scr[ms, :], ["yssd%d" % t], kts,
                            x1_scr[ms, :], "x1s%d" % t)
        for s_ in range(ns if sstop >= 2 else 0):
            x_t = xt[s_ % 2]
            xk = "xtb%d" % (s_ % 2)
            r4 = slice(4 * s_, 4 * s_ + 4)
            mk.dma("sp", "ld_" + xk, x_t[:4, :], xs[r4, :], writes=[xk])
            load_mod(gs1, sh1, ada_s[r4, :], 4, 1, 1024, 0, 0, "m1b")
            mk.dma("sp", "ld_g1", g1[:4, :], ada_s[r4, 2048:3072], reads=["ada1"], writes=["g1"])
            mk.dma("sp", "ld_ptb", ptb[:], ptab[s_:s_ + 1, :].partition_broadcast(128), writes=["ptb"])
            mk.op("dve", lambda: V.tensor_scalar(idx[:], ptb[:], 7, None, ALU.logical_shift_left), reads=["ptb"], writes=["idx"])
            mk.op("dve", lambda: V.tensor_tensor(idx[:], idx[:], iot[:], ALU.bitwise_or), reads=["idx", "iot"], writes=["idx"])
            kts = []
            for g in range(NPAGES):
                b_ = g % 4

                def prep(g=g, b_=b_):
                    mk.gather("d_Vpg%d" % b_, Vpg[b_][:, 0:256], cache_kv, idx[:, g:g + 1], reads=["idx"], writes=["Vpg%d" % b_])
                    mk.gather("d_Vpg%d" % b_, Vpg[b_][:, 256:288], cache_kr, idx[:, g:g + 1], reads=["idx"], writes=["Vpg%d" % b_])
                    for rc in range(2):
                        mk.op("pe", lambda rc=rc: PE.transpose(pbT[:, rc * 128:(rc + 1) * 128], Vpg[b_][:, rc * 128:(rc + 1) * 128], identb[:]),
                              reads=["Vpg%d" % b_, "identb"], writes=["pbT"])
                    mk.op("pe", lambda: PE.transpose(pbT[0:32, 256:384], Vpg[b_][:, 256:288], identb[:]),
                          reads=["Vpg%d" % b_, "identb"], writes=["pbT"])
                    mk.op("act", lambda: A.copy(KTpg[b_][:], pbT[:, 0:256].rearrange("p (c k) -> p c k", c=2)),
                          reads=["pbT"], writes=["KTpg%d" % b_])
                    mk.op("dve", lambda: V.tensor_copy(KTrpg[b_][:], pbT[0:32, 256:384]), reads=["pbT"], writes=["KTpg%d" % b_])
                kts.append(dict(k0=KTpg[b_][:, 0, :], k1=KTpg[b_][:, 1, :], kr=KTrpg[b_][:], v=Vpg[b_][:, 0:256], nk=128,
                                keys=["KTpg%d" % b_, "Vpg%d" % b_], mask=None, prep=prep))
            kts.append(dict(k0=KTs[:, 0, r4], k1=KTs[:, 1, r4], kr=KTrs[:, r4], v=Vs[:, s_, :], nk=4,
                            keys=["KTs%d" % s_, "Vs%d" % s_], mask=M_le[0:4, 0:4]))
            b1_tile(4, x_t, xk, rope_ch[0:32, SEQ + 4 * s_:SEQ + 4 * s_ + 4], rope_ch[32:64, SEQ + 4 * s_:SEQ + 4 * s_ + 4],
                    yssd_s[r4, :], ["yssds%d" % s_], kts, x1_s[r4, :], "x1ss%d" % s_)
    esKV.close()
    mk.barrier()

    with ExitStack() as esC:
        def sbC(name, shape, dt=F32):
            return esC.enter_context(nc.sbuf_tensor(name, list(shape), dt))
        wup = sbC("wup", [128, 8, 4096], BF16)
        load_w_bf16(wup, "wup", w_up, 8, 0, 4096, "ld_wup")
        wdn = sbC("wdn", [128, 32, D], BF16)
        load_w_bf16(wdn, "wdn", w_down, 32, 0, D, "ld_wdn")
        gs2 = sbC("gs2", [128, D])
        sh2 = sbC("sh2", [128, D])
        g2 = sbC("g2", [128, D])
        gsf = sbC("gsf", [128, D])
        shf = sbC("shf", [128, D])
        xt = [sbC("xtc%d" % i, [128, D]) for i in range(2)]
        hb = sbC("hbc", [128, D], BF16)
        hT = sbC("hTc", [128, 8, 128], BF16)
        u2T = sbC("u2T", [128, 32, 128], BF16)
        ur = sbC("ur", [128, 512])
        x2 = sbC("x2", [128, D])

        def b2_tile(T, x_t, xk, y_dst):
            front(x_t, xk, gs2, sh2, "m2", T, hb, hT, "hTc")
            for fb in range(8):
                bank = pb[fb % 4]
                bkey = "pb%d" % (fb % 4)
                for f4 in range(4):
                    fc = fb * 4 + f4
                    for k in range(8):
                        mk.op("pe", lambda fc=fc, f4=f4, k=k, bank=bank: PE.matmul(
                            bank[:, f4 * 128:f4 * 128 + T], wup[:, k, fc * 128:(fc + 1) * 128], hT[:, k, :T],
                            start=(k == 0), stop=(k == 7)), reads=["wup", "hTc"], writes=[bkey])
                urv = ur[:, 0:4 * T].rearrange("p (c t) -> p c t", c=4)
                mk.op("act", lambda bank=bank, urv=urv: A.activation(out=urv, in_=v4(bank[:], T), func=AF.Relu), reads=[bkey], writes=["ur"])
                mk.op("dve", lambda fb=fb, urv=urv: V.tensor_tensor(u2T[:, fb * 4:(fb + 1) * 4, :T], urv, urv, ALU.mult),
                      reads=["ur"], writes=["u2T"])
            for hf in range(2):
                bank = pb[4 + hf]
                for fc in range(32):
                    mk.op("pe", lambda fc=fc, hf=hf, bank=bank: PE.matmul(bank[:T, :], u2T[:, fc, :T], wdn[:, fc, hf * 512:(hf + 1) * 512],
                                                                         start=(fc == 0), stop=(fc == 31)),
                          reads=["u2T", "wdn"], writes=["pb%d" % (4 + hf)])
                mk.op("dve", lambda hf=hf, bank=bank: V.tensor_tensor(x2[:T, hf * 512:(hf + 1) * 512], bank[:T, :], g2[:T, hf * 512:(hf + 1) * 512], ALU.mult),
                      reads=["pb%d" % (4 + hf), "g2"], writes=["x2"])
            mk.op("pool", lambda: G.tensor_tensor(x2[:T, :], x2[:T, :], x_t[:T, :], ALU.add), reads=["x2", xk], writes=["x2"])
            rstd_from(x2[:T, :], T, 1024, 1, ["x2"])
            mk.op("dve", lambda: V.scalar_tensor_tensor(x2[:T, :], x2[:T, :], ss[:T, 1:2], gsf[:T, :], ALU.mult, ALU.mult),
                  reads=["x2", "ss1", "mfgs"], writes=["x2"])
            mk.op("pool", lambda: G.tensor_tensor(x2[:T, :], x2[:T, :], shf[:T, :], ALU.add), reads=["x2", "mfsh"], writes=["x2"])
            mk.dma("sp", "st_y", y_dst, x2[:T, :], reads=["x2"])

        if ntl and stop >= 3:
            load_mod(gs2, sh2, ada_p, 128, 0, 4096, 3072, 1024, "m2")
            load_mod(gsf, shf, ada_p, 128, 0, 7168, 6144, 2048, "mf")
            mk.dma("sp", "ld_g2", g2[:], ada_p[:, 5120:6144], reads=["ada0"], writes=["g2"])
            for t in range(NB):
                x_t = xt[t % 2]
                xk = "xtc%d" % (t % 2)
                ts_ = slice(t * 128, (t + 1) * 128)
                mk.dma("sp", "ld_" + xk, x_t[:], x1_scr[ts_, :], reads=["x1s%d" % t], writes=[xk])
                b2_tile(128, x_t, xk, o_y[ts_, :])
        for s_ in range(ns if sstop >= 3 else 0):
            x_t = xt[s_ % 2]
            xk = "xtc%d" % (s_ % 2)
            r4 = slice(4 * s_, 4 * s_ + 4)
            mk.dma("sp", "ld_" + xk, x_t[:4, :], x1_s[r4, :], reads=["x1ss%d" % s_], writes=[xk])
            load_mod(gs2, sh2, ada_s[r4, :], 4, 1, 4096, 3072, 1024, "m2")
            load_mod(gsf, shf, ada_s[r4, :], 4, 1, 7168, 6144, 2048, "mf")
            mk.dma("sp", "ld_g2", g2[:4, :], ada_s[r4, 5120:6144], reads=["ada1"], writes=["g2"])
            b2_tile(4, x_t, xk, os_y[r4, :])

    mk.finish("sp")
    return nc


def _consts():
    ident = np.eye(128, dtype=np.float32)
    m = np.arange(128)
    M_le = (m[:, None] <= m[None, :]).astype(np.float32)
    M_gt = (m[:, None] > m[None, :]).astype(np.float32)
    ones = np.ones((128, 128), np.float32)
    c = np.concatenate([ident, M_le, M_gt, ones], axis=1)
    return np.ascontiguousarray(c)


def _rope_tables():
    inv = 1.0 / (10000.0 ** (np.arange(0, 32, 2, dtype=np.float32) / 32.0))
    posp = np.arange(SEQ, dtype=np.float32)
    poss = np.tile(PAST + np.arange(4, dtype=np.float32), 16)
    pos = np.concatenate([posp, poss])
    ang = pos[:, None] * inv[None, :]
    cos, sin = np.cos(ang).astype(np.float32), np.sin(ang).astype(np.float32)
    tok = np.concatenate([cos, cos, -sin, sin], axis=1)
    return np.ascontiguousarray(tok), np.ascontiguousarray(tok.T)


def own_tiles(j, ntiles=NT):
    nb = ntiles // 4
    return [4 * m + (j if m < nb // 2 else 3 - j) for m in range(nb)]


def _host_inputs(inp, core, n_phys_full=True, do_sample=True, ntiles=NT):
    f = np.float32
    b = core // 4
    d = {}
    d["xp"] = np.ascontiguousarray(inp["x_prompt"][b])
    d["cp"] = np.ascontiguousarray(np.broadcast_to(inp["c_prompt"][b][None, :], (128, D))).astype(f)
    s0 = core * NSEQ_CORE
    d["xs"] = np.ascontiguousarray(inp["x_sample"][s0:s0 + NSEQ_CORE].reshape(64, D))
    d["cs"] = np.ascontiguousarray(np.repeat(inp["c_sample"][s0:s0 + NSEQ_CORE], 4, axis=0))
    d["w_ada"] = np.ascontiguousarray(inp["w_ada"][0])
    d["w_adaf"] = np.ascontiguousarray(inp["w_ada_final"])
    d["b_ada"] = np.concatenate([inp["b_ada"][0], inp["b_ada_final"]])[None, :].astype(f)
    rows = np.zeros((1, 7 * 1024), f)
    rows[0, 0:1024] = inp["norm_mix_g"][0]
    rows[0, 1024:2048] = inp["norm_mlp_g"][0]
    rows[0, 2048:3072] = inp["norm_final_g"]
    rows[0, 3072:3584] = inp["norm_ssd_g"][0]
    rows[0, 3584:4096] = inp["norm_attn_g"][0]
    rows[0, 4096:4352] = inp["kv_norm_g"][0]
    d["rows"] = rows
    small = np.zeros((1, 32), f)
    small[0, 0:8] = inp["dt_bias"][0]
    small[0, 8:16] = inp["a_log"][0]
    small[0, 16:24] = inp["d_skip"][0]
    d["small"] = small
    w_in = inp["w_in"][0]
    kr = w_in[:, 2184:2216]
    kr_sw = np.concatenate([kr[:, 16:32], kr[:, 0:16]], axis=1)
    d["w_in"] = np.ascontiguousarray(np.concatenate([w_in, kr_sw], axis=1))
    cw = np.concatenate([inp["conv_w"][0], inp["conv_b"][0][None, :]], axis=0)
    d["convw"] = np.ascontiguousarray(cw.T.reshape(8, 128, 5).transpose(1, 0, 2))
    d["qg"] = np.ascontiguousarray(inp["q_norm_g"][0].reshape(3, 128).T)
    wq = inp["w_uq"][0].reshape(384, 8, 96)
    rope = wq[:, :, 64:96]
    rope_sw = np.concatenate([rope[:, :, 16:32], rope[:, :, 0:16]], axis=2)
    d["w_uq"] = np.ascontiguousarray(np.concatenate([wq, rope_sw], axis=2).reshape(384, 1024))
    d["w_ukT"] = np.ascontiguousarray(inp["w_uk"][0].transpose(2, 1, 0).reshape(64, 2048))
    d["w_uv"] = np.ascontiguousarray(inp["w_uv"][0].reshape(256, 512))
    d["w_out"] = np.ascontiguousarray(inp["w_out"][0])
    d["w_up"] = np.ascontiguousarray(inp["w_up"][0])
    d["w_down"] = np.ascontiguousarray(inp["w_down"][0])
    rt, rc = _rope_tables()
    d["rope_tok"] = rt
    d["rope_ch"] = rc
    if ntiles % 8 == 0 and ntiles > 0:
        j = core % 4
        tl = own_tiles(j, ntiles)
        rowsel = np.concatenate([np.arange(t * 128, (t + 1) * 128) for t in tl])
        d["x_own"] = np.ascontiguousarray(inp["x_prompt"][b][rowsel])
        d["rope_own"] = np.ascontiguousarray(rc[:, rowsel])
        d["idx_own"] = np.ascontiguousarray(rowsel.reshape(len(tl), 128).T.astype(np.int32))
        m_ = np.arange(128)
        tri = (m_[:, None] <= m_[None, :]).astype(np.float32)
        am = np.zeros((128, 8, 128), np.float32)
        for sel, o in enumerate((j, 3 - j)):
            for i in range(4):
                am[:, sel * 4 + i, :] = 1.0 if i < o else (tri if i == o else 0.0)
        d["amask"] = am
    d["consts"] = _consts()
    if do_sample:
        d["cache_kv"] = inp["cache_kv_latent"][0].reshape(-1, 256)
        d["cache_kr"] = inp["cache_k_rope"][0].reshape(-1, 32)
        d["ptab"] = np.ascontiguousarray(inp["page_table"][s0:s0 + NSEQ_CORE]).astype(np.int32)
        d["st_conv"] = np.ascontiguousarray(inp["state_conv"][0, s0:s0 + NSEQ_CORE].reshape(-1, D))
        d["st_ssm"] = np.ascontiguousarray(inp["state_ssm"][0, s0:s0 + NSEQ_CORE].reshape(-1, 128))
    return d


def kernel(**inputs):
    inp = {k: np.asarray(v) for k, v in inputs.items()}
    n_phys = int(inp["cache_kv_latent"].shape[1])
    nc = build(n_phys)
    in_maps = [_host_inputs(inp, c) for c in range(8)]
    res = run_bass_kernel_spmd(nc, in_maps, core_ids=list(range(8)))
    r = res.results
    f = np.float32
    pc = (0, 4)
    y_p = np.zeros((2, SEQ, D), f)
    for c in range(8):
        for m, t in enumerate(own_tiles(c % 4)):
            y_p[c // 4, t * 128:(t + 1) * 128] = r[c]["o_y"][m * 128:(m + 1) * 128]
    kv_p = np.stack([r[c]["o_kv"] for c in pc])[None].astype(f)
    kr_p = np.stack([r[c]["o_kr"] for c in pc])[None].astype(f)
    conv_p = np.stack([r[c]["o_conv"] for c in pc])[None].astype(f)
    ssm_p = np.stack([r[c]["o_ssm"].reshape(8, 64, 128) for c in pc])[None].astype(f)
    y_s = np.concatenate([r[c]["os_y"] for c in range(8)]).reshape(128, 4, D).astype(f)
    kv_s = np.concatenate([r[c]["os_kv"] for c in range(8)]).reshape(1, 128, 4, 256).astype(f)
    kr_s = np.concatenate([r[c]["os_kr"] for c in range(8)]).reshape(1, 128, 4, 32).astype(f)
    conv_s = np.concatenate([r[c]["os_conv"] for c in range(8)]).reshape(1, 128, 3, D).astype(f)
    ssm_s = np.concatenate([r[c]["os_ssm"] for c in range(8)]).reshape(1, 128, 8, 64, 128).astype(f)
    return (y_p, y_s, kv_p, kr_p, conv_p, ssm_p, kv_s, kr_s, conv_s, ssm_s)
```

```python
import math
from contextlib import ExitStack
import numpy as np
import concourse.bass as bass
import concourse.mybir as mybir
from concourse.bass_utils import run_bass_kernel_spmd

F32 = mybir.dt.float32
BF16 = mybir.dt.bfloat16
I32 = mybir.dt.int32
AF = mybir.ActivationFunctionType
ALU = mybir.AluOpType

D = 1024
SEQ = 8192
NT = SEQ // 128
EPS = 1e-6
ATTN_SCALE = 1.0 / math.sqrt(96.0)
NSEQ_CORE = 16
PAST = 8192
NPAGES = 64
WIN = 2248


class MK:
    def __init__(self, nc):
        self.nc = nc
        self.eng = {"pe": nc.tensor, "act": nc.scalar, "dve": nc.vector,
                    "pool": nc.gpsimd, "sp": nc.sync}
        self.sems = {}
        self.ecnt = {}
        for k in self.eng:
            self.sems["es_" + k] = nc.alloc_semaphore("es_" + k)
            self.ecnt[k] = 0
        self.waited = {k: {} for k in self.eng}
        self.dcnt = {}
        self.last_w = {}
        self.readers = {}
        self.counting = False
        self.limit = None
        self.n = 0

    def _skip(self):
        if self.counting:
            self.n += 1
            if self.limit is not None and self.n > self.limit:
                return True
        return False

    def _deps(self, reads, writes):
        toks = []
        for k in reads:
            t = self.last_w.get(k)
            if t is not None:
                toks.append(t)
        for k in writes:
            t = self.last_w.get(k)
            if t is not None:
                toks.append(t)
            toks.extend(self.readers.get(k, ()))
        return toks

    def _wait(self, e, toks):
        best = {}
        for (s, v) in toks:
            if v > best.get(s, 0):
                best[s] = v
        w = self.waited[e]
        for s, v in best.items():
            if w.get(s, 0) < v:
                self.eng[e].wait_ge(self.sems[s], v)
                w[s] = v

    def _record(self, tok, reads, writes):
        for k in reads:
            lst = self.readers.setdefault(k, [])
            lst.append(tok)
            if len(lst) > 24:
                best = {}
                for (s, v) in lst:
                    if v > best.get(s, 0):
                        best[s] = v
                self.readers[k] = list(best.items())
        for k in writes:
            self.last_w[k] = tok
            self.readers[k] = []

    def record(self):
        self.rec = []
        self.grp = None
        self.gid = 0

    def atomic(self, on):
        if on:
            self.gid = getattr(self, "gid", 0) + 1
            self.grp = self.gid
        else:
            self.grp = None

    def stop_record(self):
        r, self.rec = self.rec, None
        return r

    def play(self, progs):
        progs = [p_ for p_ in progs if p_]
        pos = [0] * len(progs)
        while True:
            best, bi = None, -1
            for i, p_ in enumerate(progs):
                if pos[i] < len(p_):
                    frac = pos[i] / len(p_)
                    if best is None or frac < best:
                        best, bi = frac, i
            if bi < 0:
                break
            g0 = progs[bi][pos[bi]][-1]
            while pos[bi] < len(progs[bi]):
                item = progs[bi][pos[bi]]
                if item[-1] != g0 or (g0 is None and item is not progs[bi][pos[bi]]):
                    break
                pos[bi] += 1
                if item[0] == "op":
                    self.op(*item[1:5])
                else:
                    self.dma(*item[1:7], **item[7])
                if g0 is None:
                    break

    def op(self, e, fn, reads=(), writes=()):
        if getattr(self, "rec", None) is not None:
            self.rec.append(("op", e, fn, tuple(reads), tuple(writes), getattr(self, "grp", None)))
            return None
        if self._skip():
            return None
        self._wait(e, self._deps(reads, writes))
        ins = fn()
        self.ecnt[e] += 1
        ins.then_inc(self.sems["es_" + e], 1)
        tok = ("es_" + e, self.ecnt[e])
        self._record(tok, reads, writes)
        return tok

    def _slot(self, slot):
        if slot not in self.sems:
            self.sems[slot] = self.nc.alloc_semaphore(slot)
            self.dcnt[slot] = 0

    def dma(self, q, slot, out, in_, reads=(), writes=(), **kw):
        if getattr(self, "rec", None) is not None:
            self.rec.append(("dma", q, slot, out, in_, tuple(reads), tuple(writes), kw, getattr(self, "grp", None)))
            return None
        if self._skip():
            return None
        if slot.startswith("ld_") and writes:
            slot = "d_" + writes[0]
        self._slot(slot)
        self._wait(q, self._deps(reads, writes))
        ins = self.eng[q].dma_start(out=out, in_=in_, **kw)
        self.dcnt[slot] += 16
        ins.then_inc(self.sems[slot], 16)
        tok = (slot, self.dcnt[slot])
        self._record(tok, reads, writes)
        return tok

    def gather(self, slot, out, in_, idx_ap, reads=(), writes=()):
        self._slot(slot)
        self._wait("pool", self._deps(reads, writes))
        ins = self.nc.gpsimd.indirect_dma_start(
            out=out, out_offset=None, in_=in_,
            in_offset=bass.IndirectOffsetOnAxis(ap=idx_ap, axis=0))
        self.dcnt[slot] += 16
        ins.then_inc(self.sems[slot], 16)
        tok = (slot, self.dcnt[slot])
        self._record(tok, reads, writes)
        return tok

    def barrier(self):
        toks = [(s, v) for s, v in self.dcnt.items() if v]
        for k, c in self.ecnt.items():
            if c:
                toks.append(("es_" + k, c))
        for e in self.eng:
            self._wait(e, [t for t in toks if t[0] != "es_" + e])

    def finish(self, e="sp"):
        toks = [(s, v) for s, v in self.dcnt.items() if v]
        for k, c in self.ecnt.items():
            if c and k != e:
                toks.append(("es_" + k, c))
        self._wait(e, toks)


def build(n_phys, do_prompt=True, do_sample=True, ntiles=NT, nseq=NSEQ_CORE, stop=99, sstop=99, oplimit=None):
    nc = bass.Bass("TRN2", target_bir_lowering=False)
    mk = MK(nc)
    V, A, G, PE = nc.vector, nc.scalar, nc.gpsimd, nc.tensor

    def din(name, shape, dt=F32):
        return nc.dram_tensor(name, list(shape), dt, kind="ExternalInput").ap()

    def dout(name, shape, dt=F32):
        return nc.dram_tensor(name, list(shape), dt, kind="ExternalOutput").ap()

    def dscr(name, shape, dt=F32):
        return nc.dram_tensor(name, list(shape), dt).ap()

    xp = din("xp", [SEQ, D])
    cp = din("cp", [128, D])
    xs = din("xs", [64, D])
    cs = din("cs", [64, D])
    w_ada = din("w_ada", [D, 6144])
    w_adaf = din("w_adaf", [D, 2048])
    b_ada = din("b_ada", [1, 8192])
    rows = din("rows", [1, 7 * 1024])
    small = din("small", [1, 32])
    w_in = din("w_in", [D, WIN])
    convw = din("convw", [128, 8, 5])
    qg = din("qg", [128, 3])
    w_uq = din("w_uq", [384, 1024])
    w_ukT = din("w_ukT", [64, 8 * 256])
    w_uv = din("w_uv", [256, 512])
    w_out = din("w_out", [D, D])
    w_up = din("w_up", [D, 4096])
    w_down = din("w_down", [4096, D])
    rope_tok = din("rope_tok", [SEQ + 64, 64])
    rope_ch = din("rope_ch", [64, SEQ + 64])
    consts = din("consts", [128, 4 * 128])
    balanced = do_prompt and ntiles > 0 and ntiles % 8 == 0
    NB = ntiles // 4 if balanced else ntiles
    if balanced:
        x_own = din("x_own", [NB * 128, D])
        rope_own = din("rope_own", [64, NB * 128])
        idx_own = din("idx_own", [128, NB], I32)
        amask = din("amask", [128, 8, 128])
    if do_sample:
        cache_all = din("cache_all", [n_phys * 64, 576])
        ptab = din("ptab", [nseq, 64], I32)
        st_conv = din("st_conv", [nseq * 3, D])
        st_ssm = din("st_ssm", [nseq * 512, 128])

    o_y = dout("o_y", [NB * 128 if balanced else SEQ, D])
    o_kv = dout("o_kv", [SEQ, 256])
    o_kr = dout("o_kr", [SEQ, 32])
    o_conv = dout("o_conv", [3, D])
    o_ssm = dout("o_ssm", [512, 128])
    if do_sample:
        os_y = dout("os_y", [64, D])
        os_kv = dout("os_kv", [64, 256])
        os_kr = dout("os_kr", [64, 32])
        os_conv = dout("os_conv", [nseq * 3, D])
        os_ssm = dout("os_ssm", [nseq * 512, 128])

    ada_p = dscr("ada_p", [128, 8192])
    ada_s = dscr("ada_s", [64, 8192])
    yssd_scr = dscr("yssd_scr", [SEQ, 512], BF16)
    x1_scr = dscr("x1_scr", [SEQ, D])

    es = ExitStack()

    def sb(name, shape, dt=F32):
        return es.enter_context(nc.sbuf_tensor(name, list(shape), dt))

    pb = [nc.alloc_psum_tensor("pb%d" % i, [128, 512], F32) for i in range(7)]
    pbT = nc.alloc_psum_tensor("pbT", [128, 1024], BF16)

    cst = sb("cst", [128, 512])
    mk.dma("sp", "ld_c", cst[:], consts, writes=["cst"])
    identb = sb("identb", [128, 128], BF16)
    mk.op("dve", lambda: V.tensor_copy(identb[:], cst[:, 0:128]), reads=["cst"], writes=["identb"])
    ident = cst[:, 0:128]
    M_le = cst[:, 128:256]
    M_gt = cst[:, 256:384]
    ones = cst[:, 384:512]
    cstb = sb("cstb", [128, 512], BF16)
    mk.op("dve", lambda: V.tensor_copy(cstb[:], cst[:]), reads=["cst"], writes=["cstb"])
    onesb = sb("onesb", [128, 128], BF16)
    mk.op("dve", lambda: V.tensor_copy(onesb[:], cst[:, 384:512]), reads=["cst"], writes=["onesb"])
    smallt = sb("smallt", [128, 32])
    mk.dma("sp", "ld_c", smallt[:], small.partition_broadcast(128), writes=["smallt"])
    a_t = sb("a_t", [128, 8])
    mk.op("act", lambda: A.activation(out=a_t[:], in_=smallt[:, 8:16], func=AF.Exp), reads=["smallt"], writes=["a_t"])
    mk.op("dve", lambda: V.tensor_scalar(a_t[:], a_t[:], -1.0, None, ALU.mult), reads=["a_t"], writes=["a_t"])
    dtb = smallt[:, 0:8]
    dsk = smallt[:, 16:24]
    convw_t = sb("convw_t", [128, 8, 5])
    mk.dma("sp", "ld_c", convw_t[:], convw, writes=["convw_t"])
    qg_t = sb("qg_t", [128, 3])
    mk.dma("sp", "ld_c", qg_t[:], qg, writes=["qg_t"])
    grow = sb("grow", [128, 2048])
    mk.dma("sp", "ld_c", grow[:, 0:1024], rows[:, 3072:4096].partition_broadcast(128), writes=["grow"])
    mk.dma("sp", "ld_c", grow[:, 1024:1280], rows[:, 4096:4352].partition_broadcast(128), writes=["grow"])
    g_ssd = grow[:, 0:512]
    g_attn = grow[:, 512:1024]
    g_kv = grow[:, 1024:1280]

    ss = sb("ss", [128, 4])
    junk = sb("junk", [128, 1024])

    def rstd_from(src_ap, T, n, col, keyr, scr=None, scrkey="junk"):
        scr = junk if scr is None else scr
        mk.op("act", lambda: A.activation(out=scr[:T, 0:n], in_=src_ap, func=AF.Square,
                                          accum_out=ss[:T, col:col + 1]),
              reads=keyr, writes=[scrkey, "ss%d" % col])
        mk.op("dve", lambda: V.tensor_scalar(ss[:T, col:col + 1], ss[:T, col:col + 1], 1.0 / n, EPS, ALU.mult, ALU.add),
              reads=["ss%d" % col], writes=["ss%d" % col])
        mk.op("act", lambda: A.activation(out=ss[:T, col:col + 1], in_=ss[:T, col:col + 1], func=AF.Sqrt),
              reads=["ss%d" % col], writes=["ss%d" % col])
        mk.op("dve", lambda: V.reciprocal(ss[:T, col:col + 1], ss[:T, col:col + 1]),
              reads=["ss%d" % col], writes=["ss%d" % col])

    def load_w_bf16(dst, dst_key, src, k_chunks, c0, c1, slot):
        step = 2048
        for a in range(c0, c1, step):
            b = min(c1, a + step)
            mk.dma("pool", slot, dst[:, :, a - c0:b - c0],
                   src[:, a:b].rearrange("(k p) n -> p k n", p=128), writes=[dst_key])

    with ExitStack() as es0:
        def sb0(name, shape, dt=F32):
            return es0.enter_context(nc.sbuf_tensor(name, list(shape), dt))
        ct = sb0("ct", [128, D])
        cb = sb0("cb", [128, D], BF16)
        scT = sb0("scT", [128, 8, 256], BF16)
        wch = [sb0("wch%d" % i, [128, 8, 512], BF16) for i in range(2)]
        bch = [sb0("bch%d" % i, [128, 512]) for i in range(2)]
        ost = [sb0("ost%d" % i, [128, 512]) for i in range(2)]
        for which, (src, T, c0) in enumerate(((cp, 128, 0), (cs, 64, 128))):
            mk.dma("sp", "ld_ct", ct[:T, :], src, writes=["ct"])
            mk.op("act", lambda T=T: A.activation(out=cb[:T, :], in_=ct[:T, :], func=AF.Silu), reads=["ct"], writes=["cb"])
            for k in range(8):
                mk.op("pe", lambda k=k, T=T: PE.transpose(pbT[:, k * 128:k * 128 + T], cb[:T, k * 128:(k + 1) * 128], identb[:T, :T]),
                      reads=["cb", "identb"], writes=["pbT"])
            mk.op("dve", lambda T=T, c0=c0: V.tensor_copy(
                scT[:, :, c0:c0 + T], pbT[:].rearrange("p (k t) -> p k t", k=8)[:, :, 0:T]),
                reads=["pbT"], writes=["scT"])
        for j in range(16):
            s = j % 2
            wsrc, wc0 = (w_ada, j * 512) if j < 12 else (w_adaf, (j - 12) * 512)
            load_w_bf16(wch[s], "wch%d" % s, wsrc, 8, wc0, wc0 + 512, "ld_wch%d" % s)
            mk.dma("sp", "ld_bch%d" % s, bch[s][:], b_ada[:, j * 512:(j + 1) * 512].partition_broadcast(128),
                   writes=["bch%d" % s])
            for which, (T, c0, dst) in enumerate(((128, 0, ada_p), (64, 128, ada_s))):
                bank = pb[which]
                for k in range(8):
                    mk.op("pe", lambda k=k, T=T, c0=c0, bank=bank, s=s: PE.matmul(
                        bank[:T, :], scT[:, k, c0:c0 + T], wch[s][:, k, :], start=(k == 0), stop=(k == 7)),
                        reads=["scT", "wch%d" % s], writes=["pb%d" % which])
                o = ost[which]
                mk.op("dve", lambda T=T, bank=bank, o=o, s=s: V.tensor_tensor(o[:T, :], bank[:T, :], bch[s][:T, :], ALU.add),
                      reads=["pb%d" % which, "bch%d" % s], writes=["ost%d" % which])
                mk.dma("sp", "st_ada%d" % which, dst[:, j * 512:(j + 1) * 512], o[:T, :],
                       reads=["ost%d" % which], writes=["ada%d" % which])

    mk.barrier()

    def load_mod(gs, sh, ada_src, T, which, sc_col, sh_col, g_col, key):
        mk.dma("sp", "ld_" + key, gs[:T, :], ada_src[:, sc_col:sc_col + 1024], reads=["ada%d" % which], writes=[key + "gs"])
        mk.dma("sp", "ld_" + key, sh[:T, :], ada_src[:, sh_col:sh_col + 1024], reads=["ada%d" % which], writes=[key + "sh"])
        mk.dma("sp", "ld_" + key, junk[:T, :], rows[:, g_col:g_col + 1024].partition_broadcast(T), writes=["junk"])
        mk.op("dve", lambda: V.scalar_tensor_tensor(gs[:T, :], gs[:T, :], 1.0, junk[:T, :], ALU.add, ALU.mult),
              reads=[key + "gs", "junk"], writes=[key + "gs"])

    def front(xt, xkey, gs, sh, modkey, T, hb, hT, hkey):
        rstd_from(xt[:T, :], T, 1024, 0, [xkey])
        mk.op("dve", lambda: V.scalar_tensor_tensor(junk[:T, :], xt[:T, :], ss[:T, 0:1], gs[:T, :], ALU.mult, ALU.mult),
              reads=[xkey, "ss0", modkey + "gs"], writes=["junk"])
        mk.op("pool", lambda: G.tensor_tensor(hb[:T, :], junk[:T, :], sh[:T, :], ALU.add),
              reads=["junk", modkey + "sh"], writes=["hb"])
        transpose8(hb, "hb", T, hT, hkey)

    def transpose8(src, skey, T, dstT, dkey, nchunk=8):
        mk.atomic(True)
        for k in range(nchunk):
            mk.op("pe", lambda k=k: PE.transpose(pbT[:, k * 128:k * 128 + T], src[:T, k * 128:(k + 1) * 128], identb[:T, :T]),
                  reads=[skey, "identb"], writes=["pbT"])
        mk.op("act", lambda: A.copy(dstT[:, 0:nchunk, :T], pbT[:].rearrange("p (k t) -> p k t", k=8)[:, 0:nchunk, 0:T]),
              reads=["pbT"], writes=[dkey])
        mk.atomic(False)

    def lin_tok(outp, okey, hT, hkey, T, w, wkey, c0, c1, nk=8):
        for k in range(nk):
            mk.op("pe", lambda k=k: PE.matmul(outp, hT[:, k, :T], w[:, k, c0:c1], start=(k == 0), stop=(k == nk - 1)),
                  reads=[hkey, wkey], writes=[okey])

    def ssd_chunk(T, xa_T, xakey, dt_sb, dtkey, zs, zkey, hS, hSb, hkey, ysb, tmp_tiles, scr2):
        (xtok, dtt, e3, m1, Eexp, SC, t1, t2, xw, dttb, m1b) = tmp_tiles
        mk.atomic(True)
        for k in range(6):
            mk.op("pe", lambda k=k: PE.transpose(pbT[:T, k * 128:(k + 1) * 128], xa_T[:, k, :T], identb[:, :]),
                  reads=[xakey, "identb"], writes=["pbT"])
        mk.op("act", lambda: A.copy(xtok[:T, :], pbT[:T, 0:768]), reads=["pbT"], writes=["xtok"])
        mk.atomic(False)
        mk.op("dve", lambda: V.tensor_tensor(dtt[:T, 0:8], dt_sb, dtb[:T, :], ALU.add), reads=[dtkey, "smallt"], writes=["dtt"])
        mk.op("act", lambda: A.activation(out=dtt[:T, 0:8], in_=dtt[:T, 0:8], func=AF.Exp), reads=["dtt"], writes=["dtt"])
        mk.op("act", lambda: A.activation(out=dtt[:T, 0:8], in_=dtt[:T, 0:8], func=AF.Ln, bias=1.0), reads=["dtt"], writes=["dtt"])
        mk.op("dve", lambda: V.tensor_tensor(dtt[:T, 8:16], dtt[:T, 0:8], a_t[:T, :], ALU.mult), reads=["dtt", "a_t"], writes=["dtt"])
        dt_ = dtt[:T, 0:8]
        dtA = dtt[:T, 8:16]
        if T < 128:
            Mle, Mgt, On = cstb[:, 128:256], cstb[:, 256:384], cstb[:, 384:512]
            mk.op("dve", lambda: V.tensor_copy(dttb[:T, :], dtt[:T, 8:16]), reads=["dtt"], writes=["dttb"])
            dtA = dttb[:T, :]
            m1 = m1b
        else:
            Mle, Mgt, On = M_le, M_gt, ones
        mk.op("pe", lambda: PE.matmul(pb[4][:T, 0:8], Mle[:T, :T], dtA, start=True, stop=True), reads=["dtt", "dttb", "cst", "cstb"], writes=["pb4"])
        mk.op("pe", lambda: PE.matmul(pb[4][:T, 8:16], Mgt[:T, :T], dtA, start=True, stop=True), reads=["dtt", "dttb", "cst", "cstb"], writes=["pb4"])
        mk.op("pe", lambda: PE.matmul(pb[4][:, 16:24], On[:T, :], dtA, start=True, stop=True), reads=["dtt", "dttb", "cst", "cstb"], writes=["pb4"])
        mk.op("act", lambda: A.activation(out=e3[:T, 0:16], in_=pb[4][:T, 0:16], func=AF.Exp), reads=["pb4"], writes=["e3"])
        mk.op("act", lambda: A.activation(out=e3[:, 16:24], in_=pb[4][:, 16:24], func=AF.Exp), reads=["pb4"], writes=["e3"])
        mk.op("dve", lambda: V.tensor_tensor(e3[:T, 8:16], e3[:T, 8:16], dt_, ALU.mult), reads=["e3", "dtt"], writes=["e3"])
        for g in range(2):
            mk.op("pe", lambda g=g: PE.matmul(pb[4][:T, 128 + g * 128:128 + g * 128 + T], xa_T[:, 4 + g, :T], xa_T[:, 6 + g, :T],
                                              start=True, stop=True), reads=[xakey], writes=["pb4"])
        for h in range(8):
            mk.op("dve", lambda h=h: V.tensor_scalar(m1[:T, h, :T], M_gt[:T, :T], dtt[:T, 8 + h:9 + h], None, ALU.mult),
                  reads=["cst", "dtt"], writes=["m1_%d" % h])
            bank = pb[5 + h // 4]
            mk.op("pe", lambda h=h, bank=bank: PE.matmul(bank[:T, (h % 4) * 128:(h % 4) * 128 + T], m1[:T, h, :T], Mle[:T, :T],
                                                        start=True, stop=True),
                  reads=["m1_%d" % h, "cst", "cstb"], writes=["pb%d" % (5 + h // 4)])
        for hh in range(2):
            mk.op("act", lambda hh=hh: A.activation(
                out=Eexp[:T, hh * 4:(hh + 1) * 4, :T],
                in_=pb[5 + hh][:T, :].rearrange("p (h i) -> p h i", h=4)[:, :, 0:T], func=AF.Exp),
                reads=["pb%d" % (5 + hh)], writes=["Eexp%d" % hh])
            mk.op("dve", lambda hh=hh: V.tensor_tensor(
                Eexp[:T, hh * 4:(hh + 1) * 4, :T], Eexp[:T, hh * 4:(hh + 1) * 4, :T],
                M_le[:T, :T].unsqueeze(1).to_broadcast([T, 4, T]), ALU.mult),
                reads=["Eexp%d" % hh, "cst"], writes=["Eexp%d" % hh])
            mk.op("pool", lambda hh=hh: G.tensor_tensor(
                Eexp[:T, hh * 4:(hh + 1) * 4, :T], Eexp[:T, hh * 4:(hh + 1) * 4, :T],
                dtt[:T, hh * 4:(hh + 1) * 4].unsqueeze(2).to_broadcast([T, 4, T]), ALU.mult),
                reads=["Eexp%d" % hh, "dtt"], writes=["Eexp%d" % hh])
            mk.op("dve", lambda hh=hh: V.tensor_tensor(
                SC[:T, hh * 4:(hh + 1) * 4, :T], Eexp[:T, hh * 4:(hh + 1) * 4, :T],
                pb[4][:T, 128 + hh * 128:128 + hh * 128 + T].unsqueeze(1).to_broadcast([T, 4, T]), ALU.mult),
                reads=["Eexp%d" % hh, "pb4"], writes=["SC%d" % hh])
        for h in range(8):
            mk.op("pe", lambda h=h: PE.matmul(pb[5][:T, h * 64:(h + 1) * 64], SC[:T, h, :T], xtok[:T, h * 64:(h + 1) * 64],
                                              start=True, stop=True),
                  reads=["SC%d" % (h // 4), "xtok"], writes=["pb5"])
        for g in range(2):
            mk.op("pe", lambda g=g: PE.matmul(pb[6][:T, g * 256:(g + 1) * 256], xa_T[:, 6 + g, :T], hSb[:, g * 256:(g + 1) * 256],
                                              start=True, stop=True),
                  reads=[xakey, hkey + "b"], writes=["pb6"])

        def v3(ap):
            return ap.rearrange("p (h d) -> p h d", h=8)

        def bc(ap8):
            return ap8.unsqueeze(2).to_broadcast([T, 8, 64])
        mk.op("dve", lambda: V.tensor_tensor(v3(t1[:T, :]), v3(pb[6][:T, :]), bc(e3[:T, 0:8]), ALU.mult),
              reads=["pb6", "e3"], writes=["t1"])
        mk.op("dve", lambda: V.tensor_tensor(t1[:T, :], t1[:T, :], pb[5][:T, :], ALU.add), reads=["t1", "pb5"], writes=["t1"])
        mk.op("pool", lambda: G.tensor_tensor(v3(t2[:T, :]), v3(xtok[:T, 0:512]), bc(dsk[:T, :]), ALU.mult),
              reads=["xtok", "smallt"], writes=["t2"])
        mk.op("pool", lambda: G.tensor_tensor(t1[:T, :], t1[:T, :], t2[:T, :], ALU.add), reads=["t1", "t2"], writes=["t1"])
        mk.op("dve", lambda: V.tensor_tensor(t1[:T, :], t1[:T, :], zs, ALU.mult), reads=["t1", zkey], writes=["t1"])
        rstd_from(t1[:T, :], T, 512, 1, ["t1"], scr=scr2, scrkey="scr2")
        mk.op("dve", lambda: V.scalar_tensor_tensor(ysb[:T, :], t1[:T, :], ss[:T, 1:2], g_ssd[:T, :], ALU.mult, ALU.mult),
              reads=["t1", "ss1", "grow"], writes=["ysb"])
        mk.op("pool", lambda: G.tensor_tensor(v3(xw[:T, :]), v3(xtok[:T, 0:512]), bc(e3[:T, 8:16]), ALU.mult),
              reads=["xtok", "e3"], writes=["xw"])
        for g in range(2):
            mk.op("pe", lambda g=g: PE.matmul(pb[4][:, g * 256:(g + 1) * 256], xtok[:T, 512 + g * 128:512 + (g + 1) * 128],
                                              xw[:T, g * 256:(g + 1) * 256], start=True, stop=True),
                  reads=["xtok", "xw"], writes=["pb4"])
        mk.op("dve", lambda: V.tensor_tensor(hS[:].rearrange("p (h d) -> p h d", h=8), hS[:].rearrange("p (h d) -> p h d", h=8),
                                             e3[:, 16:24].unsqueeze(2).to_broadcast([128, 8, 64]), ALU.mult),
              reads=[hkey, "e3"], writes=[hkey])
        mk.op("dve", lambda: V.tensor_tensor(hS[:], hS[:], pb[4][:], ALU.add), reads=[hkey, "pb4"], writes=[hkey])
        mk.op("act", lambda: A.copy(hSb[:], hS[:]), reads=[hkey], writes=[hkey + "b"])

    ns = nseq if do_sample else 0
    ntl = ntiles if do_prompt else 0
    esKV = ExitStack()

    def sbK(name, shape, dt=F32):
        return esKV.enter_context(nc.sbuf_tensor(name, list(shape), dt))
    KT = sbK("KT", [128, 2, SEQ], BF16)
    KTr = sbK("KTr", [32, SEQ], BF16)
    Vt = sbK("Vt", [128, NT, 256], BF16)
    KTs = sbK("KTs", [128, 2, 64], BF16)
    KTrs = sbK("KTrs", [32, 64], BF16)
    Vs = sbK("Vs", [4, 16, 256], BF16)
    yssd_s = dscr("yssd_s", [64, 512], BF16)
    x1_s = dscr("x1_s", [64, D])

    def v4(ap, T, h=4):
        return ap.rearrange("p (h t) -> p h t", h=h)[:, :, 0:T]

    with ExitStack() as esA:
        def sbA(name, shape, dt=F32):
            return esA.enter_context(nc.sbuf_tensor(name, list(shape), dt))
        win = sbA("win", [128, 8, WIN], BF16)
        load_w_bf16(win, "win", w_in, 8, 0, WIN, "ld_win")
        gs1 = sbA("gs1", [128, D])
        sh1 = sbA("sh1", [128, D])
        xt = [sbA("xt%d" % i, [128, D]) for i in range(2)]
        hb = sbA("hb", [128, D], BF16)
        hT = sbA("hT", [128, 8, 128], BF16)
        raw = sbA("raw", [128, 8, 131])
        acc = sbA("acc", [128, 8, 128])
        acc2 = sbA("acc2", [128, 8, 128])
        xa_T = sbA("xa_T", [128, 8, 128], BF16)
        kvf = sbA("kvf", [128, 256])
        krf = sbA("krf", [128, 64])
        krb = sbA("krb", [128, 32], BF16)
        ropet = sbA("ropet", [128, 64])
        hS = sbA("hS", [128, 512])
        hSb = sbA("hSb", [128, 512], BF16)
        ysb = sbA("ysb", [128, 512], BF16)
        stin = sbA("stin", [128, 4, 128])
        tmps = (sbA("xtok", [128, 768], BF16), sbA("dtt", [128, 16]), sbA("e3", [128, 24]),
                sbA("m1", [128, 8, 128]), sbA("Eexp", [128, 8, 128]), sbA("SC", [128, 8, 128], BF16),
                sbA("t1", [128, 512]), sbA("t2", [128, 512]), sbA("xw", [128, 512], BF16),
                sbA("dttb", [128, 8], BF16), sbA("m1b", [128, 8, 128], BF16))

        xa2 = [xa_T, sbA("xa_T1", [128, 8, 128], BF16)]
        zs2 = [sbA("zs%d" % i, [128, 512]) for i in range(2)]
        dtr2 = [sbA("dtr%d" % i, [128, 8]) for i in range(2)]
        scr2 = sbA("scr2", [128, 512])
        sto = tmps[7]

        def a_s1(T, i2, x_t, xk, rope_src, kv_dst, kr_dst, V_ap, Vkey, K0_ap, K1_ap, Kr_ap, Kkey, conv_dst):
            xa = xa2[i2]
            mk.dma("sp", "ld_rope", ropet[:T, :], rope_src, writes=["ropet"])
            front(x_t, xk, gs1, sh1, "m1", T, hb, hT, "hT")
            lin_tok(pb[0][:T, 0:8], "pb0", hT, "hT", T, win, "win", 1536, 1544)
            lin_tok(pb[0][:T, 64:384], "pb0", hT, "hT", T, win, "win", 1928, 2248)
            lin_tok(pb[3][:T, :], "pb3", hT, "hT", T, win, "win", 0, 512)
            for c in range(8):
                bank = pb[1 + c // 4]
                for k in range(8):
                    mk.op("pe", lambda c=c, k=k, bank=bank: PE.matmul(
                        bank[:, (c % 4) * 128:(c % 4) * 128 + T], win[:, k, 512 + c * 128:512 + (c + 1) * 128], hT[:, k, :T],
                        start=(k == 0), stop=(k == 7)), reads=["win", "hT"], writes=["pb%d" % (1 + c // 4)])
            mk.op("dve", lambda: V.tensor_copy(dtr2[i2][:T, :], pb[0][:T, 0:8]), reads=["pb0"], writes=["dtr%d" % i2])
            mk.op("act", lambda: A.activation(out=zs2[i2][:T, :], in_=pb[3][:T, :], func=AF.Silu), reads=["pb3"], writes=["zs%d" % i2])
            rstd_from(pb[0][:T, 64:320], T, 256, 2, ["pb0"])
            mk.op("dve", lambda: V.scalar_tensor_tensor(kvf[:T, :], pb[0][:T, 64:320], ss[:T, 2:3], g_kv[:T, :], ALU.mult, ALU.mult),
                  reads=["pb0", "ss2", "grow"], writes=["kvf"])
            mk.op("pool", lambda: G.tensor_copy(V_ap, kvf[:T, :]), reads=["kvf"], writes=[Vkey])
            mk.dma("sp", "st_kv", kv_dst, kvf[:T, :], reads=["kvf"])
            mk.op("dve", lambda: V.tensor_tensor(krf[:T, :], pb[0][:T, 320:384], ropet[:T, :], ALU.mult),
                  reads=["pb0", "ropet"], writes=["krf"])
            mk.op("dve", lambda: V.tensor_tensor(krf[:T, 0:32], krf[:T, 0:32], krf[:T, 32:64], ALU.add), reads=["krf"], writes=["krf"])
            mk.op("pool", lambda: G.tensor_copy(krb[:T, :], krf[:T, 0:32]), reads=["krf"], writes=["krb"])
            mk.dma("sp", "st_kr", kr_dst, krf[:T, 0:32], reads=["krf"])
            mk.atomic(True)
            for rc in range(2):
                mk.op("pe", lambda rc=rc: PE.transpose(pbT[:, rc * 128:rc * 128 + T], V_ap[:, rc * 128:(rc + 1) * 128], identb[:T, :T]),
                      reads=[Vkey, "identb"], writes=["pbT"])
            mk.op("pe", lambda: PE.transpose(pbT[0:32, 256:256 + T], krb[:T, :], identb[:T, :T]), reads=["krb", "identb"], writes=["pbT"])
            mk.op("act", lambda: A.copy(K0_ap, pbT[:, 0:T]), reads=["pbT"], writes=[Kkey])
            mk.op("act", lambda: A.copy(K1_ap, pbT[:, 128:128 + T]), reads=["pbT"], writes=[Kkey])
            mk.op("act", lambda: A.copy(Kr_ap, pbT[0:32, 256:256 + T]), reads=["pbT"], writes=[Kkey])
            mk.atomic(False)
            for hh in range(2):
                mk.op("act", lambda hh=hh: A.copy(raw[:, hh * 4:(hh + 1) * 4, 3:3 + T], v4(pb[1 + hh][:], T)),
                      reads=["pb%d" % (1 + hh)], writes=["raw"])
            if conv_dst is not None:
                for c in range(8):
                    mk.dma("sp", "st_conv", conv_dst[:, c * 128:(c + 1) * 128].rearrange("r p -> p r"), raw[:, c, T:T + 3],
                           reads=["raw"], allow_slow_non_contiguous=True)

            def wb(k):
                return convw_t[:, :, k:k + 1].to_broadcast([128, 8, T])
            A_ = acc[:, :, 0:T]
            B_ = acc2[:, :, 0:T]
            mk.op("dve", lambda: V.tensor_tensor(A_, raw[:, :, 0:T], wb(0), ALU.mult), reads=["raw", "convw_t"], writes=["acc"])
            mk.op("pool", lambda: G.tensor_tensor(B_, raw[:, :, 1:1 + T], wb(1), ALU.mult), reads=["raw", "convw_t"], writes=["acc2"])
            mk.op("dve", lambda: V.tensor_tensor(A_, A_, B_, ALU.add), reads=["acc", "acc2"], writes=["acc"])
            mk.op("pool", lambda: G.tensor_tensor(B_, raw[:, :, 2:2 + T], wb(2), ALU.mult), reads=["raw", "convw_t", "acc"], writes=["acc2"])
            mk.op("dve", lambda: V.tensor_tensor(A_, A_, B_, ALU.add), reads=["acc", "acc2"], writes=["acc"])
            mk.op("pool", lambda: G.tensor_tensor(B_, raw[:, :, 3:3 + T], wb(3), ALU.mult), reads=["raw", "convw_t", "acc"], writes=["acc2"])
            mk.op("dve", lambda: V.tensor_tensor(A_, A_, B_, ALU.add), reads=["acc", "acc2"], writes=["acc"])
            mk.op("dve", lambda: V.tensor_tensor(A_, A_, wb(4), ALU.add), reads=["acc", "convw_t"], writes=["acc"])
            mk.op("act", lambda: A.activation(out=xa[:, :, 0:T], in_=A_, func=AF.Silu), reads=["acc"], writes=["xa%d" % i2])
            mk.op("pool", lambda: G.tensor_copy(raw[:, :, 0:3], raw[:, :, T:T + 3]), reads=["raw"], writes=["raw"])

        def a_s2(T, i2, yssd_dst, ykey):
            ssd_chunk(T, xa2[i2], "xa%d" % i2, dtr2[i2][:T, :], "dtr%d" % i2, zs2[i2][:T, :], "zs%d" % i2, hS, hSb, "hS", ysb, tmps, scr2)
            mk.dma("sp", "st_yssd", yssd_dst, ysb[:T, :], reads=["ysb"], writes=[ykey])

        def state_out(dst):
            for c4 in range(4):
                mk.op("pe", lambda c4=c4: PE.transpose(pb[5][:, c4 * 128:(c4 + 1) * 128], hS[:, c4 * 128:(c4 + 1) * 128], ident),
                      reads=["hS", "cst"], writes=["pb5"])
            mk.op("act", lambda: A.copy(sto[:], pb[5][:]), reads=["pb5"], writes=["t2"])
            mk.dma("sp", "st_ssm", dst.rearrange("(c p) n -> p c n", p=128), sto[:].rearrange("p (c n) -> p c n", c=4),
                   reads=["t2"])

        if ntl:
            load_mod(gs1, sh1, ada_p, 128, 0, 1024, 0, 0, "m1")
            mk.op("dve", lambda: V.memset(hS[:], 0.0), writes=["hS"])
            mk.op("dve", lambda: V.memset(hSb[:], 0.0), writes=["hSb"])
            mk.op("pool", lambda: G.memset(raw[:], 0.0), writes=["raw"])

        def p_s1(t):
            x_t = xt[t % 2]
            xk = "xt%d" % (t % 2)
            ts_ = slice(t * 128, (t + 1) * 128)
            mk.dma("sp", "ld_" + xk, x_t[:], xp[ts_, :], writes=[xk])
            a_s1(128, t % 2, x_t, xk, rope_tok[ts_, :], o_kv[ts_, :], o_kr[ts_, :], Vt[:, t, :], "Vt%d" % t,
                 KT[:, 0, ts_], KT[:, 1, ts_], KTr[:, ts_], "KT%d" % t, o_conv if t == ntl - 1 else None)
        if ntl:
            p_s1(0)
        for t in range(ntl):
            progs = []
            if t + 1 < ntl:
                mk.record()
                p_s1(t + 1)
                progs.append(mk.stop_record())
            mk.record()
            a_s2(128, t % 2, yssd_scr[t * 128:(t + 1) * 128, :], "yssd%d" % t)
            progs.append(mk.stop_record())
            mk.play(progs)
        if ntl:
            state_out(o_ssm)
        mk.counting = True
        mk.limit = oplimit

        def s_s1(s_):
            x_t = xt[s_ % 2]
            xk = "xt%d" % (s_ % 2)
            r4 = slice(4 * s_, 4 * s_ + 4)
            mk.dma("sp", "ld_" + xk, x_t[:4, :], xs[r4, :], writes=[xk])
            load_mod(gs1, sh1, ada_s[r4, :], 4, 1, 1024, 0, 0, "m1")
            for c in range(8):
                mk.dma("sp", "ld_raw", raw[:, c, 0:3], st_conv[3 * s_:3 * s_ + 3, c * 128:(c + 1) * 128].rearrange("r p -> p r"),
                       writes=["raw"], allow_slow_non_contiguous=True)
            a_s1(4, s_ % 2, x_t, xk, rope_tok[SEQ + 4 * s_:SEQ + 4 * s_ + 4, :], os_kv[r4, :], os_kr[r4, :], Vs[:, s_, :], "Vs%d" % s_,
                 KTs[:, 0, r4], KTs[:, 1, r4], KTrs[:, r4], "KTs%d" % s_, os_conv[3 * s_:3 * s_ + 3, :])

        def s_s2(s_):
            r4 = slice(4 * s_, 4 * s_ + 4)
            mk.dma("sp", "ld_stin", stin[:], st_ssm[512 * s_:512 * (s_ + 1), :].rearrange("(c p) n -> p c n", p=128), writes=["stin"])
            for c4 in range(4):
                mk.op("pe", lambda c4=c4: PE.transpose(pb[5][:, c4 * 128:(c4 + 1) * 128], stin[:, c4, :], ident),
                      reads=["stin", "cst"], writes=["pb5"])
            mk.op("dve", lambda: V.tensor_copy(hS[:], pb[5][:]), reads=["pb5"], writes=["hS"])
            mk.op("act", lambda: A.copy(hSb[:], hS[:]), reads=["hS"], writes=["hSb"])
            a_s2(4, s_ % 2, yssd_s[r4, :], "yssds%d" % s_)
            state_out(os_ssm[512 * s_:512 * (s_ + 1), :])
        if ns:
            s_s1(0)
        for s_ in range(ns):
            progs = []
            if s_ + 1 < ns:
                mk.record()
                s_s1(s_ + 1)
                progs.append(mk.stop_record())
            mk.record()
            s_s2(s_)
            progs.append(mk.stop_record())
            mk.play(progs)
        mk.counting = False
        mk.limit = None

    mk.barrier()
    with ExitStack() as esB:
        def sbB(name, shape, dt=F32):
            return esB.enter_context(nc.sbuf_tensor(name, list(shape), dt))
        wq = sbB("wq", [128, 8, 384], BF16)
        load_w_bf16(wq, "wq", w_in, 8, 1544, 1928, "ld_wq")
        wuq = sbB("wuq", [128, 3, 1024], BF16)
        load_w_bf16(wuq, "wuq", w_uq, 3, 0, 1024, "ld_wuq")
        wuk = sbB("wuk", [64, 2048], BF16)
        mk.dma("pool", "ld_wuk", wuk[:], w_ukT, writes=["wuk"])
        wuv = sbB("wuv", [128, 2, 512], BF16)
        load_w_bf16(wuv, "wuv", w_uv, 2, 0, 512, "ld_wuv")
        wo = sbB("wo", [128, 8, D], BF16)
        load_w_bf16(wo, "wo", w_out, 8, 0, D, "ld_wo")
        gs1 = sbB("gs1b", [128, D])
        sh1 = sbB("sh1b", [128, D])
        g1 = sbB("g1", [128, D])
        xt = [sbB("xtb%d" % i, [128, D]) for i in range(2)]
        hb = sbB("hbb", [128, D], BF16)
        hT = sbB("hTb", [128, 8, 128], BF16)
        qlT = sbB("qlT", [128, 3, 128], BF16)
        sq = sbB("sq", [128, 3, 128], BF16)
        rbc = sbB("rbc", [128, 128])
        qT = sbB("qT", [64, 8, 128], BF16)
        qa = sbB("qa", [128, 2, 1024], BF16)
        qr = sbB("qr", [32, 1024], BF16)
        qrt = sbB("qrt", [32, 1024])
        qrs = sbB("qrs", [32, 1024])
        cosT = sbB("cosT", [32, 128])
        sinT = sbB("sinT", [32, 128])
        lbc = sbB("lbc", [128, 512])
        PT = [sbB("PT%d" % i, [128, 512], BF16) for i in range(2)]
        oT = sbB("oT", [128, 2, 1024], BF16)
        of = sbB("of", [128, 512])
        mix = sbB("mix", [128, D], BF16)
        mixT = sbB("mixT", [128, 8, 128], BF16)
        x1 = sbB("x1", [128, D])
        if ns:
            Vpg = [sbB("Vpg%d" % i, [128, 2, 288], BF16) for i in range(2)]
            KTpg = [sbB("KTpg%d" % i, [128, 2, 128], BF16) for i in range(4)]
            KTrpg = [sbB("KTrpg%d" % i, [32, 128], BF16) for i in range(4)]
            ptb = sbB("ptb", [128, 64], I32)
            idx = sbB("idx", [128, 64], I32)
            iot = sbB("iot", [128, 64], I32)
            mk.op("pool", lambda: G.iota(iot[:], [[0, 64]], base=0, channel_multiplier=1), writes=["iot"])
            mk.op("dve", lambda: V.tensor_scalar(iot[:], iot[:], 63, None, ALU.bitwise_and), reads=["iot"], writes=["iot"])

        def b1_tile(T, x_t, xk, cos_src, sin_src, yssd_src, ykeys, ktiles, x1_dst, x1key):
            HPH = 4 if T == 128 else 8
            NHF = 8 // HPH
            CW = HPH * T

            def hv(ap, h=HPH):
                return ap.rearrange("p (h t) -> p h t", h=h)
            mk.dma("sp", "ld_cosT", cosT[:, :T], cos_src, writes=["cosT"])
            mk.dma("sp", "ld_sinT", sinT[:, :T], sin_src, writes=["sinT"])
            if callable(yssd_src):
                yssd_src()
            else:
                mk.dma("sp", "ld_mix", mix[:T, 0:512], yssd_src, reads=ykeys, writes=["mix"])
            front(x_t, xk, gs1, sh1, "m1b", T, hb, hT, "hTb")
            for c in range(3):
                for k in range(8):
                    mk.op("pe", lambda c=c, k=k: PE.matmul(pb[0][:, c * 128:c * 128 + T], wq[:, k, c * 128:(c + 1) * 128], hT[:, k, :T],
                                                           start=(k == 0), stop=(k == 7)), reads=["wq", "hTb"], writes=["pb0"])
            for c in range(3):
                mk.op("act", lambda c=c: A.activation(out=qlT[:, c, :T], in_=pb[0][:, c * 128:c * 128 + T], func=AF.Copy,
                                                      scale=qg_t[:, c:c + 1]), reads=["pb0", "qg_t"], writes=["qlT"])
            mk.op("act", lambda: A.activation(out=sq[:, :, :T], in_=v4(pb[0][:, 0:384], T, 3), func=AF.Square),
                  reads=["pb0"], writes=["sq"])
            for c in range(3):
                mk.op("pe", lambda c=c: PE.matmul(pb[1][:, 0:T], onesb[:, :], sq[:, c, :T], start=(c == 0), stop=(c == 2)),
                      reads=["onesb", "sq"], writes=["pb1"])
            R = rbc[:, :T]
            mk.op("dve", lambda: V.tensor_scalar(R, pb[1][:, 0:T], 1.0 / 384, EPS, ALU.mult, ALU.add), reads=["pb1"], writes=["rbc"])
            mk.op("act", lambda: A.activation(out=R, in_=R, func=AF.Sqrt), reads=["rbc"], writes=["rbc"])
            mk.op("dve", lambda: V.reciprocal(R, R), reads=["rbc"], writes=["rbc"])
            mk.op("dve", lambda: V.tensor_scalar(R, R, ATTN_SCALE, None, ALU.mult), reads=["rbc"], writes=["rbc"])
            for h in range(8):
                bank = pb[2 + h // 4]
                for c in range(3):
                    mk.op("pe", lambda h=h, c=c, bank=bank: PE.matmul(bank[0:64, (h % 4) * 128:(h % 4) * 128 + T], wuq[:, c, h * 128:h * 128 + 64],
                                                                     qlT[:, c, :T], start=(c == 0), stop=(c == 2)),
                          reads=["wuq", "qlT"], writes=["pb%d" % (2 + h // 4)])
            rb = (4, 5)
            sb_ = (6, 0)
            for h in range(8):
                for (banks, off) in ((rb, 64), (sb_, 96)):
                    bi = banks[h // 4]
                    for c in range(3):
                        mk.op("pe", lambda h=h, c=c, bi=bi, off=off: PE.matmul(
                            pb[bi][0:32, (h % 4) * 128:(h % 4) * 128 + T], wuq[:, c, h * 128 + off:h * 128 + off + 32],
                            qlT[:, c, :T], start=(c == 0), stop=(c == 2)),
                            reads=["wuq", "qlT", "sq", "qlT"], writes=["pb%d" % bi])
            qr3 = qr[:, 0:8 * T].rearrange("p (h t) -> p h t", h=8)
            qrt3 = qrt[:, 0:8 * T].rearrange("p (h t) -> p h t", h=8)
            qrs3 = qrs[:, 0:8 * T].rearrange("p (h t) -> p h t", h=8)
            for hh in range(2):
                hs = slice(hh * 4, (hh + 1) * 4)
                mk.op("act", lambda hh=hh, hs=hs: A.copy(qT[:, hs, :T], v4(pb[2 + hh][0:64, :], T)),
                      reads=["pb%d" % (2 + hh)], writes=["qT"])
                mk.op("dve", lambda hh=hh, hs=hs: V.tensor_tensor(qrt3[:, hs, :], v4(pb[rb[hh]][0:32, :], T),
                                                                 cosT[:, :T].unsqueeze(1).to_broadcast([32, 4, T]), ALU.mult),
                      reads=["pb%d" % rb[hh], "cosT"], writes=["qrt"])
                mk.op("dve", lambda hh=hh, hs=hs: V.tensor_tensor(qrs3[:, hs, :], v4(pb[sb_[hh]][0:32, :], T),
                                                                 sinT[:, :T].unsqueeze(1).to_broadcast([32, 4, T]), ALU.mult),
                      reads=["pb%d" % sb_[hh], "sinT"], writes=["qrs"])
                mk.op("pool", lambda hs=hs: G.tensor_tensor(qrt3[:, hs, :], qrt3[:, hs, :], qrs3[:, hs, :], ALU.add),
                      reads=["qrt", "qrs"], writes=["qrt"])
                mk.op("dve", lambda hs=hs: V.tensor_tensor(qr3[:, hs, :], qrt3[:, hs, :],
                                                           rbc[0:32, :T].unsqueeze(1).to_broadcast([32, 4, T]), ALU.mult),
                      reads=["qrt", "rbc"], writes=["qr"])
            for rc in range(2):
                qa3 = qa[:, rc, 0:8 * T].rearrange("p (h t) -> p h t", h=8)
                for hh in range(2):
                    bank = pb[hh]
                    for h4 in range(4):
                        h = hh * 4 + h4
                        mk.op("pe", lambda h=h, h4=h4, rc=rc, bank=bank: PE.matmul(
                            bank[:, h4 * 128:h4 * 128 + T], wuk[:, h * 256 + rc * 128:h * 256 + (rc + 1) * 128], qT[0:64, h, :T],
                            start=True, stop=True), reads=["wuk", "qT"], writes=["pb%d" % hh])
                    mk.op("dve", lambda rc=rc, hh=hh, bank=bank, qa3=qa3: V.tensor_tensor(
                        qa3[:, hh * 4:(hh + 1) * 4, :], v4(bank[:], T),
                        rbc[:, :T].unsqueeze(1).to_broadcast([128, 4, T]), ALU.mult),
                        reads=["pb%d" % hh, "rbc"], writes=["qa"])
            nk_tiles = len(ktiles)
            for hf in range(NHF):
                cols = slice(hf * CW, (hf + 1) * CW)

                def st_S(ki):
                    kd = ktiles[ki]
                    nk = kd["nk"]
                    sbank = pb[ki % 2]
                    skey = "pb%d" % (ki % 2)
                    P = PT[ki % 2]
                    pkey = "PT%d" % (ki % 2)
                    mk.op("pe", lambda: PE.matmul(sbank[:nk, :CW], kd["k0"], qa[:, 0, cols], start=True, stop=False),
                          reads=kd["keys"] + ["qa"], writes=[skey])
                    mk.op("pe", lambda: PE.matmul(sbank[:nk, :CW], kd["k1"], qa[:, 1, cols], start=False, stop=False),
                          reads=kd["keys"] + ["qa"], writes=[skey])
                    mk.op("pe", lambda: PE.matmul(sbank[:nk, :CW], kd["kr"], qr[:, cols], start=False, stop=True),
                          reads=kd["keys"] + ["qr"], writes=[skey])
                    mk.op("act", lambda: A.activation(out=P[:nk, :CW], in_=sbank[:nk, :CW], func=AF.Exp), reads=[skey], writes=[pkey])
                    if kd.get("mask") is not None:
                        mk.op("dve", lambda: V.tensor_tensor(hv(P[:nk, :CW]), hv(P[:nk, :CW]),
                                                             kd["mask"].unsqueeze(1).to_broadcast([nk, HPH, T]), ALU.mult),
                              reads=[pkey] + kd.get("mkeys", ["cst"]), writes=[pkey])

                def st_PV(ki):
                    kd = ktiles[ki]
                    nk = kd["nk"]
                    P = PT[ki % 2]
                    pkey = "PT%d" % (ki % 2)
                    first = (ki == 0)
                    last = (ki == nk_tiles - 1)
                    for rc in range(2):
                        mk.op("pe", lambda rc=rc: PE.matmul(pb[2 + rc][:, :CW], kd["v"][:, rc * 128:(rc + 1) * 128], P[:nk, :CW],
                                                            start=first, stop=last),
                              reads=kd["keys"] + [pkey], writes=["pb%d" % (2 + rc)])
                    mk.op("pe", lambda: PE.matmul(pb[4][:, :CW], onesb[:nk, :], P[:nk, :CW], start=first, stop=last),
                          reads=[pkey, "onesb"], writes=["pb4"])

                for it in range(nk_tiles + 2):
                    if it < nk_tiles and ktiles[it].get("prep") is not None:
                        ktiles[it]["prep"]()
                    if 0 <= it - 1 < nk_tiles:
                        st_S(it - 1)
                    if 0 <= it - 2 < nk_tiles:
                        st_PV(it - 2)
                mk.op("dve", lambda: V.reciprocal(lbc[:, :CW], pb[4][:, :CW]), reads=["pb4"], writes=["lbc"])
                for rc in range(2):
                    mk.op("dve", lambda rc=rc: V.tensor_tensor(oT[:, rc, cols], pb[2 + rc][:, :CW], lbc[:, :CW], ALU.mult),
                          reads=["pb%d" % (2 + rc), "lbc"], writes=["oT"])
            for h in range(8):
                for rc in range(2):
                    mk.op("pe", lambda h=h, rc=rc: PE.matmul(pb[5][:T, h * 64:(h + 1) * 64], oT[:, rc, h * T:(h + 1) * T],
                                                             wuv[:, rc, h * 64:(h + 1) * 64], start=(rc == 0), stop=(rc == 1)),
                          reads=["oT", "wuv"], writes=["pb5"])
            mk.op("act", lambda: A.copy(of[:T, :], pb[5][:T, :]), reads=["pb5"], writes=["of"])
            rstd_from(of[:T, :], T, 512, 3, ["of"])
            mk.op("dve", lambda: V.scalar_tensor_tensor(mix[:T, 512:1024], of[:T, :], ss[:T, 3:4], g_attn[:T, :], ALU.mult, ALU.mult),
                  reads=["of", "ss3", "grow"], writes=["mix"])
            transpose8(mix, "mix", T, mixT, "mixT")
            for hf in range(2):
                lin_tok(pb[5 + hf][:T, :], "pb%d" % (5 + hf), mixT, "mixT", T, wo, "wo", hf * 512, (hf + 1) * 512)
                mk.op("dve", lambda hf=hf: V.tensor_tensor(x1[:T, hf * 512:(hf + 1) * 512], pb[5 + hf][:T, :], g1[:T, hf * 512:(hf + 1) * 512], ALU.mult),
                      reads=["pb%d" % (5 + hf), "g1"], writes=["x1"])
            mk.op("pool", lambda: G.tensor_tensor(x1[:T, :], x1[:T, :], x_t[:T, :], ALU.add), reads=["x1", xk], writes=["x1"])
            mk.dma("sp", "st_x1", x1_dst, x1[:T, :], reads=["x1"], writes=[x1key])

        if ntl and stop >= 1.2:
            load_mod(gs1, sh1, ada_p, 128, 0, 1024, 0, 0, "m1b")
            mk.dma("sp", "ld_g1", g1[:], ada_p[:, 2048:3072], reads=["ada0"], writes=["g1"])
            if balanced:
                amask_t = sbB("amask_t", [128, 8, 128], BF16)
                mk.dma("pool", "ld_amask", amask_t[:], amask, writes=["amask_t"])
                idxo = sbB("idxo", [128, NB], I32)
                mk.dma("sp", "ld_idxo", idxo[:], idx_own, writes=["idxo"])
            for m in range(NB):
                x_t = xt[m % 2]
                xk = "xtb%d" % (m % 2)
                ms = slice(m * 128, (m + 1) * 128)
                kts = []
                if balanced:
                    mk.dma("sp", "ld_" + xk, x_t[:], x_own[ms, :], writes=[xk])
                    sel = 0 if m < NB // 2 else 1
                    for kt in range(4 * m + 4):
                        ks = slice(kt * 128, (kt + 1) * 128)
                        msk = amask_t[:, sel * 4 + (kt - 4 * m), :] if kt >= 4 * m else None
                        kts.append(dict(k0=KT[:, 0, ks], k1=KT[:, 1, ks], kr=KTr[:, ks], v=Vt[:, kt, :], nk=128,
                                        keys=["KT%d" % kt, "Vt%d" % kt], mask=msk, mkeys=["amask_t"]))

                    def ld_mix(m=m):
                        mk.gather("d_mix", mix[:, 0:512], yssd_scr[0:ntl * 128, :], idxo[:, m:m + 1], reads=["idxo"], writes=["mix"])
                    b1_tile(128, x_t, xk, rope_own[0:32, ms], rope_own[32:64, ms], ld_mix, [], kts, x1_scr[ms, :], "x1s%d" % m)
                else:
                    t = m
                    mk.dma("sp", "ld_" + xk, x_t[:], xp[ms, :], writes=[xk])
                    for kt in range(t + 1):
                        ks = slice(kt * 128, (kt + 1) * 128)
                        kts.append(dict(k0=KT[:, 0, ks], k1=KT[:, 1, ks], kr=KTr[:, ks], v=Vt[:, kt, :], nk=128,
                                        keys=["KT%d" % kt, "Vt%d" % kt], mask=(M_le if kt == t else None)))
                    b1_tile(128, x_t, xk, rope_ch[0:32, ms], rope_ch[32:64, ms], yssd_scr[ms, :], ["yssd%d" % t], kts,
                            x1_scr[ms, :], "x1s%d" % t)
        for s_ in range(ns if sstop >= 2 else 0):
            x_t = xt[s_ % 2]
            xk = "xtb%d" % (s_ % 2)
            r4 = slice(4 * s_, 4 * s_ + 4)
            mk.dma("sp", "ld_" + xk, x_t[:4, :], xs[r4, :], writes=[xk])
            load_mod(gs1, sh1, ada_s[r4, :], 4, 1, 1024, 0, 0, "m1b")
            mk.dma("sp", "ld_g1", g1[:4, :], ada_s[r4, 2048:3072], reads=["ada1"], writes=["g1"])
            mk.dma("sp", "ld_ptb", ptb[:], ptab[s_:s_ + 1, :].partition_broadcast(128), writes=["ptb"])
            mk.op("dve", lambda: V.tensor_scalar(idx[0:64, 0:32], ptb[0:64, 0:64:2], 6, None, ALU.logical_shift_left),
                  reads=["ptb"], writes=["idx"])
            mk.op("dve", lambda: V.tensor_scalar(idx[64:128, 0:32], ptb[64:128, 1:64:2], 6, None, ALU.logical_shift_left),
                  reads=["ptb"], writes=["idx"])
            mk.op("dve", lambda: V.tensor_tensor(idx[:, 0:32], idx[:, 0:32], iot[:, 0:32], ALU.bitwise_or), reads=["idx", "iot"], writes=["idx"])
            kts = []
            for g in range(NPAGES):
                q_, e_ = g // 2, g % 2
                vb = q_ % 2
                kb = g % 4

                def prep(g=g, q_=q_, e_=e_, vb=vb, kb=kb):
                    if e_ == 0:
                        mk.gather("d_Vpg%d" % vb, Vpg[vb][:].rearrange("p e c -> p (e c)"), cache_all, idx[:, q_:q_ + 1],
                                  reads=["idx"], writes=["Vpg%d" % vb])
                    for rc in range(2):
                        mk.op("pe", lambda rc=rc: PE.transpose(pbT[:, rc * 128:(rc + 1) * 128], Vpg[vb][:, e_, rc * 128:(rc + 1) * 128], identb[:]),
                              reads=["Vpg%d" % vb, "identb"], writes=["pbT"])
                    mk.op("pe", lambda: PE.transpose(pbT[0:32, 256:384], Vpg[vb][:, e_, 256:288], identb[:]),
                          reads=["Vpg%d" % vb, "identb"], writes=["pbT"])
                    mk.op("act", lambda: A.copy(KTpg[kb][:], pbT[:, 0:256].rearrange("p (c k) -> p c k", c=2)),
                          reads=["pbT"], writes=["KTpg%d" % kb])
                    mk.op("dve", lambda: V.tensor_copy(KTrpg[kb][:], pbT[0:32, 256:384]), reads=["pbT"], writes=["KTpg%d" % kb])
                kts.append(dict(k0=KTpg[kb][:, 0, :], k1=KTpg[kb][:, 1, :], kr=KTrpg[kb][:], v=Vpg[vb][:, e_, 0:256], nk=128,
                                keys=["KTpg%d" % kb, "Vpg%d" % vb], mask=None, prep=prep))
            kts.append(dict(k0=KTs[:, 0, r4], k1=KTs[:, 1, r4], kr=KTrs[:, r4], v=Vs[:, s_, :], nk=4,
                            keys=["KTs%d" % s_, "Vs%d" % s_], mask=M_le[0:4, 0:4]))
            b1_tile(4, x_t, xk, rope_ch[0:32, SEQ + 4 * s_:SEQ + 4 * s_ + 4], rope_ch[32:64, SEQ + 4 * s_:SEQ + 4 * s_ + 4],
                    yssd_s[r4, :], ["yssds%d" % s_], kts, x1_s[r4, :], "x1ss%d" % s_)
    esKV.close()
    mk.barrier()

    with ExitStack() as esC:
        def sbC(name, shape, dt=F32):
            return esC.enter_context(nc.sbuf_tensor(name, list(shape), dt))
        wup = sbC("wup", [128, 8, 4096], BF16)
        load_w_bf16(wup, "wup", w_up, 8, 0, 4096, "ld_wup")
        wdn = sbC("wdn", [128, 32, D], BF16)
        load_w_bf16(wdn, "wdn", w_down, 32, 0, D, "ld_wdn")
        gs2 = sbC("gs2", [128, D])
        sh2 = sbC("sh2", [128, D])
        g2 = sbC("g2", [128, D])
        gsf = sbC("gsf", [128, D])
        shf = sbC("shf", [128, D])
        xt = [sbC("xtc%d" % i, [128, D]) for i in range(2)]
        hb = sbC("hbc", [128, D], BF16)
        hT = sbC("hTc", [128, 8, 128], BF16)
        u2T = sbC("u2T", [128, 32, 128], BF16)
        ur = sbC("ur", [128, 512])
        x2 = sbC("x2", [128, D])

        def b2_tile(T, x_t, xk, y_dst):
            front(x_t, xk, gs2, sh2, "m2", T, hb, hT, "hTc")
            for fb in range(8):
                bank = pb[fb % 4]
                bkey = "pb%d" % (fb % 4)
                for f4 in range(4):
                    fc = fb * 4 + f4
                    for k in range(8):
                        mk.op("pe", lambda fc=fc, f4=f4, k=k, bank=bank: PE.matmul(
                            bank[:, f4 * 128:f4 * 128 + T], wup[:, k, fc * 128:(fc + 1) * 128], hT[:, k, :T],
                            start=(k == 0), stop=(k == 7)), reads=["wup", "hTc"], writes=[bkey])
                urv = ur[:, 0:4 * T].rearrange("p (c t) -> p c t", c=4)
                mk.op("act", lambda bank=bank, urv=urv: A.activation(out=urv, in_=v4(bank[:], T), func=AF.Relu), reads=[bkey], writes=["ur"])
                mk.op("dve", lambda fb=fb, urv=urv: V.tensor_tensor(u2T[:, fb * 4:(fb + 1) * 4, :T], urv, urv, ALU.mult),
                      reads=["ur"], writes=["u2T"])
            for hf in range(2):
                bank = pb[4 + hf]
                for fc in range(32):
                    mk.op("pe", lambda fc=fc, hf=hf, bank=bank: PE.matmul(bank[:T, :], u2T[:, fc, :T], wdn[:, fc, hf * 512:(hf + 1) * 512],
                                                                         start=(fc == 0), stop=(fc == 31)),
                          reads=["u2T", "wdn"], writes=["pb%d" % (4 + hf)])
                mk.op("dve", lambda hf=hf, bank=bank: V.tensor_tensor(x2[:T, hf * 512:(hf + 1) * 512], bank[:T, :], g2[:T, hf * 512:(hf + 1) * 512], ALU.mult),
                      reads=["pb%d" % (4 + hf), "g2"], writes=["x2"])
            mk.op("pool", lambda: G.tensor_tensor(x2[:T, :], x2[:T, :], x_t[:T, :], ALU.add), reads=["x2", xk], writes=["x2"])
            rstd_from(x2[:T, :], T, 1024, 1, ["x2"])
            mk.op("dve", lambda: V.scalar_tensor_tensor(x2[:T, :], x2[:T, :], ss[:T, 1:2], gsf[:T, :], ALU.mult, ALU.mult),
                  reads=["x2", "ss1", "mfgs"], writes=["x2"])
            mk.op("pool", lambda: G.tensor_tensor(x2[:T, :], x2[:T, :], shf[:T, :], ALU.add), reads=["x2", "mfsh"], writes=["x2"])
            mk.dma("sp", "st_y", y_dst, x2[:T, :], reads=["x2"])

        if ntl and stop >= 3:
            load_mod(gs2, sh2, ada_p, 128, 0, 4096, 3072, 1024, "m2")
            load_mod(gsf, shf, ada_p, 128, 0, 7168, 6144, 2048, "mf")
            mk.dma("sp", "ld_g2", g2[:], ada_p[:, 5120:6144], reads=["ada0"], writes=["g2"])
            for t in range(NB):
                x_t = xt[t % 2]
                xk = "xtc%d" % (t % 2)
                ts_ = slice(t * 128, (t + 1) * 128)
                mk.dma("sp", "ld_" + xk, x_t[:], x1_scr[ts_, :], reads=["x1s%d" % t], writes=[xk])
                b2_tile(128, x_t, xk, o_y[ts_, :])
        if ns and sstop >= 3:
            TS = 4 * ns
            x_t = xt[0]
            xk = "xtc0"
            mk.dma("sp", "ld_" + xk, x_t[:TS, :], x1_s[0:TS, :], reads=["x1ss%d" % i for i in range(ns)], writes=[xk])
            load_mod(gs2, sh2, ada_s[0:TS, :], TS, 1, 4096, 3072, 1024, "m2")
            load_mod(gsf, shf, ada_s[0:TS, :], TS, 1, 7168, 6144, 2048, "mf")
            mk.dma("sp", "ld_g2", g2[:TS, :], ada_s[0:TS, 5120:6144], reads=["ada1"], writes=["g2"])
            b2_tile(TS, x_t, xk, os_y[0:TS, :])

    mk.finish("sp")
    return nc


def _consts():
    ident = np.eye(128, dtype=np.float32)
    m = np.arange(128)
    M_le = (m[:, None] <= m[None, :]).astype(np.float32)
    M_gt = (m[:, None] > m[None, :]).astype(np.float32)
    ones = np.ones((128, 128), np.float32)
    c = np.concatenate([ident, M_le, M_gt, ones], axis=1)
    return np.ascontiguousarray(c)


def _rope_tables():
    inv = 1.0 / (10000.0 ** (np.arange(0, 32, 2, dtype=np.float32) / 32.0))
    posp = np.arange(SEQ, dtype=np.float32)
    poss = np.tile(PAST + np.arange(4, dtype=np.float32), 16)
    pos = np.concatenate([posp, poss])
    ang = pos[:, None] * inv[None, :]
    cos, sin = np.cos(ang).astype(np.float32), np.sin(ang).astype(np.float32)
    tok = np.concatenate([cos, cos, -sin, sin], axis=1)
    return np.ascontiguousarray(tok), np.ascontiguousarray(tok.T)


def own_tiles(j, ntiles=NT):
    nb = ntiles // 4
    return [4 * m + (j if m < nb // 2 else 3 - j) for m in range(nb)]


def _host_inputs(inp, core, n_phys_full=True, do_sample=True, ntiles=NT):
    f = np.float32
    b = core // 4
    d = {}
    d["xp"] = np.ascontiguousarray(inp["x_prompt"][b])
    d["cp"] = np.ascontiguousarray(np.broadcast_to(inp["c_prompt"][b][None, :], (128, D))).astype(f)
    s0 = core * NSEQ_CORE
    d["xs"] = np.ascontiguousarray(inp["x_sample"][s0:s0 + NSEQ_CORE].reshape(64, D))
    d["cs"] = np.ascontiguousarray(np.repeat(inp["c_sample"][s0:s0 + NSEQ_CORE], 4, axis=0))
    d["w_ada"] = np.ascontiguousarray(inp["w_ada"][0])
    d["w_adaf"] = np.ascontiguousarray(inp["w_ada_final"])
    d["b_ada"] = np.concatenate([inp["b_ada"][0], inp["b_ada_final"]])[None, :].astype(f)
    rows = np.zeros((1, 7 * 1024), f)
    rows[0, 0:1024] = inp["norm_mix_g"][0]
    rows[0, 1024:2048] = inp["norm_mlp_g"][0]
    rows[0, 2048:3072] = inp["norm_final_g"]
    rows[0, 3072:3584] = inp["norm_ssd_g"][0]
    rows[0, 3584:4096] = inp["norm_attn_g"][0]
    rows[0, 4096:4352] = inp["kv_norm_g"][0]
    d["rows"] = rows
    small = np.zeros((1, 32), f)
    small[0, 0:8] = inp["dt_bias"][0]
    small[0, 8:16] = inp["a_log"][0]
    small[0, 16:24] = inp["d_skip"][0]
    d["small"] = small
    w_in = inp["w_in"][0]
    kr = w_in[:, 2184:2216]
    kr_sw = np.concatenate([kr[:, 16:32], kr[:, 0:16]], axis=1)
    d["w_in"] = np.ascontiguousarray(np.concatenate([w_in, kr_sw], axis=1))
    cw = np.concatenate([inp["conv_w"][0], inp["conv_b"][0][None, :]], axis=0)
    d["convw"] = np.ascontiguousarray(cw.T.reshape(8, 128, 5).transpose(1, 0, 2))
    d["qg"] = np.ascontiguousarray(inp["q_norm_g"][0].reshape(3, 128).T)
    wq = inp["w_uq"][0].reshape(384, 8, 96)
    rope = wq[:, :, 64:96]
    rope_sw = np.concatenate([rope[:, :, 16:32], rope[:, :, 0:16]], axis=2)
    d["w_uq"] = np.ascontiguousarray(np.concatenate([wq, rope_sw], axis=2).reshape(384, 1024))
    d["w_ukT"] = np.ascontiguousarray(inp["w_uk"][0].transpose(2, 1, 0).reshape(64, 2048))
    d["w_uv"] = np.ascontiguousarray(inp["w_uv"][0].reshape(256, 512))
    d["w_out"] = np.ascontiguousarray(inp["w_out"][0])
    d["w_up"] = np.ascontiguousarray(inp["w_up"][0])
    d["w_down"] = np.ascontiguousarray(inp["w_down"][0])
    rt, rc = _rope_tables()
    d["rope_tok"] = rt
    d["rope_ch"] = rc
    if ntiles % 8 == 0 and ntiles > 0:
        j = core % 4
        tl = own_tiles(j, ntiles)
        rowsel = np.concatenate([np.arange(t * 128, (t + 1) * 128) for t in tl])
        d["x_own"] = np.ascontiguousarray(inp["x_prompt"][b][rowsel])
        d["rope_own"] = np.ascontiguousarray(rc[:, rowsel])
        d["idx_own"] = np.ascontiguousarray(rowsel.reshape(len(tl), 128).T.astype(np.int32))
        m_ = np.arange(128)
        tri = (m_[:, None] <= m_[None, :]).astype(np.float32)
        am = np.zeros((128, 8, 128), np.float32)
        for sel, o in enumerate((j, 3 - j)):
            for i in range(4):
                am[:, sel * 4 + i, :] = 1.0 if i < o else (tri if i == o else 0.0)
        d["amask"] = am
    d["consts"] = _consts()
    if do_sample:
        if "_cache_all" not in inp:
            inp["_cache_all"] = np.ascontiguousarray(np.concatenate(
                [inp["cache_kv_latent"][0].reshape(-1, 256), inp["cache_k_rope"][0].reshape(-1, 32)], axis=1)).reshape(-1, 576)
        d["cache_all"] = inp["_cache_all"]
        d["ptab"] = np.ascontiguousarray(inp["page_table"][s0:s0 + NSEQ_CORE]).astype(np.int32)
        d["st_conv"] = np.ascontiguousarray(inp["state_conv"][0, s0:s0 + NSEQ_CORE].reshape(-1, D))
        d["st_ssm"] = np.ascontiguousarray(inp["state_ssm"][0, s0:s0 + NSEQ_CORE].reshape(-1, 128))
    return d


def kernel(**inputs):
    inp = {k: np.asarray(v) for k, v in inputs.items()}
    n_phys = int(inp["cache_kv_latent"].shape[1])
    nc = build(n_phys)
    in_maps = [_host_inputs(inp, c) for c in range(8)]
    res = run_bass_kernel_spmd(nc, in_maps, core_ids=list(range(8)))
    r = res.results
    f = np.float32
    pc = (0, 4)
    y_p = np.zeros((2, SEQ, D), f)
    for c in range(8):
        for m, t in enumerate(own_tiles(c % 4)):
            y_p[c // 4, t * 128:(t + 1) * 128] = r[c]["o_y"][m * 128:(m + 1) * 128]
    kv_p = np.stack([r[c]["o_kv"] for c in pc])[None].astype(f)
    kr_p = np.stack([r[c]["o_kr"] for c in pc])[None].astype(f)
    conv_p = np.stack([r[c]["o_conv"] for c in pc])[None].astype(f)
    ssm_p = np.stack([r[c]["o_ssm"].reshape(8, 64, 128) for c in pc])[None].astype(f)
    y_s = np.concatenate([r[c]["os_y"] for c in range(8)]).reshape(128, 4, D).astype(f)
    kv_s = np.concatenate([r[c]["os_kv"] for c in range(8)]).reshape(1, 128, 4, 256).astype(f)
    kr_s = np.concatenate([r[c]["os_kr"] for c in range(8)]).reshape(1, 128, 4, 32).astype(f)
    conv_s = np.concatenate([r[c]["os_conv"] for c in range(8)]).reshape(1, 128, 3, D).astype(f)
    ssm_s = np.concatenate([r[c]["os_ssm"] for c in range(8)]).reshape(1, 128, 8, 64, 128).astype(f)
    return (y_p, y_s, kv_p, kr_p, conv_p, ssm_p, kv_s, kr_s, conv_s, ssm_s)
```

```python
import math
from contextlib import ExitStack
import numpy as np
import concourse.bass as bass
import concourse.mybir as mybir
from concourse.bass_utils import run_bass_kernel_spmd

F32 = mybir.dt.float32
BF16 = mybir.dt.bfloat16
I32 = mybir.dt.int32
AF = mybir.ActivationFunctionType
ALU = mybir.AluOpType

D = 1024
SEQ = 8192
NT = SEQ // 128
EPS = 1e-6
ATTN_SCALE = 1.0 / math.sqrt(96.0)
NSEQ_CORE = 16
PAST = 8192
NPAGES = 64
WIN = 2248


class MK:
    def __init__(self, nc):
        self.nc = nc
        self.eng = {"pe": nc.tensor, "act": nc.scalar, "dve": nc.vector,
                    "pool": nc.gpsimd, "sp": nc.sync}
        self.sems = {}
        self.ecnt = {}
        for k in self.eng:
            self.sems["es_" + k] = nc.alloc_semaphore("es_" + k)
            self.ecnt[k] = 0
        self.waited = {k: {} for k in self.eng}
        self.dcnt = {}
        self.last_w = {}
        self.readers = {}
        self.counting = False
        self.limit = None
        self.n = 0

    def _skip(self):
        if self.counting:
            self.n += 1
            if self.limit is not None and self.n > self.limit:
                return True
        return False

    def _deps(self, reads, writes):
        toks = []
        for k in reads:
            t = self.last_w.get(k)
            if t is not None:
                toks.append(t)
        for k in writes:
            t = self.last_w.get(k)
            if t is not None:
                toks.append(t)
            toks.extend(self.readers.get(k, ()))
        return toks

    def _wait(self, e, toks):
        best = {}
        for (s, v) in toks:
            if v > best.get(s, 0):
                best[s] = v
        w = self.waited[e]
        for s, v in best.items():
            if w.get(s, 0) < v:
                self.eng[e].wait_ge(self.sems[s], v)
                w[s] = v

    def _record(self, tok, reads, writes):
        for k in reads:
            lst = self.readers.setdefault(k, [])
            lst.append(tok)
            if len(lst) > 24:
                best = {}
                for (s, v) in lst:
                    if v > best.get(s, 0):
                        best[s] = v
                self.readers[k] = list(best.items())
        for k in writes:
            self.last_w[k] = tok
            self.readers[k] = []

    def record(self):
        self.rec = []
        self.grp = None
        self.gid = 0

    def atomic(self, on):
        if on:
            self.gid = getattr(self, "gid", 0) + 1
            self.grp = self.gid
        else:
            self.grp = None

    def stop_record(self):
        r, self.rec = self.rec, None
        return r

    def play(self, progs):
        progs = [p_ for p_ in progs if p_]
        pos = [0] * len(progs)
        while True:
            best, bi = None, -1
            for i, p_ in enumerate(progs):
                if pos[i] < len(p_):
                    frac = pos[i] / len(p_)
                    if best is None or frac < best:
                        best, bi = frac, i
            if bi < 0:
                break
            g0 = progs[bi][pos[bi]][-1]
            while pos[bi] < len(progs[bi]):
                item = progs[bi][pos[bi]]
                if item[-1] != g0 or (g0 is None and item is not progs[bi][pos[bi]]):
                    break
                pos[bi] += 1
                if item[0] == "op":
                    self.op(*item[1:5])
                elif item[0] == "gather":
                    self.gather(*item[1:7])
                else:
                    self.dma(*item[1:7], **item[7])
                if g0 is None:
                    break

    def op(self, e, fn, reads=(), writes=()):
        if getattr(self, "rec", None) is not None:
            self.rec.append(("op", e, fn, tuple(reads), tuple(writes), getattr(self, "grp", None)))
            return None
        if self._skip():
            return None
        self._wait(e, self._deps(reads, writes))
        ins = fn()
        self.ecnt[e] += 1
        ins.then_inc(self.sems["es_" + e], 1)
        tok = ("es_" + e, self.ecnt[e])
        self._record(tok, reads, writes)
        return tok

    def _slot(self, slot):
        if slot not in self.sems:
            self.sems[slot] = self.nc.alloc_semaphore(slot)
            self.dcnt[slot] = 0

    def dma(self, q, slot, out, in_, reads=(), writes=(), **kw):
        if getattr(self, "rec", None) is not None:
            self.rec.append(("dma", q, slot, out, in_, tuple(reads), tuple(writes), kw, getattr(self, "grp", None)))
            return None
        if self._skip():
            return None
        if slot.startswith("ld_") and writes:
            slot = "d_" + writes[0]
        self._slot(slot)
        self._wait(q, self._deps(reads, writes))
        ins = self.eng[q].dma_start(out=out, in_=in_, **kw)
        self.dcnt[slot] += 16
        ins.then_inc(self.sems[slot], 16)
        tok = (slot, self.dcnt[slot])
        self._record(tok, reads, writes)
        return tok

    def gather(self, slot, out, in_, idx_ap, reads=(), writes=()):
        if getattr(self, "rec", None) is not None:
            self.rec.append(("gather", slot, out, in_, idx_ap, tuple(reads), tuple(writes), getattr(self, "grp", None)))
            return None
        self._slot(slot)
        self._wait("pool", self._deps(reads, writes))
        ins = self.nc.gpsimd.indirect_dma_start(
            out=out, out_offset=None, in_=in_,
            in_offset=bass.IndirectOffsetOnAxis(ap=idx_ap, axis=0))
        self.dcnt[slot] += 16
        ins.then_inc(self.sems[slot], 16)
        tok = (slot, self.dcnt[slot])
        self._record(tok, reads, writes)
        return tok

    def barrier(self):
        toks = [(s, v) for s, v in self.dcnt.items() if v]
        for k, c in self.ecnt.items():
            if c:
                toks.append(("es_" + k, c))
        for e in self.eng:
            self._wait(e, [t for t in toks if t[0] != "es_" + e])

    def finish(self, e="sp"):
        toks = [(s, v) for s, v in self.dcnt.items() if v]
        for k, c in self.ecnt.items():
            if c and k != e:
                toks.append(("es_" + k, c))
        self._wait(e, toks)


def build(n_phys, do_prompt=True, do_sample=True, ntiles=NT, nseq=NSEQ_CORE, stop=99, sstop=99, oplimit=None):
    nc = bass.Bass("TRN2", target_bir_lowering=False)
    mk = MK(nc)
    V, A, G, PE = nc.vector, nc.scalar, nc.gpsimd, nc.tensor

    def din(name, shape, dt=F32):
        return nc.dram_tensor(name, list(shape), dt, kind="ExternalInput").ap()

    def dout(name, shape, dt=F32):
        return nc.dram_tensor(name, list(shape), dt, kind="ExternalOutput").ap()

    def dscr(name, shape, dt=F32):
        return nc.dram_tensor(name, list(shape), dt).ap()

    xp = din("xp", [SEQ, D])
    cp = din("cp", [128, D])
    xs = din("xs", [64, D])
    cs = din("cs", [64, D])
    w_ada = din("w_ada", [D, 6144])
    w_adaf = din("w_adaf", [D, 2048])
    b_ada = din("b_ada", [1, 8192])
    rows = din("rows", [1, 7 * 1024])
    small = din("small", [1, 32])
    w_in = din("w_in", [D, WIN])
    convw = din("convw", [128, 8, 5])
    qg = din("qg", [128, 3])
    w_uq = din("w_uq", [384, 1024])
    w_ukT = din("w_ukT", [64, 8 * 256])
    w_uv = din("w_uv", [256, 512])
    w_out = din("w_out", [D, D])
    w_up = din("w_up", [D, 4096])
    w_down = din("w_down", [4096, D])
    rope_tok = din("rope_tok", [SEQ + 64, 64])
    rope_ch = din("rope_ch", [64, SEQ + 64])
    consts = din("consts", [128, 4 * 128])
    balanced = do_prompt and ntiles > 0 and ntiles % 8 == 0
    NB = ntiles // 4 if balanced else ntiles
    if balanced:
        x_own = din("x_own", [NB * 128, D])
        rope_own = din("rope_own", [64, NB * 128])
        idx_own = din("idx_own", [128, NB], I32)
        amask = din("amask", [128, 8, 128])
    if do_sample:
        cache_all = din("cache_all", [n_phys * 64, 576])
        ptab = din("ptab", [nseq, 64], I32)
        st_conv = din("st_conv", [nseq * 3, D])
        st_ssm = din("st_ssm", [nseq * 512, 128])

    o_y = dout("o_y", [NB * 128 if balanced else SEQ, D])
    o_kv = dout("o_kv", [SEQ, 256])
    o_kr = dout("o_kr", [SEQ, 32])
    o_conv = dout("o_conv", [3, D])
    o_ssm = dout("o_ssm", [512, 128])
    if do_sample:
        os_y = dout("os_y", [64, D])
        os_kv = dout("os_kv", [64, 256])
        os_kr = dout("os_kr", [64, 32])
        os_conv = dout("os_conv", [nseq * 3, D])
        os_ssm = dout("os_ssm", [nseq * 512, 128])

    ada_p = dscr("ada_p", [128, 8192])
    ada_s = dscr("ada_s", [64, 8192])
    yssd_scr = dscr("yssd_scr", [SEQ, 512], BF16)
    x1_scr = dscr("x1_scr", [SEQ, D])

    es = ExitStack()

    def sb(name, shape, dt=F32):
        return es.enter_context(nc.sbuf_tensor(name, list(shape), dt))

    pb = [nc.alloc_psum_tensor("pb%d" % i, [128, 512], F32) for i in range(7)]
    pbT = nc.alloc_psum_tensor("pbT", [128, 1024], BF16)

    cst = sb("cst", [128, 512])
    mk.dma("sp", "ld_c", cst[:], consts, writes=["cst"])
    identb = sb("identb", [128, 128], BF16)
    mk.op("dve", lambda: V.tensor_copy(identb[:], cst[:, 0:128]), reads=["cst"], writes=["identb"])
    ident = cst[:, 0:128]
    M_le = cst[:, 128:256]
    M_gt = cst[:, 256:384]
    ones = cst[:, 384:512]
    cstb = sb("cstb", [128, 512], BF16)
    mk.op("dve", lambda: V.tensor_copy(cstb[:], cst[:]), reads=["cst"], writes=["cstb"])
    onesb = sb("onesb", [128, 128], BF16)
    mk.op("dve", lambda: V.tensor_copy(onesb[:], cst[:, 384:512]), reads=["cst"], writes=["onesb"])
    smallt = sb("smallt", [128, 32])
    mk.dma("sp", "ld_c", smallt[:], small.partition_broadcast(128), writes=["smallt"])
    a_t = sb("a_t", [128, 8])
    mk.op("act", lambda: A.activation(out=a_t[:], in_=smallt[:, 8:16], func=AF.Exp), reads=["smallt"], writes=["a_t"])
    mk.op("dve", lambda: V.tensor_scalar(a_t[:], a_t[:], -1.0, None, ALU.mult), reads=["a_t"], writes=["a_t"])
    dtb = smallt[:, 0:8]
    dsk = smallt[:, 16:24]
    convw_t = sb("convw_t", [128, 8, 5])
    mk.dma("sp", "ld_c", convw_t[:], convw, writes=["convw_t"])
    qg_t = sb("qg_t", [128, 3])
    mk.dma("sp", "ld_c", qg_t[:], qg, writes=["qg_t"])
    grow = sb("grow", [128, 2048])
    mk.dma("sp", "ld_c", grow[:, 0:1024], rows[:, 3072:4096].partition_broadcast(128), writes=["grow"])
    mk.dma("sp", "ld_c", grow[:, 1024:1280], rows[:, 4096:4352].partition_broadcast(128), writes=["grow"])
    g_ssd = grow[:, 0:512]
    g_attn = grow[:, 512:1024]
    g_kv = grow[:, 1024:1280]

    ss = sb("ss", [128, 4])
    junk = sb("junk", [128, 1024])

    def rstd_from(src_ap, T, n, col, keyr, scr=None, scrkey="junk"):
        scr = junk if scr is None else scr
        mk.op("act", lambda: A.activation(out=scr[:T, 0:n], in_=src_ap, func=AF.Square,
                                          accum_out=ss[:T, col:col + 1]),
              reads=keyr, writes=[scrkey, "ss%d" % col])
        mk.op("dve", lambda: V.tensor_scalar(ss[:T, col:col + 1], ss[:T, col:col + 1], 1.0 / n, EPS, ALU.mult, ALU.add),
              reads=["ss%d" % col], writes=["ss%d" % col])
        mk.op("act", lambda: A.activation(out=ss[:T, col:col + 1], in_=ss[:T, col:col + 1], func=AF.Sqrt),
              reads=["ss%d" % col], writes=["ss%d" % col])
        mk.op("dve", lambda: V.reciprocal(ss[:T, col:col + 1], ss[:T, col:col + 1]),
              reads=["ss%d" % col], writes=["ss%d" % col])

    def load_w_bf16(dst, dst_key, src, k_chunks, c0, c1, slot):
        step = 2048
        for a in range(c0, c1, step):
            b = min(c1, a + step)
            mk.dma("pool", slot, dst[:, :, a - c0:b - c0],
                   src[:, a:b].rearrange("(k p) n -> p k n", p=128), writes=[dst_key])

    with ExitStack() as es0:
        def sb0(name, shape, dt=F32):
            return es0.enter_context(nc.sbuf_tensor(name, list(shape), dt))
        ct = sb0("ct", [128, D])
        cb = sb0("cb", [128, D], BF16)
        scT = sb0("scT", [128, 8, 256], BF16)
        wch = [sb0("wch%d" % i, [128, 8, 512], BF16) for i in range(2)]
        bch = [sb0("bch%d" % i, [128, 512]) for i in range(2)]
        ost = [sb0("ost%d" % i, [128, 512]) for i in range(2)]
        for which, (src, T, c0) in enumerate(((cp, 128, 0), (cs, 64, 128))):
            mk.dma("sp", "ld_ct", ct[:T, :], src, writes=["ct"])
            mk.op("act", lambda T=T: A.activation(out=cb[:T, :], in_=ct[:T, :], func=AF.Silu), reads=["ct"], writes=["cb"])
            for k in range(8):
                mk.op("pe", lambda k=k, T=T: PE.transpose(pbT[:, k * 128:k * 128 + T], cb[:T, k * 128:(k + 1) * 128], identb[:T, :T]),
                      reads=["cb", "identb"], writes=["pbT"])
            mk.op("dve", lambda T=T, c0=c0: V.tensor_copy(
                scT[:, :, c0:c0 + T], pbT[:].rearrange("p (k t) -> p k t", k=8)[:, :, 0:T]),
                reads=["pbT"], writes=["scT"])
        for j in range(16):
            s = j % 2
            wsrc, wc0 = (w_ada, j * 512) if j < 12 else (w_adaf, (j - 12) * 512)
            load_w_bf16(wch[s], "wch%d" % s, wsrc, 8, wc0, wc0 + 512, "ld_wch%d" % s)
            mk.dma("sp", "ld_bch%d" % s, bch[s][:], b_ada[:, j * 512:(j + 1) * 512].partition_broadcast(128),
                   writes=["bch%d" % s])
            for which, (T, c0, dst) in enumerate(((128, 0, ada_p), (64, 128, ada_s))):
                bank = pb[which]
                for k in range(8):
                    mk.op("pe", lambda k=k, T=T, c0=c0, bank=bank, s=s: PE.matmul(
                        bank[:T, :], scT[:, k, c0:c0 + T], wch[s][:, k, :], start=(k == 0), stop=(k == 7)),
                        reads=["scT", "wch%d" % s], writes=["pb%d" % which])
                o = ost[which]
                mk.op("dve", lambda T=T, bank=bank, o=o, s=s: V.tensor_tensor(o[:T, :], bank[:T, :], bch[s][:T, :], ALU.add),
                      reads=["pb%d" % which, "bch%d" % s], writes=["ost%d" % which])
                mk.dma("sp", "st_ada%d" % which, dst[:, j * 512:(j + 1) * 512], o[:T, :],
                       reads=["ost%d" % which], writes=["ada%d" % which])

    mk.barrier()

    def load_mod(gs, sh, ada_src, T, which, sc_col, sh_col, g_col, key):
        mk.dma("sp", "ld_" + key, gs[:T, :], ada_src[:, sc_col:sc_col + 1024], reads=["ada%d" % which], writes=[key + "gs"])
        mk.dma("sp", "ld_" + key, sh[:T, :], ada_src[:, sh_col:sh_col + 1024], reads=["ada%d" % which], writes=[key + "sh"])
        mk.dma("sp", "ld_" + key, junk[:T, :], rows[:, g_col:g_col + 1024].partition_broadcast(T), writes=["junk"])
        mk.op("dve", lambda: V.scalar_tensor_tensor(gs[:T, :], gs[:T, :], 1.0, junk[:T, :], ALU.add, ALU.mult),
              reads=[key + "gs", "junk"], writes=[key + "gs"])

    def front(xt, xkey, gs, sh, modkey, T, hb, hT, hkey):
        rstd_from(xt[:T, :], T, 1024, 0, [xkey])
        mk.op("dve", lambda: V.scalar_tensor_tensor(junk[:T, :], xt[:T, :], ss[:T, 0:1], gs[:T, :], ALU.mult, ALU.mult),
              reads=[xkey, "ss0", modkey + "gs"], writes=["junk"])
        mk.op("pool", lambda: G.tensor_tensor(hb[:T, :], junk[:T, :], sh[:T, :], ALU.add),
              reads=["junk", modkey + "sh"], writes=["hb"])
        transpose8(hb, "hb", T, hT, hkey)

    def transpose8(src, skey, T, dstT, dkey, nchunk=8):
        mk.atomic(True)
        for k in range(nchunk):
            mk.op("pe", lambda k=k: PE.transpose(pbT[:, k * 128:k * 128 + T], src[:T, k * 128:(k + 1) * 128], identb[:T, :T]),
                  reads=[skey, "identb"], writes=["pbT"])
        mk.op("act", lambda: A.copy(dstT[:, 0:nchunk, :T], pbT[:].rearrange("p (k t) -> p k t", k=8)[:, 0:nchunk, 0:T]),
              reads=["pbT"], writes=[dkey])
        mk.atomic(False)

    def lin_tok(outp, okey, hT, hkey, T, w, wkey, c0, c1, nk=8):
        for k in range(nk):
            mk.op("pe", lambda k=k: PE.matmul(outp, hT[:, k, :T], w[:, k, c0:c1], start=(k == 0), stop=(k == nk - 1)),
                  reads=[hkey, wkey], writes=[okey])

    def ssd_chunk(T, xa_T, xakey, dt_sb, dtkey, zs, zkey, hS, hSb, hkey, ysb, tmp_tiles, scr2):
        (xtok, dtt, e3, m1, Eexp, SC, t1, t2, xw, dttb, m1b) = tmp_tiles
        mk.atomic(True)
        for k in range(6):
            mk.op("pe", lambda k=k: PE.transpose(pbT[:T, k * 128:(k + 1) * 128], xa_T[:, k, :T], identb[:, :]),
                  reads=[xakey, "identb"], writes=["pbT"])
        mk.op("act", lambda: A.copy(xtok[:T, :], pbT[:T, 0:768]), reads=["pbT"], writes=["xtok"])
        mk.atomic(False)
        mk.op("dve", lambda: V.tensor_tensor(dtt[:T, 0:8], dt_sb, dtb[:T, :], ALU.add), reads=[dtkey, "smallt"], writes=["dtt"])
        mk.op("act", lambda: A.activation(out=dtt[:T, 0:8], in_=dtt[:T, 0:8], func=AF.Exp), reads=["dtt"], writes=["dtt"])
        mk.op("act", lambda: A.activation(out=dtt[:T, 0:8], in_=dtt[:T, 0:8], func=AF.Ln, bias=1.0), reads=["dtt"], writes=["dtt"])
        mk.op("dve", lambda: V.tensor_tensor(dtt[:T, 8:16], dtt[:T, 0:8], a_t[:T, :], ALU.mult), reads=["dtt", "a_t"], writes=["dtt"])
        dt_ = dtt[:T, 0:8]
        dtA = dtt[:T, 8:16]
        if T < 128:
            Mle, Mgt, On = cstb[:, 128:256], cstb[:, 256:384], cstb[:, 384:512]
            mk.op("dve", lambda: V.tensor_copy(dttb[:T, :], dtt[:T, 8:16]), reads=["dtt"], writes=["dttb"])
            dtA = dttb[:T, :]
            m1 = m1b
        else:
            Mle, Mgt, On = M_le, M_gt, ones
        mk.op("pe", lambda: PE.matmul(pb[4][:T, 0:8], Mle[:T, :T], dtA, start=True, stop=True), reads=["dtt", "dttb", "cst", "cstb"], writes=["pb4"])
        mk.op("pe", lambda: PE.matmul(pb[4][:T, 8:16], Mgt[:T, :T], dtA, start=True, stop=True), reads=["dtt", "dttb", "cst", "cstb"], writes=["pb4"])
        mk.op("pe", lambda: PE.matmul(pb[4][:, 16:24], On[:T, :], dtA, start=True, stop=True), reads=["dtt", "dttb", "cst", "cstb"], writes=["pb4"])
        mk.op("act", lambda: A.activation(out=e3[:T, 0:16], in_=pb[4][:T, 0:16], func=AF.Exp), reads=["pb4"], writes=["e3"])
        mk.op("act", lambda: A.activation(out=e3[:, 16:24], in_=pb[4][:, 16:24], func=AF.Exp), reads=["pb4"], writes=["e3"])
        mk.op("dve", lambda: V.tensor_tensor(e3[:T, 8:16], e3[:T, 8:16], dt_, ALU.mult), reads=["e3", "dtt"], writes=["e3"])
        for g in range(2):
            mk.op("pe", lambda g=g: PE.matmul(pb[4][:T, 128 + g * 128:128 + g * 128 + T], xa_T[:, 4 + g, :T], xa_T[:, 6 + g, :T],
                                              start=True, stop=True), reads=[xakey], writes=["pb4"])
        for h in range(8):
            mk.op("dve", lambda h=h: V.tensor_scalar(m1[:T, h, :T], M_gt[:T, :T], dtt[:T, 8 + h:9 + h], None, ALU.mult),
                  reads=["cst", "dtt"], writes=["m1_%d" % h])
            bank = pb[5 + h // 4]
            mk.op("pe", lambda h=h, bank=bank: PE.matmul(bank[:T, (h % 4) * 128:(h % 4) * 128 + T], m1[:T, h, :T], Mle[:T, :T],
                                                        start=True, stop=True),
                  reads=["m1_%d" % h, "cst", "cstb"], writes=["pb%d" % (5 + h // 4)])
        for hh in range(2):
            mk.op("act", lambda hh=hh: A.activation(
                out=Eexp[:T, hh * 4:(hh + 1) * 4, :T],
                in_=pb[5 + hh][:T, :].rearrange("p (h i) -> p h i", h=4)[:, :, 0:T], func=AF.Exp),
                reads=["pb%d" % (5 + hh)], writes=["Eexp%d" % hh])
            mk.op("dve", lambda hh=hh: V.tensor_tensor(
                Eexp[:T, hh * 4:(hh + 1) * 4, :T], Eexp[:T, hh * 4:(hh + 1) * 4, :T],
                M_le[:T, :T].unsqueeze(1).to_broadcast([T, 4, T]), ALU.mult),
                reads=["Eexp%d" % hh, "cst"], writes=["Eexp%d" % hh])
            mk.op("pool", lambda hh=hh: G.tensor_tensor(
                Eexp[:T, hh * 4:(hh + 1) * 4, :T], Eexp[:T, hh * 4:(hh + 1) * 4, :T],
                dtt[:T, hh * 4:(hh + 1) * 4].unsqueeze(2).to_broadcast([T, 4, T]), ALU.mult),
                reads=["Eexp%d" % hh, "dtt"], writes=["Eexp%d" % hh])
            mk.op("dve", lambda hh=hh: V.tensor_tensor(
                SC[:T, hh * 4:(hh + 1) * 4, :T], Eexp[:T, hh * 4:(hh + 1) * 4, :T],
                pb[4][:T, 128 + hh * 128:128 + hh * 128 + T].unsqueeze(1).to_broadcast([T, 4, T]), ALU.mult),
                reads=["Eexp%d" % hh, "pb4"], writes=["SC%d" % hh])
        for h in range(8):
            mk.op("pe", lambda h=h: PE.matmul(pb[5][:T, h * 64:(h + 1) * 64], SC[:T, h, :T], xtok[:T, h * 64:(h + 1) * 64],
                                              start=True, stop=True),
                  reads=["SC%d" % (h // 4), "xtok"], writes=["pb5"])
        for g in range(2):
            mk.op("pe", lambda g=g: PE.matmul(pb[6][:T, g * 256:(g + 1) * 256], xa_T[:, 6 + g, :T], hSb[:, g * 256:(g + 1) * 256],
                                              start=True, stop=True),
                  reads=[xakey, hkey + "b"], writes=["pb6"])

        def v3(ap):
            return ap.rearrange("p (h d) -> p h d", h=8)

        def bc(ap8):
            return ap8.unsqueeze(2).to_broadcast([T, 8, 64])
        mk.op("dve", lambda: V.tensor_tensor(v3(t1[:T, :]), v3(pb[6][:T, :]), bc(e3[:T, 0:8]), ALU.mult),
              reads=["pb6", "e3"], writes=["t1"])
        mk.op("dve", lambda: V.tensor_tensor(t1[:T, :], t1[:T, :], pb[5][:T, :], ALU.add), reads=["t1", "pb5"], writes=["t1"])
        mk.op("pool", lambda: G.tensor_tensor(v3(t2[:T, :]), v3(xtok[:T, 0:512]), bc(dsk[:T, :]), ALU.mult),
              reads=["xtok", "smallt"], writes=["t2"])
        mk.op("pool", lambda: G.tensor_tensor(t1[:T, :], t1[:T, :], t2[:T, :], ALU.add), reads=["t1", "t2"], writes=["t1"])
        mk.op("dve", lambda: V.tensor_tensor(t1[:T, :], t1[:T, :], zs, ALU.mult), reads=["t1", zkey], writes=["t1"])
        rstd_from(t1[:T, :], T, 512, 1, ["t1"], scr=scr2, scrkey="scr2")
        mk.op("dve", lambda: V.scalar_tensor_tensor(ysb[:T, :], t1[:T, :], ss[:T, 1:2], g_ssd[:T, :], ALU.mult, ALU.mult),
              reads=["t1", "ss1", "grow"], writes=["ysb"])
        mk.op("pool", lambda: G.tensor_tensor(v3(xw[:T, :]), v3(xtok[:T, 0:512]), bc(e3[:T, 8:16]), ALU.mult),
              reads=["xtok", "e3"], writes=["xw"])
        for g in range(2):
            mk.op("pe", lambda g=g: PE.matmul(pb[4][:, g * 256:(g + 1) * 256], xtok[:T, 512 + g * 128:512 + (g + 1) * 128],
                                              xw[:T, g * 256:(g + 1) * 256], start=True, stop=True),
                  reads=["xtok", "xw"], writes=["pb4"])
        mk.op("dve", lambda: V.tensor_tensor(hS[:].rearrange("p (h d) -> p h d", h=8), hS[:].rearrange("p (h d) -> p h d", h=8),
                                             e3[:, 16:24].unsqueeze(2).to_broadcast([128, 8, 64]), ALU.mult),
              reads=[hkey, "e3"], writes=[hkey])
        mk.op("dve", lambda: V.tensor_tensor(hS[:], hS[:], pb[4][:], ALU.add), reads=[hkey, "pb4"], writes=[hkey])
        mk.op("act", lambda: A.copy(hSb[:], hS[:]), reads=[hkey], writes=[hkey + "b"])

    ns = nseq if do_sample else 0
    ntl = ntiles if do_prompt else 0
    esKV = ExitStack()

    def sbK(name, shape, dt=F32):
        return esKV.enter_context(nc.sbuf_tensor(name, list(shape), dt))
    KT = sbK("KT", [128, 2, SEQ], BF16)
    KTr = sbK("KTr", [32, SEQ], BF16)
    Vt = sbK("Vt", [128, NT, 256], BF16)
    KTs = sbK("KTs", [128, 2, 64], BF16)
    KTrs = sbK("KTrs", [32, 64], BF16)
    Vs = sbK("Vs", [4, 16, 256], BF16)
    yssd_s = dscr("yssd_s", [64, 512], BF16)
    x1_s = dscr("x1_s", [64, D])

    def v4(ap, T, h=4):
        return ap.rearrange("p (h t) -> p h t", h=h)[:, :, 0:T]

    with ExitStack() as esA:
        def sbA(name, shape, dt=F32):
            return esA.enter_context(nc.sbuf_tensor(name, list(shape), dt))
        win = sbA("win", [128, 8, WIN], BF16)
        load_w_bf16(win, "win", w_in, 8, 0, WIN, "ld_win")
        gs1 = sbA("gs1", [128, D])
        sh1 = sbA("sh1", [128, D])
        xt = [sbA("xt%d" % i, [128, D]) for i in range(2)]
        hb = sbA("hb", [128, D], BF16)
        hT = sbA("hT", [128, 8, 128], BF16)
        raw = sbA("raw", [128, 8, 131])
        acc = sbA("acc", [128, 8, 128])
        acc2 = sbA("acc2", [128, 8, 128])
        xa_T = sbA("xa_T", [128, 8, 128], BF16)
        kvf = sbA("kvf", [128, 256])
        krf = sbA("krf", [128, 64])
        krb = sbA("krb", [128, 32], BF16)
        ropet = sbA("ropet", [128, 64])
        hS = sbA("hS", [128, 512])
        hSb = sbA("hSb", [128, 512], BF16)
        ysb = sbA("ysb", [128, 512], BF16)
        stin = sbA("stin", [128, 4, 128])
        tmps = (sbA("xtok", [128, 768], BF16), sbA("dtt", [128, 16]), sbA("e3", [128, 24]),
                sbA("m1", [128, 8, 128]), sbA("Eexp", [128, 8, 128]), sbA("SC", [128, 8, 128], BF16),
                sbA("t1", [128, 512]), sbA("t2", [128, 512]), sbA("xw", [128, 512], BF16),
                sbA("dttb", [128, 8], BF16), sbA("m1b", [128, 8, 128], BF16))

        xa2 = [xa_T, sbA("xa_T1", [128, 8, 128], BF16)]
        zs2 = [sbA("zs%d" % i, [128, 512]) for i in range(2)]
        dtr2 = [sbA("dtr%d" % i, [128, 8]) for i in range(2)]
        scr2 = sbA("scr2", [128, 512])
        sto = tmps[7]

        def a_s1(T, i2, x_t, xk, rope_src, kv_dst, kr_dst, V_ap, Vkey, K0_ap, K1_ap, Kr_ap, Kkey, conv_dst):
            xa = xa2[i2]
            mk.dma("sp", "ld_rope", ropet[:T, :], rope_src, writes=["ropet"])
            front(x_t, xk, gs1, sh1, "m1", T, hb, hT, "hT")
            lin_tok(pb[0][:T, 0:8], "pb0", hT, "hT", T, win, "win", 1536, 1544)
            lin_tok(pb[0][:T, 64:384], "pb0", hT, "hT", T, win, "win", 1928, 2248)
            lin_tok(pb[3][:T, :], "pb3", hT, "hT", T, win, "win", 0, 512)
            for c in range(8):
                bank = pb[1 + c // 4]
                for k in range(8):
                    mk.op("pe", lambda c=c, k=k, bank=bank: PE.matmul(
                        bank[:, (c % 4) * 128:(c % 4) * 128 + T], win[:, k, 512 + c * 128:512 + (c + 1) * 128], hT[:, k, :T],
                        start=(k == 0), stop=(k == 7)), reads=["win", "hT"], writes=["pb%d" % (1 + c // 4)])
            mk.op("dve", lambda: V.tensor_copy(dtr2[i2][:T, :], pb[0][:T, 0:8]), reads=["pb0"], writes=["dtr%d" % i2])
            mk.op("act", lambda: A.activation(out=zs2[i2][:T, :], in_=pb[3][:T, :], func=AF.Silu), reads=["pb3"], writes=["zs%d" % i2])
            rstd_from(pb[0][:T, 64:320], T, 256, 2, ["pb0"])
            mk.op("dve", lambda: V.scalar_tensor_tensor(kvf[:T, :], pb[0][:T, 64:320], ss[:T, 2:3], g_kv[:T, :], ALU.mult, ALU.mult),
                  reads=["pb0", "ss2", "grow"], writes=["kvf"])
            mk.op("pool", lambda: G.tensor_copy(V_ap, kvf[:T, :]), reads=["kvf"], writes=[Vkey])
            mk.dma("sp", "st_kv", kv_dst, kvf[:T, :], reads=["kvf"])
            mk.op("dve", lambda: V.tensor_tensor(krf[:T, :], pb[0][:T, 320:384], ropet[:T, :], ALU.mult),
                  reads=["pb0", "ropet"], writes=["krf"])
            mk.op("dve", lambda: V.tensor_tensor(krf[:T, 0:32], krf[:T, 0:32], krf[:T, 32:64], ALU.add), reads=["krf"], writes=["krf"])
            mk.op("pool", lambda: G.tensor_copy(krb[:T, :], krf[:T, 0:32]), reads=["krf"], writes=["krb"])
            mk.dma("sp", "st_kr", kr_dst, krf[:T, 0:32], reads=["krf"])
            mk.atomic(True)
            for rc in range(2):
                mk.op("pe", lambda rc=rc: PE.transpose(pbT[:, rc * 128:rc * 128 + T], V_ap[:, rc * 128:(rc + 1) * 128], identb[:T, :T]),
                      reads=[Vkey, "identb"], writes=["pbT"])
            mk.op("pe", lambda: PE.transpose(pbT[0:32, 256:256 + T], krb[:T, :], identb[:T, :T]), reads=["krb", "identb"], writes=["pbT"])
            mk.op("act", lambda: A.copy(K0_ap, pbT[:, 0:T]), reads=["pbT"], writes=[Kkey])
            mk.op("act", lambda: A.copy(K1_ap, pbT[:, 128:128 + T]), reads=["pbT"], writes=[Kkey])
            mk.op("act", lambda: A.copy(Kr_ap, pbT[0:32, 256:256 + T]), reads=["pbT"], writes=[Kkey])
            mk.atomic(False)
            for hh in range(2):
                mk.op("act", lambda hh=hh: A.copy(raw[:, hh * 4:(hh + 1) * 4, 3:3 + T], v4(pb[1 + hh][:], T)),
                      reads=["pb%d" % (1 + hh)], writes=["raw"])
            if conv_dst is not None:
                for c in range(8):
                    mk.dma("sp", "st_conv", conv_dst[:, c * 128:(c + 1) * 128].rearrange("r p -> p r"), raw[:, c, T:T + 3],
                           reads=["raw"], allow_slow_non_contiguous=True)

            def wb(k):
                return convw_t[:, :, k:k + 1].to_broadcast([128, 8, T])
            A_ = acc[:, :, 0:T]
            B_ = acc2[:, :, 0:T]
            mk.op("dve", lambda: V.tensor_tensor(A_, raw[:, :, 0:T], wb(0), ALU.mult), reads=["raw", "convw_t"], writes=["acc"])
            mk.op("pool", lambda: G.tensor_tensor(B_, raw[:, :, 1:1 + T], wb(1), ALU.mult), reads=["raw", "convw_t"], writes=["acc2"])
            mk.op("dve", lambda: V.tensor_tensor(A_, A_, B_, ALU.add), reads=["acc", "acc2"], writes=["acc"])
            mk.op("pool", lambda: G.tensor_tensor(B_, raw[:, :, 2:2 + T], wb(2), ALU.mult), reads=["raw", "convw_t", "acc"], writes=["acc2"])
            mk.op("dve", lambda: V.tensor_tensor(A_, A_, B_, ALU.add), reads=["acc", "acc2"], writes=["acc"])
            mk.op("pool", lambda: G.tensor_tensor(B_, raw[:, :, 3:3 + T], wb(3), ALU.mult), reads=["raw", "convw_t", "acc"], writes=["acc2"])
            mk.op("dve", lambda: V.tensor_tensor(A_, A_, B_, ALU.add), reads=["acc", "acc2"], writes=["acc"])
            mk.op("dve", lambda: V.tensor_tensor(A_, A_, wb(4), ALU.add), reads=["acc", "convw_t"], writes=["acc"])
            mk.op("act", lambda: A.activation(out=xa[:, :, 0:T], in_=A_, func=AF.Silu), reads=["acc"], writes=["xa%d" % i2])
            mk.op("pool", lambda: G.tensor_copy(raw[:, :, 0:3], raw[:, :, T:T + 3]), reads=["raw"], writes=["raw"])

        def a_s2(T, i2, yssd_dst, ykey):
            ssd_chunk(T, xa2[i2], "xa%d" % i2, dtr2[i2][:T, :], "dtr%d" % i2, zs2[i2][:T, :], "zs%d" % i2, hS, hSb, "hS", ysb, tmps, scr2)
            mk.dma("sp", "st_yssd", yssd_dst, ysb[:T, :], reads=["ysb"], writes=[ykey])

        def state_out(dst):
            for c4 in range(4):
                mk.op("pe", lambda c4=c4: PE.transpose(pb[5][:, c4 * 128:(c4 + 1) * 128], hS[:, c4 * 128:(c4 + 1) * 128], ident),
                      reads=["hS", "cst"], writes=["pb5"])
            mk.op("act", lambda: A.copy(sto[:], pb[5][:]), reads=["pb5"], writes=["t2"])
            mk.dma("sp", "st_ssm", dst.rearrange("(c p) n -> p c n", p=128), sto[:].rearrange("p (c n) -> p c n", c=4),
                   reads=["t2"])

        if ntl:
            load_mod(gs1, sh1, ada_p, 128, 0, 1024, 0, 0, "m1")
            mk.op("dve", lambda: V.memset(hS[:], 0.0), writes=["hS"])
            mk.op("dve", lambda: V.memset(hSb[:], 0.0), writes=["hSb"])
            mk.op("pool", lambda: G.memset(raw[:], 0.0), writes=["raw"])

        def p_s1(t):
            x_t = xt[t % 2]
            xk = "xt%d" % (t % 2)
            ts_ = slice(t * 128, (t + 1) * 128)
            mk.dma("sp", "ld_" + xk, x_t[:], xp[ts_, :], writes=[xk])
            a_s1(128, t % 2, x_t, xk, rope_tok[ts_, :], o_kv[ts_, :], o_kr[ts_, :], Vt[:, t, :], "Vt%d" % t,
                 KT[:, 0, ts_], KT[:, 1, ts_], KTr[:, ts_], "KT%d" % t, o_conv if t == ntl - 1 else None)
        if ntl:
            p_s1(0)
        for t in range(ntl):
            progs = []
            if t + 1 < ntl:
                mk.record()
                p_s1(t + 1)
                progs.append(mk.stop_record())
            mk.record()
            a_s2(128, t % 2, yssd_scr[t * 128:(t + 1) * 128, :], "yssd%d" % t)
            progs.append(mk.stop_record())
            mk.play(progs)
        if ntl:
            state_out(o_ssm)
        mk.counting = True
        mk.limit = oplimit

        def s_s1(s_):
            x_t = xt[s_ % 2]
            xk = "xt%d" % (s_ % 2)
            r4 = slice(4 * s_, 4 * s_ + 4)
            mk.dma("sp", "ld_" + xk, x_t[:4, :], xs[r4, :], writes=[xk])
            load_mod(gs1, sh1, ada_s[r4, :], 4, 1, 1024, 0, 0, "m1")
            for c in range(8):
                mk.dma("sp", "ld_raw", raw[:, c, 0:3], st_conv[3 * s_:3 * s_ + 3, c * 128:(c + 1) * 128].rearrange("r p -> p r"),
                       writes=["raw"], allow_slow_non_contiguous=True)
            a_s1(4, s_ % 2, x_t, xk, rope_tok[SEQ + 4 * s_:SEQ + 4 * s_ + 4, :], os_kv[r4, :], os_kr[r4, :], Vs[:, s_, :], "Vs%d" % s_,
                 KTs[:, 0, r4], KTs[:, 1, r4], KTrs[:, r4], "KTs%d" % s_, os_conv[3 * s_:3 * s_ + 3, :])

        def s_s2(s_):
            r4 = slice(4 * s_, 4 * s_ + 4)
            mk.dma("sp", "ld_stin", stin[:], st_ssm[512 * s_:512 * (s_ + 1), :].rearrange("(c p) n -> p c n", p=128), writes=["stin"])
            for c4 in range(4):
                mk.op("pe", lambda c4=c4: PE.transpose(pb[5][:, c4 * 128:(c4 + 1) * 128], stin[:, c4, :], ident),
                      reads=["stin", "cst"], writes=["pb5"])
            mk.op("dve", lambda: V.tensor_copy(hS[:], pb[5][:]), reads=["pb5"], writes=["hS"])
            mk.op("act", lambda: A.copy(hSb[:], hS[:]), reads=["hS"], writes=["hSb"])
            a_s2(4, s_ % 2, yssd_s[r4, :], "yssds%d" % s_)
            state_out(os_ssm[512 * s_:512 * (s_ + 1), :])
        if ns:
            s_s1(0)
        for s_ in range(ns):
            progs = []
            if s_ + 1 < ns:
                mk.record()
                s_s1(s_ + 1)
                progs.append(mk.stop_record())
            mk.record()
            s_s2(s_)
            progs.append(mk.stop_record())
            mk.play(progs)
        mk.counting = False
        mk.limit = None

    mk.barrier()
    with ExitStack() as esB:
        def sbB(name, shape, dt=F32):
            return esB.enter_context(nc.sbuf_tensor(name, list(shape), dt))
        wq = sbB("wq", [128, 8, 384], BF16)
        load_w_bf16(wq, "wq", w_in, 8, 1544, 1928, "ld_wq")
        wuq = sbB("wuq", [128, 3, 1024], BF16)
        load_w_bf16(wuq, "wuq", w_uq, 3, 0, 1024, "ld_wuq")
        wuk = sbB("wuk", [64, 2048], BF16)
        mk.dma("pool", "ld_wuk", wuk[:], w_ukT, writes=["wuk"])
        wuv = sbB("wuv", [128, 2, 512], BF16)
        load_w_bf16(wuv, "wuv", w_uv, 2, 0, 512, "ld_wuv")
        wo = sbB("wo", [128, 8, D], BF16)
        load_w_bf16(wo, "wo", w_out, 8, 0, D, "ld_wo")
        gs1 = sbB("gs1b", [128, D])
        sh1 = sbB("sh1b", [128, D])
        g1 = sbB("g1", [128, D])
        xt = [sbB("xtb%d" % i, [128, D]) for i in range(2)]
        hb = sbB("hbb", [128, D], BF16)
        hT = sbB("hTb", [128, 8, 128], BF16)
        qlT = sbB("qlT", [128, 3, 128], BF16)
        sq = sbB("sq", [128, 3, 128], BF16)
        rbc = sbB("rbc", [128, 128])
        qT = sbB("qT", [64, 8, 128], BF16)
        qa = sbB("qa", [128, 2, 1024], BF16)
        qr = sbB("qr", [32, 1024], BF16)
        qrt = sbB("qrt", [32, 1024])
        qrs = sbB("qrs", [32, 1024])
        cosT = sbB("cosT", [32, 128])
        sinT = sbB("sinT", [32, 128])
        lbc = sbB("lbc", [128, 512])
        PT = [sbB("PT%d" % i, [128, 512], BF16) for i in range(2)]
        oT = sbB("oT", [128, 2, 1024], BF16)
        of = sbB("of", [128, 512])
        mix = sbB("mix", [128, D], BF16)
        mixT = sbB("mixT", [128, 8, 128], BF16)
        x1 = sbB("x1", [128, D])
        if ns:
            Vpg = [sbB("Vpg%d" % i, [128, 2, 288], BF16) for i in range(2)]
            KTpg = [sbB("KTpg%d" % i, [128, 2, 128], BF16) for i in range(4)]
            KTrpg = [sbB("KTrpg%d" % i, [32, 128], BF16) for i in range(4)]
            ptb = sbB("ptb", [128, 64], I32)
            idx = sbB("idx", [128, 64], I32)
            iot = sbB("iot", [128, 64], I32)
            mk.op("pool", lambda: G.iota(iot[:], [[0, 64]], base=0, channel_multiplier=1), writes=["iot"])
            mk.op("dve", lambda: V.tensor_scalar(iot[:], iot[:], 63, None, ALU.bitwise_and), reads=["iot"], writes=["iot"])

        def b1_tile(T, x_t, xk, cos_src, sin_src, yssd_src, ykeys, ktiles, x1_dst, x1key, part):
            HPH = 4 if T == 128 else 8
            NHF = 8 // HPH
            CW = HPH * T

            def hv(ap, h=HPH):
                return ap.rearrange("p (h t) -> p h t", h=h)
            if part == "q":
                mk.dma("sp", "ld_cosT", cosT[:, :T], cos_src, writes=["cosT"])
                mk.dma("sp", "ld_sinT", sinT[:, :T], sin_src, writes=["sinT"])
                front(x_t, xk, gs1, sh1, "m1b", T, hb, hT, "hTb")
                for c in range(3):
                    for k in range(8):
                        mk.op("pe", lambda c=c, k=k: PE.matmul(pb[0][:, c * 128:c * 128 + T], wq[:, k, c * 128:(c + 1) * 128], hT[:, k, :T],
                                                               start=(k == 0), stop=(k == 7)), reads=["wq", "hTb"], writes=["pb0"])
                for c in range(3):
                    mk.op("act", lambda c=c: A.activation(out=qlT[:, c, :T], in_=pb[0][:, c * 128:c * 128 + T], func=AF.Copy,
                                                          scale=qg_t[:, c:c + 1]), reads=["pb0", "qg_t"], writes=["qlT"])
                mk.op("act", lambda: A.activation(out=sq[:, :, :T], in_=v4(pb[0][:, 0:384], T, 3), func=AF.Square),
                      reads=["pb0"], writes=["sq"])
                for c in range(3):
                    mk.op("pe", lambda c=c: PE.matmul(pb[1][:, 0:T], onesb[:, :], sq[:, c, :T], start=(c == 0), stop=(c == 2)),
                          reads=["onesb", "sq"], writes=["pb1"])
                R = rbc[:, :T]
                mk.op("dve", lambda: V.tensor_scalar(R, pb[1][:, 0:T], 1.0 / 384, EPS, ALU.mult, ALU.add), reads=["pb1"], writes=["rbc"])
                mk.op("act", lambda: A.activation(out=R, in_=R, func=AF.Sqrt), reads=["rbc"], writes=["rbc"])
                mk.op("dve", lambda: V.reciprocal(R, R), reads=["rbc"], writes=["rbc"])
                mk.op("dve", lambda: V.tensor_scalar(R, R, ATTN_SCALE, None, ALU.mult), reads=["rbc"], writes=["rbc"])
                for h in range(8):
                    bank = pb[2 + h // 4]
                    for c in range(3):
                        mk.op("pe", lambda h=h, c=c, bank=bank: PE.matmul(bank[0:64, (h % 4) * 128:(h % 4) * 128 + T], wuq[:, c, h * 128:h * 128 + 64],
                                                                         qlT[:, c, :T], start=(c == 0), stop=(c == 2)),
                              reads=["wuq", "qlT"], writes=["pb%d" % (2 + h // 4)])
                mk.atomic(True)
                rb = (4, 5)
                sb_ = (6, 0)
                for h in range(8):
                    for (banks, off) in ((rb, 64), (sb_, 96)):
                        bi = banks[h // 4]
                        for c in range(3):
                            mk.op("pe", lambda h=h, c=c, bi=bi, off=off: PE.matmul(
                                pb[bi][0:32, (h % 4) * 128:(h % 4) * 128 + T], wuq[:, c, h * 128 + off:h * 128 + off + 32],
                                qlT[:, c, :T], start=(c == 0), stop=(c == 2)),
                                reads=["wuq", "qlT", "sq", "qlT"], writes=["pb%d" % bi])
                qr3 = qr[:, 0:8 * T].rearrange("p (h t) -> p h t", h=8)
                qrt3 = qrt[:, 0:8 * T].rearrange("p (h t) -> p h t", h=8)
                qrs3 = qrs[:, 0:8 * T].rearrange("p (h t) -> p h t", h=8)
                for hh in range(2):
                    hs = slice(hh * 4, (hh + 1) * 4)
                    mk.op("act", lambda hh=hh, hs=hs: A.copy(qT[:, hs, :T], v4(pb[2 + hh][0:64, :], T)),
                          reads=["pb%d" % (2 + hh)], writes=["qT"])
                    mk.op("dve", lambda hh=hh, hs=hs: V.tensor_tensor(qrt3[:, hs, :], v4(pb[rb[hh]][0:32, :], T),
                                                                     cosT[:, :T].unsqueeze(1).to_broadcast([32, 4, T]), ALU.mult),
                          reads=["pb%d" % rb[hh], "cosT"], writes=["qrt"])
                    mk.op("dve", lambda hh=hh, hs=hs: V.tensor_tensor(qrs3[:, hs, :], v4(pb[sb_[hh]][0:32, :], T),
                                                                     sinT[:, :T].unsqueeze(1).to_broadcast([32, 4, T]), ALU.mult),
                          reads=["pb%d" % sb_[hh], "sinT"], writes=["qrs"])
                    mk.op("pool", lambda hs=hs: G.tensor_tensor(qrt3[:, hs, :], qrt3[:, hs, :], qrs3[:, hs, :], ALU.add),
                          reads=["qrt", "qrs"], writes=["qrt"])
                    mk.op("dve", lambda hs=hs: V.tensor_tensor(qr3[:, hs, :], qrt3[:, hs, :],
                                                               rbc[0:32, :T].unsqueeze(1).to_broadcast([32, 4, T]), ALU.mult),
                          reads=["qrt", "rbc"], writes=["qr"])
                mk.atomic(False)
                for rc in range(2):
                    qa3 = qa[:, rc, 0:8 * T].rearrange("p (h t) -> p h t", h=8)
                    for hh in range(2):
                        bank = pb[hh]
                        for h4 in range(4):
                            h = hh * 4 + h4
                            mk.op("pe", lambda h=h, h4=h4, rc=rc, bank=bank: PE.matmul(
                                bank[:, h4 * 128:h4 * 128 + T], wuk[:, h * 256 + rc * 128:h * 256 + (rc + 1) * 128], qT[0:64, h, :T],
                                start=True, stop=True), reads=["wuk", "qT"], writes=["pb%d" % hh])
                        mk.op("dve", lambda rc=rc, hh=hh, bank=bank, qa3=qa3: V.tensor_tensor(
                            qa3[:, hh * 4:(hh + 1) * 4, :], v4(bank[:], T),
                            rbc[:, :T].unsqueeze(1).to_broadcast([128, 4, T]), ALU.mult),
                            reads=["pb%d" % hh, "rbc"], writes=["qa"])
            if part == "attn":
                nk_tiles = len(ktiles)
                for hf in range(NHF):
                    cols = slice(hf * CW, (hf + 1) * CW)

                    def st_S(ki):
                        kd = ktiles[ki]
                        nk = kd["nk"]
                        sbank = pb[ki % 2]
                        skey = "pb%d" % (ki % 2)
                        P = PT[ki % 2]
                        pkey = "PT%d" % (ki % 2)
                        mk.op("pe", lambda: PE.matmul(sbank[:nk, :CW], kd["k0"], qa[:, 0, cols], start=True, stop=False),
                              reads=kd["keys"] + ["qa"], writes=[skey])
                        mk.op("pe", lambda: PE.matmul(sbank[:nk, :CW], kd["k1"], qa[:, 1, cols], start=False, stop=False),
                              reads=kd["keys"] + ["qa"], writes=[skey])
                        mk.op("pe", lambda: PE.matmul(sbank[:nk, :CW], kd["kr"], qr[:, cols], start=False, stop=True),
                              reads=kd["keys"] + ["qr"], writes=[skey])
                        mk.op("act", lambda: A.activation(out=P[:nk, :CW], in_=sbank[:nk, :CW], func=AF.Exp), reads=[skey], writes=[pkey])
                        if kd.get("mask") is not None:
                            mk.op("dve", lambda: V.tensor_tensor(hv(P[:nk, :CW]), hv(P[:nk, :CW]),
                                                                 kd["mask"].unsqueeze(1).to_broadcast([nk, HPH, T]), ALU.mult),
                                  reads=[pkey] + kd.get("mkeys", ["cst"]), writes=[pkey])

                    def st_PV(ki):
                        kd = ktiles[ki]
                        nk = kd["nk"]
                        P = PT[ki % 2]
                        pkey = "PT%d" % (ki % 2)
                        first = (ki == 0)
                        last = (ki == nk_tiles - 1)
                        for rc in range(2):
                            mk.op("pe", lambda rc=rc: PE.matmul(pb[2 + rc][:, :CW], kd["v"][:, rc * 128:(rc + 1) * 128], P[:nk, :CW],
                                                                start=first, stop=last),
                                  reads=kd["keys"] + [pkey], writes=["pb%d" % (2 + rc)])
                        mk.op("pe", lambda: PE.matmul(pb[4][:, :CW], onesb[:nk, :], P[:nk, :CW], start=first, stop=last),
                              reads=[pkey, "onesb"], writes=["pb4"])

                    for it in range(nk_tiles + 2):
                        if it < nk_tiles and ktiles[it].get("prep") is not None:
                            ktiles[it]["prep"]()
                        if 0 <= it - 1 < nk_tiles:
                            st_S(it - 1)
                        if 0 <= it - 2 < nk_tiles:
                            st_PV(it - 2)
                    mk.op("dve", lambda: V.reciprocal(lbc[:, :CW], pb[4][:, :CW]), reads=["pb4"], writes=["lbc"])
                    for rc in range(2):
                        mk.op("dve", lambda rc=rc: V.tensor_tensor(oT[:, rc, cols], pb[2 + rc][:, :CW], lbc[:, :CW], ALU.mult),
                              reads=["pb%d" % (2 + rc), "lbc"], writes=["oT"])
            if part == "out":
                if callable(yssd_src):
                    yssd_src()
                else:
                    mk.dma("sp", "ld_mix", mix[:T, 0:512], yssd_src, reads=ykeys, writes=["mix"])
                mk.atomic(True)
                for h in range(8):
                    for rc in range(2):
                        mk.op("pe", lambda h=h, rc=rc: PE.matmul(pb[5][:T, h * 64:(h + 1) * 64], oT[:, rc, h * T:(h + 1) * T],
                                                                 wuv[:, rc, h * 64:(h + 1) * 64], start=(rc == 0), stop=(rc == 1)),
                              reads=["oT", "wuv"], writes=["pb5"])
                mk.op("act", lambda: A.copy(of[:T, :], pb[5][:T, :]), reads=["pb5"], writes=["of"])
                mk.atomic(False)
                rstd_from(of[:T, :], T, 512, 3, ["of"], scr=lbc, scrkey="lbc")
                mk.op("dve", lambda: V.scalar_tensor_tensor(mix[:T, 512:1024], of[:T, :], ss[:T, 3:4], g_attn[:T, :], ALU.mult, ALU.mult),
                      reads=["of", "ss3", "grow"], writes=["mix"])
                transpose8(mix, "mix", T, mixT, "mixT")
                for hf in range(2):
                    mk.atomic(True)
                    lin_tok(pb[5 + hf][:T, :], "pb%d" % (5 + hf), mixT, "mixT", T, wo, "wo", hf * 512, (hf + 1) * 512)
                    mk.op("dve", lambda hf=hf: V.tensor_tensor(x1[:T, hf * 512:(hf + 1) * 512], pb[5 + hf][:T, :], g1[:T, hf * 512:(hf + 1) * 512], ALU.mult),
                          reads=["pb%d" % (5 + hf), "g1"], writes=["x1"])
                    mk.atomic(False)
                mk.op("pool", lambda: G.tensor_tensor(x1[:T, :], x1[:T, :], x_t[:T, :], ALU.add), reads=["x1", xk], writes=["x1"])
                mk.dma("sp", "st_x1", x1_dst, x1[:T, :], reads=["x1"], writes=[x1key])

        if ntl and stop >= 1.2:
            load_mod(gs1, sh1, ada_p, 128, 0, 1024, 0, 0, "m1b")
            mk.dma("sp", "ld_g1", g1[:], ada_p[:, 2048:3072], reads=["ada0"], writes=["g1"])
            if balanced:
                amask_t = sbB("amask_t", [128, 8, 128], BF16)
                mk.dma("pool", "ld_amask", amask_t[:], amask, writes=["amask_t"])
                idxo = sbB("idxo", [128, NB], I32)
                mk.dma("sp", "ld_idxo", idxo[:], idx_own, writes=["idxo"])
            def p_args(m):
                x_t = xt[m % 2]
                xk = "xtb%d" % (m % 2)
                ms = slice(m * 128, (m + 1) * 128)
                kts = []
                if balanced:
                    sel = 0 if m < NB // 2 else 1
                    for kt in range(4 * m + 4):
                        ks = slice(kt * 128, (kt + 1) * 128)
                        msk = amask_t[:, sel * 4 + (kt - 4 * m), :] if kt >= 4 * m else None
                        kts.append(dict(k0=KT[:, 0, ks], k1=KT[:, 1, ks], kr=KTr[:, ks], v=Vt[:, kt, :], nk=128,
                                        keys=["KT%d" % kt, "Vt%d" % kt], mask=msk, mkeys=["amask_t"]))

                    def ld_mix(m=m):
                        mk.gather("d_mix", mix[:, 0:512], yssd_scr[0:ntl * 128, :], idxo[:, m:m + 1], reads=["idxo"], writes=["mix"])
                    return (x_own[ms, :], (128, x_t, xk, rope_own[0:32, ms], rope_own[32:64, ms], ld_mix, [], kts, x1_scr[ms, :], "x1s%d" % m))
                t = m
                for kt in range(t + 1):
                    ks = slice(kt * 128, (kt + 1) * 128)
                    kts.append(dict(k0=KT[:, 0, ks], k1=KT[:, 1, ks], kr=KTr[:, ks], v=Vt[:, kt, :], nk=128,
                                    keys=["KT%d" % kt, "Vt%d" % kt], mask=(M_le if kt == t else None)))
                return (xp[ms, :], (128, x_t, xk, rope_ch[0:32, ms], rope_ch[32:64, ms], yssd_scr[ms, :], ["yssd%d" % t], kts,
                                    x1_scr[ms, :], "x1s%d" % t))

            def p_q(m):
                xsrc, a_ = p_args(m)
                mk.dma("sp", "ld_" + a_[2], a_[1][:], xsrc, writes=[a_[2]])
                b1_tile(*a_, part="q")
            if NB:
                p_q(0)
            for m in range(NB):
                a_ = p_args(m)[1]
                b1_tile(*a_, part="attn")
                progs = []
                mk.record()
                b1_tile(*a_, part="out")
                progs.append(mk.stop_record())
                if m + 1 < NB:
                    mk.record()
                    p_q(m + 1)
                    progs.append(mk.stop_record())
                mk.play(progs)
        def s_args(s_):
            x_t = xt[s_ % 2]
            xk = "xtb%d" % (s_ % 2)
            r4 = slice(4 * s_, 4 * s_ + 4)
            kts = []
            return x_t, xk, r4, kts

        def s_q(s_):
            x_t, xk, r4, _ = s_args(s_)
            mk.dma("sp", "ld_" + xk, x_t[:4, :], xs[r4, :], writes=[xk])
            load_mod(gs1, sh1, ada_s[r4, :], 4, 1, 1024, 0, 0, "m1b")
            b1_tile(4, x_t, xk, rope_ch[0:32, SEQ + 4 * s_:SEQ + 4 * s_ + 4], rope_ch[32:64, SEQ + 4 * s_:SEQ + 4 * s_ + 4],
                    yssd_s[r4, :], ["yssds%d" % s_], [], x1_s[r4, :], "x1ss%d" % s_, part="q")

        def s_attn_out(s_, nxt):
            x_t, xk, r4, _ = s_args(s_)
            mk.dma("sp", "ld_ptb", ptb[:], ptab[s_:s_ + 1, :].partition_broadcast(128), writes=["ptb"])
            mk.op("dve", lambda: V.tensor_scalar(idx[0:64, 0:32], ptb[0:64, 0:64:2], 6, None, ALU.logical_shift_left),
                  reads=["ptb"], writes=["idx"])
            mk.op("dve", lambda: V.tensor_scalar(idx[64:128, 0:32], ptb[64:128, 1:64:2], 6, None, ALU.logical_shift_left),
                  reads=["ptb"], writes=["idx"])
            mk.op("dve", lambda: V.tensor_tensor(idx[:, 0:32], idx[:, 0:32], iot[:, 0:32], ALU.bitwise_or), reads=["idx", "iot"], writes=["idx"])
            kts = []
            for g in range(NPAGES):
                q_, e_ = g // 2, g % 2
                vb = q_ % 2
                kb = g % 4

                def prep(g=g, q_=q_, e_=e_, vb=vb, kb=kb):
                    if e_ == 0:
                        mk.gather("d_Vpg%d" % vb, Vpg[vb][:].rearrange("p e c -> p (e c)"), cache_all, idx[:, q_:q_ + 1],
                                  reads=["idx"], writes=["Vpg%d" % vb])
                    for rc in range(2):
                        mk.op("pe", lambda rc=rc: PE.transpose(pbT[:, rc * 128:(rc + 1) * 128], Vpg[vb][:, e_, rc * 128:(rc + 1) * 128], identb[:]),
                              reads=["Vpg%d" % vb, "identb"], writes=["pbT"])
                    mk.op("pe", lambda: PE.transpose(pbT[0:32, 256:384], Vpg[vb][:, e_, 256:288], identb[:]),
                          reads=["Vpg%d" % vb, "identb"], writes=["pbT"])
                    mk.op("act", lambda: A.copy(KTpg[kb][:], pbT[:, 0:256].rearrange("p (c k) -> p c k", c=2)),
                          reads=["pbT"], writes=["KTpg%d" % kb])
                    mk.op("dve", lambda: V.tensor_copy(KTrpg[kb][:], pbT[0:32, 256:384]), reads=["pbT"], writes=["KTpg%d" % kb])
                kts.append(dict(k0=KTpg[kb][:, 0, :], k1=KTpg[kb][:, 1, :], kr=KTrpg[kb][:], v=Vpg[vb][:, e_, 0:256], nk=128,
                                keys=["KTpg%d" % kb, "Vpg%d" % vb], mask=None, prep=prep))
            kts.append(dict(k0=KTs[:, 0, r4], k1=KTs[:, 1, r4], kr=KTrs[:, r4], v=Vs[:, s_, :], nk=4,
                            keys=["KTs%d" % s_, "Vs%d" % s_], mask=M_le[0:4, 0:4]))
            a_ = (4, x_t, xk, rope_ch[0:32, SEQ + 4 * s_:SEQ + 4 * s_ + 4], rope_ch[32:64, SEQ + 4 * s_:SEQ + 4 * s_ + 4],
                  yssd_s[r4, :], ["yssds%d" % s_], kts, x1_s[r4, :], "x1ss%d" % s_)
            b1_tile(*a_, part="attn")
            progs = []
            mk.record()
            mk.dma("sp", "ld_g1", g1[:4, :], ada_s[r4, 2048:3072], reads=["ada1"], writes=["g1"])
            b1_tile(*a_, part="out")
            progs.append(mk.stop_record())
            if nxt is not None:
                mk.record()
                s_q(nxt)
                progs.append(mk.stop_record())
            mk.play(progs)
        nsb = ns if sstop >= 2 else 0
        if nsb:
            s_q(0)
        for s_ in range(nsb):
            s_attn_out(s_, s_ + 1 if s_ + 1 < nsb else None)
    esKV.close()
    mk.barrier()

    with ExitStack() as esC:
        def sbC(name, shape, dt=F32):
            return esC.enter_context(nc.sbuf_tensor(name, list(shape), dt))
        wup = sbC("wup", [128, 8, 4096], BF16)
        load_w_bf16(wup, "wup", w_up, 8, 0, 4096, "ld_wup")
        wdn = sbC("wdn", [128, 32, D], BF16)
        load_w_bf16(wdn, "wdn", w_down, 32, 0, D, "ld_wdn")
        gs2 = sbC("gs2", [128, D])
        sh2 = sbC("sh2", [128, D])
        g2 = sbC("g2", [128, D])
        gsf = sbC("gsf", [128, D])
        shf = sbC("shf", [128, D])
        xt = [sbC("xtc%d" % i, [128, D]) for i in range(2)]
        hb = sbC("hbc", [128, D], BF16)
        hT = sbC("hTc", [128, 8, 128], BF16)
        u2T = sbC("u2T", [128, 32, 128], BF16)
        ur = sbC("ur", [128, 512])
        x2 = sbC("x2", [128, D])

        def b2_tile(T, x_t, xk, y_dst):
            front(x_t, xk, gs2, sh2, "m2", T, hb, hT, "hTc")
            for fb in range(8):
                bank = pb[fb % 4]
                bkey = "pb%d" % (fb % 4)
                for f4 in range(4):
                    fc = fb * 4 + f4
                    for k in range(8):
                        mk.op("pe", lambda fc=fc, f4=f4, k=k, bank=bank: PE.matmul(
                            bank[:, f4 * 128:f4 * 128 + T], wup[:, k, fc * 128:(fc + 1) * 128], hT[:, k, :T],
                            start=(k == 0), stop=(k == 7)), reads=["wup", "hTc"], writes=[bkey])
                urv = ur[:, 0:4 * T].rearrange("p (c t) -> p c t", c=4)
                mk.op("act", lambda bank=bank, urv=urv: A.activation(out=urv, in_=v4(bank[:], T), func=AF.Relu), reads=[bkey], writes=["ur"])
                mk.op("dve", lambda fb=fb, urv=urv: V.tensor_tensor(u2T[:, fb * 4:(fb + 1) * 4, :T], urv, urv, ALU.mult),
                      reads=["ur"], writes=["u2T"])
            for hf in range(2):
                bank = pb[4 + hf]
                for fc in range(32):
                    mk.op("pe", lambda fc=fc, hf=hf, bank=bank: PE.matmul(bank[:T, :], u2T[:, fc, :T], wdn[:, fc, hf * 512:(hf + 1) * 512],
                                                                         start=(fc == 0), stop=(fc == 31)),
                          reads=["u2T", "wdn"], writes=["pb%d" % (4 + hf)])
                mk.op("dve", lambda hf=hf, bank=bank: V.tensor_tensor(x2[:T, hf * 512:(hf + 1) * 512], bank[:T, :], g2[:T, hf * 512:(hf + 1) * 512], ALU.mult),
                      reads=["pb%d" % (4 + hf), "g2"], writes=["x2"])
            mk.op("pool", lambda: G.tensor_tensor(x2[:T, :], x2[:T, :], x_t[:T, :], ALU.add), reads=["x2", xk], writes=["x2"])
            rstd_from(x2[:T, :], T, 1024, 1, ["x2"])
            mk.op("dve", lambda: V.scalar_tensor_tensor(x2[:T, :], x2[:T, :], ss[:T, 1:2], gsf[:T, :], ALU.mult, ALU.mult),
                  reads=["x2", "ss1", "mfgs"], writes=["x2"])
            mk.op("pool", lambda: G.tensor_tensor(x2[:T, :], x2[:T, :], shf[:T, :], ALU.add), reads=["x2", "mfsh"], writes=["x2"])
            mk.dma("sp", "st_y", y_dst, x2[:T, :], reads=["x2"])

        if ntl and stop >= 3:
            load_mod(gs2, sh2, ada_p, 128, 0, 4096, 3072, 1024, "m2")
            load_mod(gsf, shf, ada_p, 128, 0, 7168, 6144, 2048, "mf")
            mk.dma("sp", "ld_g2", g2[:], ada_p[:, 5120:6144], reads=["ada0"], writes=["g2"])
            for t in range(NB):
                x_t = xt[t % 2]
                xk = "xtc%d" % (t % 2)
                ts_ = slice(t * 128, (t + 1) * 128)
                mk.dma("sp", "ld_" + xk, x_t[:], x1_scr[ts_, :], reads=["x1s%d" % t], writes=[xk])
                b2_tile(128, x_t, xk, o_y[ts_, :])
        if ns and sstop >= 3:
            TS = 4 * ns
            x_t = xt[0]
            xk = "xtc0"
            mk.dma("sp", "ld_" + xk, x_t[:TS, :], x1_s[0:TS, :], reads=["x1ss%d" % i for i in range(ns)], writes=[xk])
            load_mod(gs2, sh2, ada_s[0:TS, :], TS, 1, 4096, 3072, 1024, "m2")
            load_mod(gsf, shf, ada_s[0:TS, :], TS, 1, 7168, 6144, 2048, "mf")
            mk.dma("sp", "ld_g2", g2[:TS, :], ada_s[0:TS, 5120:6144], reads=["ada1"], writes=["g2"])
            b2_tile(TS, x_t, xk, os_y[0:TS, :])

    mk.finish("sp")
    return nc


def _consts():
    ident = np.eye(128, dtype=np.float32)
    m = np.arange(128)
    M_le = (m[:, None] <= m[None, :]).astype(np.float32)
    M_gt = (m[:, None] > m[None, :]).astype(np.float32)
    ones = np.ones((128, 128), np.float32)
    c = np.concatenate([ident, M_le, M_gt, ones], axis=1)
    return np.ascontiguousarray(c)


def _rope_tables():
    inv = 1.0 / (10000.0 ** (np.arange(0, 32, 2, dtype=np.float32) / 32.0))
    posp = np.arange(SEQ, dtype=np.float32)
    poss = np.tile(PAST + np.arange(4, dtype=np.float32), 16)
    pos = np.concatenate([posp, poss])
    ang = pos[:, None] * inv[None, :]
    cos, sin = np.cos(ang).astype(np.float32), np.sin(ang).astype(np.float32)
    tok = np.concatenate([cos, cos, -sin, sin], axis=1)
    return np.ascontiguousarray(tok), np.ascontiguousarray(tok.T)


def own_tiles(j, ntiles=NT):
    nb = ntiles // 4
    return [4 * m + (j if m < nb // 2 else 3 - j) for m in range(nb)]


def _host_inputs(inp, core, n_phys_full=True, do_sample=True, ntiles=NT):
    f = np.float32
    b = core // 4
    d = {}
    d["xp"] = np.ascontiguousarray(inp["x_prompt"][b])
    d["cp"] = np.ascontiguousarray(np.broadcast_to(inp["c_prompt"][b][None, :], (128, D))).astype(f)
    s0 = core * NSEQ_CORE
    d["xs"] = np.ascontiguousarray(inp["x_sample"][s0:s0 + NSEQ_CORE].reshape(64, D))
    d["cs"] = np.ascontiguousarray(np.repeat(inp["c_sample"][s0:s0 + NSEQ_CORE], 4, axis=0))
    d["w_ada"] = np.ascontiguousarray(inp["w_ada"][0])
    d["w_adaf"] = np.ascontiguousarray(inp["w_ada_final"])
    d["b_ada"] = np.concatenate([inp["b_ada"][0], inp["b_ada_final"]])[None, :].astype(f)
    rows = np.zeros((1, 7 * 1024), f)
    rows[0, 0:1024] = inp["norm_mix_g"][0]
    rows[0, 1024:2048] = inp["norm_mlp_g"][0]
    rows[0, 2048:3072] = inp["norm_final_g"]
    rows[0, 3072:3584] = inp["norm_ssd_g"][0]
    rows[0, 3584:4096] = inp["norm_attn_g"][0]
    rows[0, 4096:4352] = inp["kv_norm_g"][0]
    d["rows"] = rows
    small = np.zeros((1, 32), f)
    small[0, 0:8] = inp["dt_bias"][0]
    small[0, 8:16] = inp["a_log"][0]
    small[0, 16:24] = inp["d_skip"][0]
    d["small"] = small
    w_in = inp["w_in"][0]
    kr = w_in[:, 2184:2216]
    kr_sw = np.concatenate([kr[:, 16:32], kr[:, 0:16]], axis=1)
    d["w_in"] = np.ascontiguousarray(np.concatenate([w_in, kr_sw], axis=1))
    cw = np.concatenate([inp["conv_w"][0], inp["conv_b"][0][None, :]], axis=0)
    d["convw"] = np.ascontiguousarray(cw.T.reshape(8, 128, 5).transpose(1, 0, 2))
    d["qg"] = np.ascontiguousarray(inp["q_norm_g"][0].reshape(3, 128).T)
    wq = inp["w_uq"][0].reshape(384, 8, 96)
    rope = wq[:, :, 64:96]
    rope_sw = np.concatenate([rope[:, :, 16:32], rope[:, :, 0:16]], axis=2)
    d["w_uq"] = np.ascontiguousarray(np.concatenate([wq, rope_sw], axis=2).reshape(384, 1024))
    d["w_ukT"] = np.ascontiguousarray(inp["w_uk"][0].transpose(2, 1, 0).reshape(64, 2048))
    d["w_uv"] = np.ascontiguousarray(inp["w_uv"][0].reshape(256, 512))
    d["w_out"] = np.ascontiguousarray(inp["w_out"][0])
    d["w_up"] = np.ascontiguousarray(inp["w_up"][0])
    d["w_down"] = np.ascontiguousarray(inp["w_down"][0])
    rt, rc = _rope_tables()
    d["rope_tok"] = rt
    d["rope_ch"] = rc
    if ntiles % 8 == 0 and ntiles > 0:
        j = core % 4
        tl = own_tiles(j, ntiles)
        rowsel = np.concatenate([np.arange(t * 128, (t + 1) * 128) for t in tl])
        d["x_own"] = np.ascontiguousarray(inp["x_prompt"][b][rowsel])
        d["rope_own"] = np.ascontiguousarray(rc[:, rowsel])
        d["idx_own"] = np.ascontiguousarray(rowsel.reshape(len(tl), 128).T.astype(np.int32))
        m_ = np.arange(128)
        tri = (m_[:, None] <= m_[None, :]).astype(np.float32)
        am = np.zeros((128, 8, 128), np.float32)
        for sel, o in enumerate((j, 3 - j)):
            for i in range(4):
                am[:, sel * 4 + i, :] = 1.0 if i < o else (tri if i == o else 0.0)
        d["amask"] = am
    d["consts"] = _consts()
    if do_sample:
        if "_cache_all" not in inp:
            inp["_cache_all"] = np.ascontiguousarray(np.concatenate(
                [inp["cache_kv_latent"][0].reshape(-1, 256), inp["cache_k_rope"][0].reshape(-1, 32)], axis=1)).reshape(-1, 576)
        d["cache_all"] = inp["_cache_all"]
        d["ptab"] = np.ascontiguousarray(inp["page_table"][s0:s0 + NSEQ_CORE]).astype(np.int32)
        d["st_conv"] = np.ascontiguousarray(inp["state_conv"][0, s0:s0 + NSEQ_CORE].reshape(-1, D))
        d["st_ssm"] = np.ascontiguousarray(inp["state_ssm"][0, s0:s0 + NSEQ_CORE].reshape(-1, 128))
    return d


def kernel(**inputs):
    inp = {k: np.asarray(v) for k, v in inputs.items()}
    n_phys = int(inp["cache_kv_latent"].shape[1])
    nc = build(n_phys)
    in_maps = [_host_inputs(inp, c) for c in range(8)]
    res = run_bass_kernel_spmd(nc, in_maps, core_ids=list(range(8)))
    r = res.results
    f = np.float32
    pc = (0, 4)
    y_p = np.zeros((2, SEQ, D), f)
    for c in range(8):
        for m, t in enumerate(own_tiles(c % 4)):
            y_p[c // 4, t * 128:(t + 1) * 128] = r[c]["o_y"][m * 128:(m + 1) * 128]
    kv_p = np.stack([r[c]["o_kv"] for c in pc])[None].astype(f)
    kr_p = np.stack([r[c]["o_kr"] for c in pc])[None].astype(f)
    conv_p = np.stack([r[c]["o_conv"] for c in pc])[None].astype(f)
    ssm_p = np.stack([r[c]["o_ssm"].reshape(8, 64, 128) for c in pc])[None].astype(f)
    y_s = np.concatenate([r[c]["os_y"] for c in range(8)]).reshape(128, 4, D).astype(f)
    kv_s = np.concatenate([r[c]["os_kv"] for c in range(8)]).reshape(1, 128, 4, 256).astype(f)
    kr_s = np.concatenate([r[c]["os_kr"] for c in range(8)]).reshape(1, 128, 4, 32).astype(f)
    conv_s = np.concatenate([r[c]["os_conv"] for c in range(8)]).reshape(1, 128, 3, D).astype(f)
    ssm_s = np.concatenate([r[c]["os_ssm"] for c in range(8)]).reshape(1, 128, 8, 64, 128).astype(f)
    return (y_p, y_s, kv_p, kr_p, conv_p, ssm_p, kv_s, kr_s, conv_s, ssm_s)
```
